# Optimizing a Trainium2 kernel written in Bass

```python
import math
import jax
import jax.numpy as jnp
from jax import lax
import numpy as np

D_MODEL = 2048
BATCH = 1
SEQ = 8192
DEPTH = 2

EPS = 1e-6
Q_BLOCK = 128
NEG_INF = -1e30
FORCE = 1e9
NSA_HEADS = 16
NSA_GROUPS = 2
NSA_HPG = NSA_HEADS // NSA_GROUPS
NSA_DK = 64
NSA_DV = 64
NSA_CMP_HID = 64
CMP_LEN = 32
CMP_STRIDE = 16
SLC_LEN = 64
SLC_TOPK = 16
WINDOW = 512
MLA_HEADS = 8
Q_LORA = 512
KV_LORA = 512
QK_NOPE = 128
QK_ROPE = 64
V_HEAD = 128
ROPE_THETA = 10000.0
RWKV_HEAD = 64
RWKV_HEADS = D_MODEL // RWKV_HEAD
DECAY_LORA = 96
A_LORA = 96
LNX_EPS = 64e-5
NSA_Q_W = NSA_HEADS * NSA_DK
NSA_KV_W = 6 * NSA_GROUPS * NSA_DK
NSA_G_W = 3 * NSA_HEADS
NSA_Z_W = NSA_HEADS * NSA_DV
MLA_QA_W = Q_LORA
MLA_KVA_W = KV_LORA + QK_ROPE
MLA_Z_W = MLA_HEADS * V_HEAD
IN0_SIZES = (NSA_Q_W, NSA_KV_W, NSA_G_W, NSA_Z_W, MLA_QA_W, MLA_KVA_W, MLA_Z_W)
IN0_W = NSA_Q_W + NSA_KV_W + NSA_G_W + NSA_Z_W + MLA_QA_W + MLA_KVA_W + MLA_Z_W
MIX0_W = NSA_HEADS * NSA_DV + MLA_HEADS * V_HEAD

kernel_name = "nsa_mla_rwkv7_adaln_hybrid"


def rmsnorm(x, g):
    xf = x.astype(jnp.float32)
    y = xf * lax.rsqrt(jnp.mean(xf * xf, axis=-1, keepdims=True) + EPS)
    return (y * g.astype(jnp.float32)).astype(x.dtype)


def masked_softmax(s, mask, axis=-1):
    s = jnp.where(mask, s.astype(jnp.float32), NEG_INF)
    p = jax.nn.softmax(s, axis=axis)
    return jnp.where(mask, p, 0.0)


def alibi_slopes(n):
    return 2.0 ** (-8.0 * jnp.arange(1, n + 1, dtype=jnp.float32) / n)


def rope_angles(S, dim):
    inv = ROPE_THETA ** (-jnp.arange(0, dim, 2, dtype=jnp.float32) / dim)
    ang = jnp.arange(S, dtype=jnp.float32)[:, None] * inv[None]
    return jnp.cos(ang), jnp.sin(ang)


def apply_rope(x, cos, sin):
    x1, x2 = jnp.split(x.astype(jnp.float32), 2, axis=-1)
    extra = x.ndim - 3
    cos = cos.reshape(cos.shape[0], *([1] * extra), cos.shape[-1])
    sin = sin.reshape(sin.shape[0], *([1] * extra), sin.shape[-1])
    return jnp.concatenate([x1 * cos - x2 * sin, x1 * sin + x2 * cos], axis=-1).astype(x.dtype)


def split_cols(x, sizes):
    offs, acc = [], 0
    for s in sizes[:-1]:
        acc += s
        offs.append(acc)
    return jnp.split(x, offs, axis=-1)


def nsa_attention(q, kc, vc, ks, vs, kw, vw, gates,
                  pe_k, w1_k, b1_k, w2_k, pe_v, w1_v, b1_v, w2_v):
    B, S = q.shape[0], q.shape[1]
    G, HPG, DK = NSA_GROUPS, NSA_HPG, NSA_DK
    n_cmp = (S - CMP_LEN) // CMP_STRIDE + 1
    n_slc = S // SLC_LEN
    top = min(SLC_TOPK, n_slc)
    idx = jnp.arange(n_cmp)[:, None] * CMP_STRIDE + jnp.arange(CMP_LEN)[None]

    def compress(t, pe, w1, b1, w2):
        blk = t[:, idx] + pe[None, None, :, None, :]
        hid = jax.nn.silu(jnp.einsum('bnlgd,lde->bnge', blk, w1) + b1)
        return jnp.einsum('bnge,ed->bngd', hid, w2)

    k_cmp = compress(kc, pe_k, w1_k, b1_k, w2_k)
    v_cmp = compress(vc, pe_v, w1_v, b1_v, w2_v)
    cmp_start = jnp.arange(n_cmp) * CMP_STRIDE
    cmp_end = cmp_start + CMP_LEN - 1
    slc_start = jnp.arange(n_slc) * SLC_LEN
    overlap = ((cmp_start[:, None] < slc_start[None] + SLC_LEN)
               & (cmp_start[:, None] + CMP_LEN > slc_start[None])).astype(jnp.float32)
    ks_blk = ks.reshape(B, n_slc, SLC_LEN, G, DK).transpose(0, 3, 1, 2, 4)
    vs_blk = vs.reshape(B, n_slc, SLC_LEN, G, DK).transpose(0, 3, 1, 2, 4)
    pad = ((0, 0), (WINDOW, 0), (0, 0), (0, 0))
    kw_pad = jnp.pad(kw, pad)
    vw_pad = jnp.pad(vw, pad)
    slopes = alibi_slopes(NSA_HEADS).reshape(G, HPG)
    scale = DK ** -0.5
    qg = q.reshape(B, S, G, HPG, DK)
    gg = gates.reshape(B, S, G, HPG, 3)
    bidx = jnp.arange(B)[:, None, None, None]
    gidx = jnp.arange(G)[None, :, None, None]
    jb = jnp.arange(n_slc)

    def block(nb):
        start = nb * Q_BLOCK
        qb = lax.dynamic_slice_in_dim(qg, start, Q_BLOCK, axis=1)
        gb = lax.dynamic_slice_in_dim(gg, start, Q_BLOCK, axis=1)
        t = start + jnp.arange(Q_BLOCK)
        tf = t.astype(jnp.float32)
        s = jnp.einsum('bqghd,bngd->bghqn', qb, k_cmp).astype(jnp.float32) * scale
        s = s - slopes[None, :, :, None, None] * (tf[:, None] - cmp_end[None].astype(jnp.float32))
        p_cmp = masked_softmax(s, cmp_end[None] <= t[:, None])
        o_cmp = jnp.einsum('bghqn,bngd->bqghd', p_cmp.astype(v_cmp.dtype), v_cmp)
        imp = jnp.einsum('bghqn,nj->bgqj', p_cmp, overlap)
        cur = t // SLC_LEN
        forced = (jb[None] == 0) | (jb[None] == cur[:, None]) | (jb[None] == cur[:, None] - 1)
        imp = jnp.where(forced, FORCE, imp)
        imp = jnp.where(slc_start[None] <= t[:, None], imp, NEG_INF)
        _, sel = lax.top_k(imp, top)
        k_sel = ks_blk[bidx, gidx, sel]
        v_sel = vs_blk[bidx, gidx, sel]
        pos = sel[..., None] * SLC_LEN + jnp.arange(SLC_LEN)
        dist = t[None, None, :, None, None] - pos
        s = jnp.einsum('bqghd,bgqnld->bghqnl', qb, k_sel).astype(jnp.float32) * scale
        s = s - slopes[None, :, :, None, None, None] * dist[:, :, None].astype(jnp.float32)
        p = masked_softmax(s, (dist >= 0)[:, :, None], axis=(-2, -1))
        o_slc = jnp.einsum('bghqnl,bgqnld->bqghd', p.astype(v_sel.dtype), v_sel)
        kwb = lax.dynamic_slice_in_dim(kw_pad, start, WINDOW + Q_BLOCK, axis=1)
        vwb = lax.dynamic_slice_in_dim(vw_pad, start, WINDOW + Q_BLOCK, axis=1)
        spos = start - WINDOW + jnp.arange(WINDOW + Q_BLOCK)
        d = t[:, None] - spos[None]
        s = jnp.einsum('bqghd,bkgd->bghqk', qb, kwb).astype(jnp.float32) * scale
        s = s - slopes[None, :, :, None, None] * d.astype(jnp.float32)
        mw = (d >= 0) & (d < WINDOW) & (spos[None] >= 0)
        p = masked_softmax(s, mw)
        o_win = jnp.einsum('bghqk,bkgd->bqghd', p.astype(vwb.dtype), vwb)
        return gb[..., 0:1] * o_cmp + gb[..., 1:2] * o_slc + gb[..., 2:3] * o_win

    out = lax.map(block, jnp.arange(S // Q_BLOCK))
    return out.transpose(1, 0, 2, 3, 4, 5).reshape(B, S, NSA_HEADS * NSA_DV)


def mla_attention(q_a, kv_a, qa_g, w_qb, kva_g, w_kvb):
    B, S = q_a.shape[0], q_a.shape[1]
    q = (rmsnorm(q_a, qa_g) @ w_qb).reshape(B, S, MLA_HEADS, QK_NOPE + QK_ROPE)
    q_nope, q_pe = q[..., :QK_NOPE], q[..., QK_NOPE:]
    c_kv, k_pe = kv_a[..., :KV_LORA], kv_a[..., KV_LORA:]
    kv = (rmsnorm(c_kv, kva_g) @ w_kvb).reshape(B, S, MLA_HEADS, QK_NOPE + V_HEAD)
    k_nope, v = kv[..., :QK_NOPE], kv[..., QK_NOPE:]
    cos, sin = rope_angles(S, QK_ROPE)
    q_pe = apply_rope(q_pe, cos, sin)
    k_pe = apply_rope(k_pe, cos, sin)
    scale = (QK_NOPE + QK_ROPE) ** -0.5
    kpos = jnp.arange(S)

    def block(nb):
        start = nb * Q_BLOCK
        qn = lax.dynamic_slice_in_dim(q_nope, start, Q_BLOCK, axis=1)
        qp = lax.dynamic_slice_in_dim(q_pe, start, Q_BLOCK, axis=1)
        t = start + jnp.arange(Q_BLOCK)
        s = (jnp.einsum('bqhd,bkhd->bhqk', qn, k_nope)
             + jnp.einsum('bqhd,bkd->bhqk', qp, k_pe)).astype(jnp.float32) * scale
        p = masked_softmax(s, kpos[None] <= t[:, None])
        return jnp.einsum('bhqk,bkhd->bqhd', p.astype(v.dtype), v)

    out = lax.map(block, jnp.arange(S // Q_BLOCK))
    return out.transpose(1, 0, 2, 3, 4).reshape(B, S, MLA_HEADS * V_HEAD)


def hybrid_attention_mixer(h, w_in, w_out, pe_k, w1_k, b1_k, w2_k, pe_v, w1_v, b1_v, w2_v,
                           qa_g, w_qb, kva_g, w_kvb):
    B, S, _ = h.shape
    proj = h @ w_in
    nsa_q, nsa_kv, nsa_g, nsa_z, mla_qa, mla_kva, mla_z = split_cols(proj, IN0_SIZES)
    q = nsa_q.reshape(B, S, NSA_HEADS, NSA_DK)
    kv6 = nsa_kv.reshape(B, S, 6, NSA_GROUPS, NSA_DK)
    gates = jax.nn.sigmoid(nsa_g).reshape(B, S, NSA_HEADS, 3)
    o_nsa = nsa_attention(q, kv6[:, :, 0], kv6[:, :, 1], kv6[:, :, 2], kv6[:, :, 3],
                          kv6[:, :, 4], kv6[:, :, 5], gates,
                          pe_k, w1_k, b1_k, w2_k, pe_v, w1_v, b1_v, w2_v)
    o_mla = mla_attention(mla_qa, mla_kva, qa_g, w_qb, kva_g, w_kvb)
    y = jnp.concatenate([o_nsa * jax.nn.silu(nsa_z), o_mla * jax.nn.silu(mla_z)], axis=-1)
    return y @ w_out


def rwkv7_mixer(h, mu, w_r, w_k, w_v, w_z, w_o, w0, w1, w2, a0, a1, a2, k_k, k_a, r_k, lnx_g, lnx_b):
    B, S, D = h.shape
    H, N = RWKV_HEADS, RWKV_HEAD
    xx = jnp.pad(h, ((0, 0), (1, 0), (0, 0)))[:, :-1] - h
    xr, xw, xk, xv, xa, xz = [h + xx * mu[i] for i in range(6)]
    r = xr @ w_r
    k = xk @ w_k
    v = xv @ w_v
    z = xz @ w_z
    w = -jax.nn.softplus(-(w0 + jnp.tanh(xw @ w1) @ w2)) - 0.5
    decay = jnp.exp(-jnp.exp(w.astype(jnp.float32)))
    a = jax.nn.sigmoid(a0 + (xa @ a1) @ a2)

    def heads(t):
        return t.reshape(B, S, H, N).astype(jnp.float32)

    kk = heads(k * k_k)
    kk = kk / jnp.maximum(jnp.sqrt(jnp.sum(kk * kk, axis=-1, keepdims=True)), 1e-12)
    k = k * (1.0 + (a - 1.0) * k_a)
    r_h, k_h, v_h, a_h, w_h = heads(r), heads(k), heads(v), heads(a), heads(decay)

    def step(state, inp):
        r_t, w_t, k_t, v_t, kk_t, a_t = inp
        sa = jnp.einsum('bhvk,bhk->bhv', state, -kk_t)
        state = (state * w_t[:, :, None, :] + sa[..., None] * (kk_t * a_t)[:, :, None, :]
                 + v_t[..., None] * k_t[:, :, None, :])
        return state, jnp.einsum('bhvk,bhk->bhv', state, r_t)

    xs = tuple(t.transpose(1, 0, 2, 3) for t in (r_h, w_h, k_h, v_h, kk, a_h))
    s0 = jnp.zeros((B, H, N, N), jnp.float32)
    _, y = lax.scan(step, s0, xs)
    y = y.transpose(1, 0, 2, 3)
    mean = jnp.mean(y, axis=-1, keepdims=True)
    var = jnp.mean((y - mean) ** 2, axis=-1, keepdims=True)
    y = ((y - mean) * lax.rsqrt(var + LNX_EPS)).reshape(B, S, D) * lnx_g + lnx_b
    bonus = jnp.sum(r_h * k_h * r_k.reshape(H, N), axis=-1, keepdims=True) * v_h
    y = y + bonus.reshape(B, S, D)
    y = (y * jax.nn.silu(z.astype(jnp.float32))).astype(h.dtype)
    return y @ w_o


def setup_inputs(seed: int = 0) -> dict:
    key = jax.random.key(seed)
    keys = iter(jax.random.split(key, 64))
    D = D_MODEL
    E = (DEPTH + 1) // 2
    O = DEPTH // 2

    def nrm(shape, scale):
        return jax.random.normal(next(keys), shape, jnp.float32) * scale

    def uni(shape, lo, hi):
        return jax.random.uniform(next(keys), shape, jnp.float32, lo, hi)

    L, DK, HID = CMP_LEN, NSA_DK, NSA_CMP_HID
    return {
        "x": nrm((BATCH, SEQ, D), 1.0),
        "c": nrm((BATCH, D), 1.0),
        "norm_g": 1.0 + nrm((DEPTH, D), 0.02),
        "ada_w": nrm((DEPTH, D, 3 * D), D ** -0.5),
        "ada_b": nrm((DEPTH, 3 * D), 0.02),
        "final_g": 1.0 + nrm((D,), 0.02),
        "a_w_in": nrm((E, D, IN0_W), D ** -0.5),
        "a_w_out": nrm((E, MIX0_W, D), MIX0_W ** -0.5),
        "nsa_pe_k": nrm((E, L, DK), 0.5),
        "nsa_w1_k": nrm((E, L, DK, HID), (L * DK) ** -0.5),
        "nsa_b1_k": nrm((E, HID), 0.02),
        "nsa_w2_k": nrm((E, HID, DK), HID ** -0.5),
        "nsa_pe_v": nrm((E, L, DK), 0.5),
        "nsa_w1_v": nrm((E, L, DK, HID), (L * DK) ** -0.5),
        "nsa_b1_v": nrm((E, HID), 0.02),
        "nsa_w2_v": nrm((E, HID, DK), HID ** -0.5),
        "mla_qa_g": 1.0 + nrm((E, Q_LORA), 0.02),
        "mla_w_qb": nrm((E, Q_LORA, MLA_HEADS * (QK_NOPE + QK_ROPE)), Q_LORA ** -0.5),
        "mla_kva_g": 1.0 + nrm((E, KV_LORA), 0.02),
        "mla_w_kvb": nrm((E, KV_LORA, MLA_HEADS * (QK_NOPE + V_HEAD)), KV_LORA ** -0.5),
        "r_mu": uni((O, 6, D), 0.0, 1.0),
        "r_w_r": nrm((O, D, D), D ** -0.5),
        "r_w_k": nrm((O, D, D), D ** -0.5),
        "r_w_v": nrm((O, D, D), D ** -0.5),
        "r_w_z": nrm((O, D, D), D ** -0.5),
        "r_w_o": nrm((O, D, D), D ** -0.5),
        "r_w0": uni((O, D), -6.0, 1.0),
        "r_w1": nrm((O, D, DECAY_LORA), D ** -0.5),
        "r_w2": nrm((O, DECAY_LORA, D), 0.1 * DECAY_LORA ** -0.5),
        "r_a0": nrm((O, D), 0.1),
        "r_a1": nrm((O, D, A_LORA), D ** -0.5),
        "r_a2": nrm((O, A_LORA, D), 0.1 * A_LORA ** -0.5),
        "r_k_k": 0.85 + nrm((O, D), 0.05),
        "r_k_a": 1.0 + nrm((O, D), 0.05),
        "r_r_k": nrm((O, D), 0.1),
        "r_lnx_g": 1.0 + nrm((O, D), 0.02),
        "r_lnx_b": nrm((O, D), 0.02),
    }


def reference(x, c, norm_g, ada_w, ada_b, final_g,
              a_w_in, a_w_out, nsa_pe_k, nsa_w1_k, nsa_b1_k, nsa_w2_k,
              nsa_pe_v, nsa_w1_v, nsa_b1_v, nsa_w2_v,
              mla_qa_g, mla_w_qb, mla_kva_g, mla_w_kvb,
              r_mu, r_w_r, r_w_k, r_w_v, r_w_z, r_w_o, r_w0, r_w1, r_w2,
              r_a0, r_a1, r_a2, r_k_k, r_k_a, r_r_k, r_lnx_g, r_lnx_b):
    sc = jax.nn.silu(c)
    for i in range(DEPTH):
        mod = (sc @ ada_w[i] + ada_b[i])[:, None, :]
        shift, scale, gate = jnp.split(mod, 3, axis=-1)
        h = rmsnorm(x, norm_g[i]) * (1.0 + scale) + shift
        j = i // 2
        if i % 2 == 0:
            y = hybrid_attention_mixer(h, a_w_in[j], a_w_out[j],
                                       nsa_pe_k[j], nsa_w1_k[j], nsa_b1_k[j], nsa_w2_k[j],
                                       nsa_pe_v[j], nsa_w1_v[j], nsa_b1_v[j], nsa_w2_v[j],
                                       mla_qa_g[j], mla_w_qb[j], mla_kva_g[j], mla_w_kvb[j])
        else:
            y = rwkv7_mixer(h, r_mu[j], r_w_r[j], r_w_k[j], r_w_v[j], r_w_z[j], r_w_o[j],
                            r_w0[j], r_w1[j], r_w2[j], r_a0[j], r_a1[j], r_a2[j],
                            r_k_k[j], r_k_a[j], r_r_k[j], r_lnx_g[j], r_lnx_b[j])
        x = x + gate * y
    return rmsnorm(x, final_g)
```

```python
from contextlib import ExitStack
import numpy as np
import ml_dtypes
import concourse.bass as bass
import concourse.mybir as mybir
from concourse.bass_utils import run_bass_kernel_spmd

F32 = mybir.dt.float32
BF16 = mybir.dt.bfloat16
AF = mybir.ActivationFunctionType
ALU = mybir.AluOpType
AX = mybir.AxisListType
NPBF = ml_dtypes.bfloat16
NCORES = 8

ENGS = ('pe', 'act', 'dve', 'pool', 'sp')


class V:
    __slots__ = ('tt', 'ap')

    def __init__(self, tt, ap):
        self.tt = tt
        self.ap = ap


class TT:
    def __init__(self, h, name):
        self.h = h
        self.name = name
        self.w = None
        self.r = {}
        self.dsem = None
        self.dval = 0

    def __getitem__(self, idx):
        return V(self, self.h[idx])

    def re(self, pattern, **kw):
        return TT_view(self, self.h.rearrange(pattern, **kw))


class TT_view:
    def __init__(self, tt, ap):
        self.tt = tt
        self.apx = ap

    def __getitem__(self, idx):
        return V(self.tt, self.apx[idx])


class Prog:
    def __init__(self, nc):
        self.nc = nc
        self.es = ExitStack()
        self.q = {e: [] for e in ENGS}
        self.seq = {e: 0 for e in ENGS}
        self.sems = {}
        self.known = {e: {} for e in ENGS}
        for e in ENGS:
            self._sem('E_' + e)
        self.ntile = 0
        self.outs = []
        self.outsem = {}
        self.mute = False

    def _sem(self, key):
        s = self.es.enter_context(self.nc.semaphore(key))
        self.sems[key] = s
        return key

    def sb(self, shape, dt, name=None):
        self.ntile += 1
        name = "sb_" + (name or f"t{self.ntile}")
        h = self.es.enter_context(self.nc.sbuf_tensor(name, list(shape), dt))
        return TT(h, name)

    def ps(self, shape, dt, name=None):
        self.ntile += 1
        name = "ps_" + (name or f"p{self.ntile}")
        h = self.es.enter_context(self.nc.psum_tensor(name, list(shape), dt))
        return TT(h, name)

    def dram(self, name, shape, dt, kind="ExternalInput"):
        h = self.nc.dram_tensor(name, list(shape), dt, kind=kind)
        t = TT(h.ap(), name)
        if kind == "ExternalOutput":
            self.outs.append(t)
        return t

    def _collect(self, e, reads, writes):
        waits = {}

        def need(ev, war=False):
            if ev is None:
                return
            key, val, eng = ev
            if eng == e:
                if e == 'pe':
                    return
            if self.known[e].get(key, 0) >= val:
                return
            if waits.get(key, 0) < val:
                waits[key] = val
        for t in reads:
            need(t.w)
        for t in writes:
            need(t.w)
            for r in t.r.values():
                need(r, war=True)
        for key, val in waits.items():
            self.known[e][key] = val
            self.q[e].append(('wait', key, val))

    def op(self, e, fn, reads=(), writes=()):
        if self.mute:
            return None
        reads = [t for t in reads if t is not None]
        self._collect(e, reads, writes)
        self.seq[e] += 1
        ev = ('E_' + e, self.seq[e], e)
        self.q[e].append(('op', fn, 'E_' + e))
        for t in reads:
            t.r[ev[0]] = ev
        for t in writes:
            t.w = ev
            t.r = {}
        return ev

    def dma(self, e, out, in_, owner=None):
        if self.mute:
            return None
        pairs = out if isinstance(out, list) else [(out, in_)]
        reads = list({id(i.tt): i.tt for (_, i) in pairs}.values())
        writes = list({id(o.tt): o.tt for (o, _) in pairs}.values())
        if owner is None:
            owner = writes[0]
        if owner.dsem is None:
            owner.dsem = self._sem('D_' + owner.name)
        self._collect(e, reads, writes)
        for (o, i) in pairs:
            owner.dval += 16
            self.q[e].append(('dma', o.ap, i.ap, owner.dsem))
        ev = (owner.dsem, owner.dval, 'dma')
        for t in reads:
            t.r[ev[0]] = ev
        for t in writes:
            t.w = ev
            t.r = {}
        return ev

    def mm(self, out, lhsT, rhs, start=True, stop=True, skip=False):
        o, l, r = out.ap, lhsT.ap, rhs.ap
        if skip:
            fn = lambda e: e.matmul(o, lhsT=l, rhs=r, start=start, stop=stop, skip_group_check=True)
        else:
            fn = lambda e: e.matmul(o, lhsT=l, rhs=r, start=start, stop=stop)
        return self.op('pe', fn, [lhsT.tt, rhs.tt], [out.tt])

    def transpose(self, out, in_, ident):
        o, i, d = out.ap, in_.ap, ident.ap
        return self.op('pe', lambda e: e.transpose(o, i, d), [in_.tt, ident.tt], [out.tt])

    def act(self, out, in_, func, bias=None, scale=None, accum=None, eng='act'):
        kw = {}
        rd = [in_.tt]
        wr = [out.tt]
        if bias is not None:
            if isinstance(bias, V):
                kw['bias'] = bias.ap
                rd.append(bias.tt)
            else:
                kw['bias'] = bias
        if scale is not None:
            if isinstance(scale, V):
                kw['scale'] = scale.ap
                rd.append(scale.tt)
            else:
                kw['scale'] = scale
        if accum is not None:
            kw['accum_out'] = accum.ap
            wr.append(accum.tt)
        o, i = out.ap, in_.ap
        return self.op(eng, lambda e: e.activation(out=o, in_=i, func=func, **kw), rd, wr)

    def tt(self, eng, out, in0, in1, op):
        o, a, b = out.ap, in0.ap, in1.ap
        return self.op(eng, lambda e: e.tensor_tensor(out=o, in0=a, in1=b, op=op), [in0.tt, in1.tt], [out.tt])

    def ts(self, eng, out, in0, s1, s2=None, op0=ALU.mult, op1=None):
        rd = [in0.tt]
        a1 = s1
        if isinstance(s1, V):
            a1 = s1.ap
            rd.append(s1.tt)
        a2 = s2
        if isinstance(s2, V):
            a2 = s2.ap
            rd.append(s2.tt)
        o, a = out.ap, in0.ap
        if op1 is None:
            fn = lambda e: e.tensor_scalar(out=o, in0=a, scalar1=a1, scalar2=None, op0=op0)
        else:
            fn = lambda e: e.tensor_scalar(out=o, in0=a, scalar1=a1, scalar2=a2, op0=op0, op1=op1)
        return self.op(eng, fn, rd, [out.tt])

    def stt(self, eng, out, in0, scalar, in1, op0, op1):
        rd = [in0.tt, in1.tt]
        s = scalar
        if isinstance(scalar, V):
            s = scalar.ap
            rd.append(scalar.tt)
        o, a, b = out.ap, in0.ap, in1.ap
        return self.op(eng, lambda e: e.scalar_tensor_tensor(out=o, in0=a, scalar=s, in1=b, op0=op0, op1=op1),
                       rd, [out.tt])

    def copy(self, eng, out, in_):
        o, i = out.ap, in_.ap
        if eng == 'act':
            return self.op(eng, lambda e: e.activation(out=o, in_=i, func=AF.Copy), [in_.tt], [out.tt])
        return self.op(eng, lambda e: e.tensor_copy(out=o, in_=i), [in_.tt], [out.tt])

    def memset(self, eng, out, val):
        o = out.ap
        return self.op(eng, lambda e: e.memset(o, val), [], [out.tt])

    def recip(self, out, in_):
        o, i = out.ap, in_.ap
        return self.op('dve', lambda e: e.reciprocal(out=o, in_=i), [in_.tt], [out.tt])

    def emit(self):
        nc = self.nc
        sems = self.sems
        q = self.q
        waits = {}
        for t in self.outs:
            if t.w is not None:
                key, val, _ = t.w
                waits[key] = max(waits.get(key, 0), val)
        for key, val in waits.items():
            q['sp'].append(('wait', key, val))
        for key, val in self.outsem.items():
            q['sp'].append(('wait', key, val))

        def replay(eng, items):
            for it in items:
                if it[0] == 'wait':
                    eng.wait_ge(sems[it[1]], it[2])
                elif it[0] == 'op':
                    it[1](eng).then_inc(sems[it[2]], 1)
                elif it[0] == 'dma':
                    eng.dma_start(out=it[1], in_=it[2]).then_inc(sems[it[3]], 16)
        with nc.Block() as block:
            @block.sync
            def _(eng):
                replay(eng, q['sp'])

            @block.tensor
            def _(eng):
                replay(eng, q['pe'])

            @block.scalar
            def _(eng):
                replay(eng, q['act'])

            @block.vector
            def _(eng):
                replay(eng, q['dve'])

            @block.gpsimd
            def _(eng):
                replay(eng, q['pool'])
        self.es.close()

    def dma_out(self, e, out, in_):
        ev = self.dma(e, out, in_, owner=in_.tt)
        if ev is None:
            return None
        self.outsem[ev[0]] = max(self.outsem.get(ev[0], 0), ev[1])
        return ev


def new_prog():
    nc = bass.Bass("TRN2", target_bir_lowering=False)
    return nc, Prog(nc)


def run(nc, in_maps):
    res = run_bass_kernel_spmd(nc, in_maps, core_ids=list(range(NCORES)))
    return res.results


D = 2048
S = 8192
TPC = S // NCORES
EPS = 1e-6
IN0_SEGS = [('nsa_q', 1024), ('nsa_kv', 768), ('nsa_g', 48), ('nsa_z', 1024),
            ('mla_qa', 512), ('mla_ckv', 512), ('mla_kpe', 64), ('mla_z', 1024)]
IN0_W = 4976


def seg_offsets():
    o = {}
    acc = 0
    for n, s in IN0_SEGS:
        o[n] = (acc, s)
        acc += s
    return o


def build_l0():
    nc, P = new_prog()
    CW = 768
    c_in = P.dram("c", [128, 16], F32)
    w_in = P.dram("w", [2, D, CW], F32)
    b_in = P.dram("b", [2, CW], F32)
    o = P.dram("o", [2, CW], F32, "ExternalOutput")
    cs = P.sb([128, 16], F32)
    sc = P.sb([128, 16], F32)
    ws = [P.sb([128, 16, CW], F32, name=f"w{l}") for l in range(2)]
    bs = P.sb([1, 2, CW], F32)
    os_ = P.sb([1, 2, CW], F32)
    P.dma('sp', cs[:, :], c_in[:, :])
    P.dma('sp', bs[0:1, :, :], b_in.re("(o l) c -> o l c", o=1)[:, :, :])
    for l in range(2):
        wv = w_in.re("l (kc p) c -> l p kc c", p=128)
        P.dma('sp' if l == 0 else 'pool', ws[l][:, :, :], wv[l])
    P.act(sc[:, :], cs[:, :], AF.Silu)
    pss = [P.ps([128, 512], F32, name=f"ps{i}") for i in range(4)]
    for l in range(2):
        for hf in range(2):
            ps = pss[l * 2 + hf]
            for kc in range(16):
                P.mm(ps[0:1, 0:384], sc[:, kc:kc + 1], ws[l][:, kc, hf * 384:(hf + 1) * 384],
                     start=(kc == 0), stop=(kc == 15))
            P.tt('dve', os_[0:1, l, hf * 384:(hf + 1) * 384], ps[0:1, 0:384], bs[0:1, l, hf * 384:(hf + 1) * 384],
                 ALU.add)
    P.dma_out('sp', o.re("(o l) c -> o l c", o=1)[:, :, :], os_[0:1, :, :])
    P.emit()
    return nc


def run_l0(inp):
    nc = build_l0()
    CW = 768
    c = np.ascontiguousarray(inp['c'][0].reshape(16, 128).T)
    maps = []
    for i in range(NCORES):
        maps.append({"c": c,
                     "w": np.ascontiguousarray(inp['ada_w'][:, :, i * CW:(i + 1) * CW]),
                     "b": np.ascontiguousarray(inp['ada_b'][:, i * CW:(i + 1) * CW])})
    res = run(nc, maps)
    mod = np.concatenate([r["o"] for r in res], axis=1)
    return mod


class LinCtx:
    def __init__(self, P, kcmax=16, npsum=4):
        self.P = P
        self.wst = [P.sb([128, kcmax, 256], F32, name=f"wst{i}") for i in range(2)]
        self.wbf = [P.sb([128, kcmax, 256], BF16, name=f"wbf{i}") for i in range(2)]
        self.pss = [P.ps([128, 512], F32, name=f"lps{i}") for i in range(npsum)]
        self.wi = 0
        self.pi = 0

    def next_ps(self):
        p = self.pss[self.pi % len(self.pss)]
        self.pi += 1
        return p


def group_chunks(chunks, maxw=256):
    groups = []
    cur = []
    for (f0, fs, tag) in chunks:
        if cur and cur[-1][0] + cur[-1][1] == f0 and (f0 + fs - cur[0][0]) <= maxw:
            cur.append((f0, fs, tag))
        else:
            if cur:
                groups.append(cur)
            cur = [(f0, fs, tag)]
    if cur:
        groups.append(cur)
    return groups


def linear(L, actT, KP, KC, tgroups, w_dram, chunks, evac, cast_eng='pool'):
    P = L.P
    wv = w_dram.re("(kc p) f -> p kc f", p=KP)
    groups = group_chunks(chunks)
    loaded = {}

    def load(gi):
        g = groups[gi]
        c0 = g[0][0]
        fw = g[-1][0] + g[-1][1] - c0
        b = L.wi % 2
        L.wi += 1
        P.dma('sp', L.wst[b][0:KP, 0:KC, 0:fw], wv[:, :, c0:c0 + fw])
        P.copy(cast_eng, L.wbf[b][0:KP, 0:KC, 0:fw], L.wst[b][0:KP, 0:KC, 0:fw])
        loaded[gi] = (b, c0)
    load(0)
    for gi, g in enumerate(groups):
        if gi + 1 < len(groups):
            load(gi + 1)
        b, c0 = loaded[gi]
        for (f0, fs, tag) in g:
            for (g0, gs) in tgroups:
                ps = L.next_ps()
                for kc in range(KC):
                    P.mm(ps[0:fs, 0:gs], L.wbf[b][0:KP, kc, f0 - c0:f0 - c0 + fs], actT[0:KP, kc, g0:g0 + gs],
                         start=(kc == 0), stop=(kc == KC - 1))
                evac(ps, f0, fs, tag, g0, gs)


def fm_rstd(P, src, KC, T, Dn, eps, rstd, ones, sqb, pss):
    gi = 0
    for g0 in range(0, T, 512):
        gs = min(512, T - g0)
        ps = pss[gi % len(pss)]
        gi += 1
        for kc in range(KC):
            sq = sqb[kc % len(sqb)]
            P.act(sq[:, 0:gs], src(kc, g0, gs), AF.Square)
            P.mm(ps[:, 0:gs], ones[:, :], sq[:, 0:gs], start=(kc == 0), stop=(kc == KC - 1))
        P.act(rstd[:, g0:g0 + gs], ps[:, 0:gs], AF.Sqrt, bias=eps_tile(P, eps)[:, 0:1], scale=1.0 / Dn)
        P.recip(rstd[:, g0:g0 + gs], rstd[:, g0:g0 + gs])


_eps_tiles = {}


def eps_tile(P, eps):
    key = (id(P), eps)
    if key not in _eps_tiles:
        t = P.sb([128, 1], F32, name=f"eps{len(_eps_tiles)}")
        P.memset('pool', t[:, :], eps)
        _eps_tiles[key] = t
    return _eps_tiles[key]


MLA_SCALE = 192 ** -0.5
NSA_SCALE = 64 ** -0.5


def build_l1():
    nc, P = new_prog()
    T = TPC
    TG = [(0, 512), (512, 512)]
    so = seg_offsets()
    xT = P.dram("xT", [D, T], F32)
    vecs = P.dram("vecs", [128, 3, 16], F32)
    w_in = P.dram("w_in", [D, IN0_W], F32)
    mg = P.dram("mg", [128, 2, 4], F32)
    w_qb = P.dram("w_qb", [512, 1536], F32)
    w_kvb = P.dram("w_kvb", [512, 2048], F32)
    cs_in = P.dram("cs", [32, 2, T], F32)
    o_q = P.dram("o_q", [1024, T], BF16, "ExternalOutput")
    o_kv = P.dram("o_kv", [768, T], BF16, "ExternalOutput")
    o_g = P.dram("o_g", [48, T], F32, "ExternalOutput")
    o_z = P.dram("o_z", [1024, T], F32, "ExternalOutput")
    o_mz = P.dram("o_mz", [1024, T], F32, "ExternalOutput")
    o_mq = P.dram("o_mq", [8, 192, T], BF16, "ExternalOutput")
    o_mk = P.dram("o_mk", [8, 128, T], BF16, "ExternalOutput")
    o_mv = P.dram("o_mv", [8, 128, T], BF16, "ExternalOutput")
    o_kpe = P.dram("o_kpe", [64, T], BF16, "ExternalOutput")

    L = LinCtx(P)
    ones = P.sb([128, 128], F32, name="ones")
    P.memset('pool', ones[:, :], 1.0)
    vs = P.sb([128, 3, 16], F32, name="vs")
    gsc = P.sb([128, 16], F32, name="gsc")
    mgs = P.sb([128, 2, 4], F32, name="mgs")
    cs = P.sb([32, 2, T], F32, name="cs")
    P.dma('sp', vs[:, :, :], vecs[:, :, :])
    P.dma('sp', mgs[:, :, :], mg[:, :, :])
    P.dma('sp', cs[:, :, :], cs_in[:, :, :])
    P.stt('dve', gsc[:, :], vs[:, 1, :], 1.0, vs[:, 0, :], ALU.add, ALU.mult)
    xb = [P.sb([128, T], F32, name=f"xb{i}") for i in range(3)]
    sqb = [P.sb([128, 512], F32, name=f"sq{i}") for i in range(2)]
    stat_ps = [P.ps([128, 512], F32, name=f"sps{i}") for i in range(2)]
    rstd = P.sb([128, T], F32, name="rstd")
    hT = P.sb([128, 16, T], BF16, name="hT")
    tmp = P.sb([128, T], F32, name="tmp")
    xv = xT.re("(kc p) t -> p kc t", p=128)

    xi = [0]

    def src1(kc, g0, gs):
        if g0 == 0:
            b = xb[xi[0] % 3]
            xi[0] += 1
            P.dma('sp', b[:, :], xv[:, kc, :])
            src1.cur[kc] = b
        return src1.cur[kc][:, g0:g0 + gs]
    src1.cur = {}
    ps0, ps1 = stat_ps
    for kc in range(16):
        b = xb[kc % 3]
        P.dma('sp', b[:, :], xv[:, kc, :])
        for gi, (g0, gs) in enumerate(TG):
            sq = sqb[gi]
            P.act(sq[:, 0:gs], b[:, g0:g0 + gs], AF.Square)
            P.mm(stat_ps[gi][:, 0:gs], ones[:, :], sq[:, 0:gs], start=(kc == 0), stop=(kc == 15))
    et = eps_tile(P, EPS)
    for gi, (g0, gs) in enumerate(TG):
        P.act(rstd[:, g0:g0 + gs], stat_ps[gi][:, 0:gs], AF.Sqrt, bias=et[:, 0:1], scale=1.0 / D)
        P.recip(rstd[:, g0:g0 + gs], rstd[:, g0:g0 + gs])
    for kc in range(16):
        b = xb[(kc + 1) % 3]
        P.dma('sp', b[:, :], xv[:, kc, :])
        P.tt('pool', tmp[:, :], b[:, :], rstd[:, :], ALU.mult)
        P.ts('dve', hT[:, kc, :], tmp[:, :], gsc[:, kc:kc + 1], vs[:, 2, kc:kc + 1], ALU.mult, ALU.add)

    chunks = []
    for name, (o0, sz) in so.items():
        step = 32 if name == 'mla_kpe' else 128
        for f0 in range(o0, o0 + sz, step):
            chunks.append((f0, min(step, o0 + sz - f0), name))
    qa = P.sb([128, 4, T], F32, name="qa")
    ckv = P.sb([128, 4, T], F32, name="ckv")
    kpeA = P.sb([32, T], F32, name="kpeA")
    st32 = [P.sb([128, 512], F32, name=f"st32_{i}") for i in range(4)]
    st16 = [P.sb([128, 512], BF16, name=f"st16_{i}") for i in range(4)]
    r1 = P.sb([32, 512], F32, name="r1")
    r2 = P.sb([32, 512], F32, name="r2")
    cnt = {'a': 0, 'b': 0}

    def s32():
        cnt['a'] += 1
        return st32[cnt['a'] % 4]

    def s16():
        cnt['b'] += 1
        return st16[cnt['b'] % 4]

    def rope(Av, Bv, g0, gs, scale, out1, out2):
        cosv = cs[0:32, 0, g0:g0 + gs]
        sinv = cs[0:32, 1, g0:g0 + gs]
        P.tt('dve', r1[:, 0:gs], Av, cosv, ALU.mult)
        P.tt('dve', r2[:, 0:gs], Bv, sinv, ALU.mult)
        P.tt('dve', r1[:, 0:gs], r1[:, 0:gs], r2[:, 0:gs], ALU.subtract)
        s = s16()
        P.act(s[0:32, 0:gs], r1[:, 0:gs], AF.Copy, scale=scale)
        P.dma_out('sp', out1, s[0:32, 0:gs])
        P.tt('dve', r1[:, 0:gs], Av, sinv, ALU.mult)
        P.tt('dve', r2[:, 0:gs], Bv, cosv, ALU.mult)
        P.tt('dve', r1[:, 0:gs], r1[:, 0:gs], r2[:, 0:gs], ALU.add)
        s = s16()
        P.act(s[0:32, 0:gs], r1[:, 0:gs], AF.Copy, scale=scale)
        P.dma_out('sp', out2, s[0:32, 0:gs])

    def evac_in(ps, f0, fs, name, g0, gs):
        o0, sz = so[name]
        r0 = f0 - o0
        if name == 'nsa_q':
            s = s16()
            P.act(s[0:fs, 0:gs], ps[0:fs, 0:gs], AF.Copy, scale=NSA_SCALE)
            P.dma_out('sp', o_q[r0:r0 + fs, g0:g0 + gs], s[0:fs, 0:gs])
        elif name == 'nsa_kv':
            s = s16()
            P.copy('dve', s[0:fs, 0:gs], ps[0:fs, 0:gs])
            P.dma_out('sp', o_kv[r0:r0 + fs, g0:g0 + gs], s[0:fs, 0:gs])
        elif name == 'nsa_g':
            s = s32()
            P.act(s[0:fs, 0:gs], ps[0:fs, 0:gs], AF.Sigmoid)
            P.dma_out('sp', o_g[r0:r0 + fs, g0:g0 + gs], s[0:fs, 0:gs])
        elif name in ('nsa_z', 'mla_z'):
            s = s32()
            P.act(s[0:fs, 0:gs], ps[0:fs, 0:gs], AF.Silu)
            dst = o_z if name == 'nsa_z' else o_mz
            P.dma_out('sp', dst[r0:r0 + fs, g0:g0 + gs], s[0:fs, 0:gs])
        elif name == 'mla_qa':
            P.copy('dve', qa[:, r0 // 128, g0:g0 + gs], ps[0:fs, 0:gs])
        elif name == 'mla_ckv':
            P.copy('dve', ckv[:, r0 // 128, g0:g0 + gs], ps[0:fs, 0:gs])
        elif name == 'mla_kpe':
            if r0 == 0:
                P.copy('dve', kpeA[:, g0:g0 + gs], ps[0:32, 0:gs])
            else:
                rope(kpeA[:, g0:g0 + gs], ps[0:32, 0:gs], g0, gs, 1.0,
                     o_kpe[0:32, g0:g0 + gs], o_kpe[32:64, g0:g0 + gs])
    linear(L, hT, 128, 16, TG, w_in, chunks, evac_in)

    qn = P.sb([128, 4, T], BF16, name="qn")
    cn = P.sb([128, 4, T], BF16, name="cn")
    for (srcT, dstT, gi_) in ((qa, qn, 0), (ckv, cn, 1)):
        for gi, (g0, gs) in enumerate(TG):
            for kc in range(4):
                sq = sqb[kc % 2]
                P.act(sq[:, 0:gs], srcT[:, kc, g0:g0 + gs], AF.Square)
                P.mm(stat_ps[gi][:, 0:gs], ones[:, :], sq[:, 0:gs], start=(kc == 0), stop=(kc == 3))
            P.act(rstd[:, g0:g0 + gs], stat_ps[gi][:, 0:gs], AF.Sqrt, bias=et[:, 0:1], scale=1.0 / 512)
            P.recip(rstd[:, g0:g0 + gs], rstd[:, g0:g0 + gs])
        for kc in range(4):
            P.stt('dve', dstT[:, kc, :], srcT[:, kc, :], mgs[:, gi_, kc:kc + 1], rstd[:, :], ALU.mult, ALU.mult)

    qchunks = []
    for h in range(8):
        qchunks.append((h * 192, 128, ('n', h)))
        qchunks.append((h * 192 + 128, 32, ('a', h)))
        qchunks.append((h * 192 + 160, 32, ('b', h)))
    qA = [P.sb([32, 512], F32, name=f"qA{i}") for i in range(2)]

    def evac_q(ps, f0, fs, tag, g0, gs):
        kind, h = tag
        if kind == 'n':
            s = s16()
            P.act(s[0:128, 0:gs], ps[0:128, 0:gs], AF.Copy, scale=MLA_SCALE)
            P.dma_out('sp', o_mq[h, 0:128, g0:g0 + gs], s[0:128, 0:gs])
        elif kind == 'a':
            P.copy('dve', qA[g0 // 512][:, 0:gs], ps[0:32, 0:gs])
        else:
            rope(qA[g0 // 512][:, 0:gs], ps[0:32, 0:gs], g0, gs, MLA_SCALE,
                 o_mq[h, 128:160, g0:g0 + gs], o_mq[h, 160:192, g0:g0 + gs])
    linear(L, qn, 128, 4, TG, w_qb, qchunks, evac_q)

    kvchunks = []
    for h in range(8):
        kvchunks.append((h * 256, 128, ('k', h)))
        kvchunks.append((h * 256 + 128, 128, ('v', h)))

    def evac_kv(ps, f0, fs, tag, g0, gs):
        kind, h = tag
        s = s16()
        P.copy('dve', s[0:128, 0:gs], ps[0:128, 0:gs])
        dst = o_mk if kind == 'k' else o_mv
        P.dma_out('sp', dst[h, :, g0:g0 + gs], s[0:128, 0:gs])
    linear(L, cn, 128, 4, TG, w_kvb, kvchunks, evac_kv)
    P.emit()
    return nc


def rope_tables():
    inv = (10000.0 ** (-np.arange(0, 64, 2, dtype=np.float32) / np.float32(64))).astype(np.float32)
    ang = np.arange(S, dtype=np.float32)[:, None] * inv[None]
    return np.cos(ang).astype(np.float32), np.sin(ang).astype(np.float32)


def pk(v):
    return np.ascontiguousarray(v.reshape(-1, 128).T)


def run_l1(inp, mod):
    nc = build_l1()
    shift, scale = mod[0, 0:D], mod[0, D:2 * D]
    vecs = np.ascontiguousarray(np.stack([pk(inp['norm_g'][0]), pk(scale), pk(shift)], axis=1))
    mg = np.ascontiguousarray(np.stack([pk(inp['mla_qa_g'][0]), pk(inp['mla_kva_g'][0])], axis=1))
    cos, sin = rope_tables()
    xTfull = np.ascontiguousarray(inp['x'][0].T)
    maps = []
    for i in range(NCORES):
        sl = slice(i * TPC, (i + 1) * TPC)
        maps.append({"xT": np.ascontiguousarray(xTfull[:, sl]), "vecs": vecs,
                     "w_in": inp['a_w_in'][0], "mg": mg,
                     "w_qb": inp['mla_w_qb'][0], "w_kvb": inp['mla_w_kvb'][0],
                     "cs": np.ascontiguousarray(np.stack([cos[sl].T, sin[sl].T], axis=1))})
    res = run(nc, maps)
    out = {}
    for k in ("o_q", "o_kv", "o_g", "o_z", "o_mz", "o_kpe"):
        out[k] = np.concatenate([r[k] for r in res], axis=-1)
    for k in ("o_mq", "o_mk", "o_mv"):
        out[k] = np.concatenate([r[k] for r in res], axis=-1)
    return out


NEGM = -30000.0


def blk_of(i, j):
    return 8 * j + (i if j % 2 == 0 else 7 - i)


def split3(x):
    x = np.asarray(x, dtype=np.float64)
    x1 = x.astype(NPBF).astype(np.float64)
    x2 = (x - x1).astype(NPBF).astype(np.float64)
    x3 = (x - x1 - x2).astype(NPBF).astype(np.float64)
    return x1, x2, x3


def alibi_q_rows(tok):
    slopes = (2.0 ** (-8.0 * np.arange(1, 17, dtype=np.float32) / np.float32(16))).astype(np.float32)
    out = np.zeros((16, 10, len(tok)), np.float64)
    for h in range(16):
        s1, s2, s3 = split3(slopes[h])
        t1, t2, t3 = split3(np.float64(slopes[h]) * tok.astype(np.float64))
        out[h, 0] = s1; out[h, 1] = s2; out[h, 2] = s3
        out[h, 3] = s1; out[h, 4] = s2; out[h, 5] = s3
        out[h, 6] = -t1; out[h, 7] = -t2; out[h, 8] = -t3
        out[h, 9] = 1.0
    return out.astype(NPBF)


def alibi_k_rows(pos, valid=None):
    pos = np.asarray(pos, dtype=np.int64)
    out = np.zeros((10, len(pos)), np.float64)
    p = np.maximum(pos, 0)
    hi = (p // 128) * 128
    lo = p % 128
    out[0:3] = hi
    out[3:6] = lo
    out[6:9] = 1.0
    if valid is not None:
        out[9] = np.where(valid, 0.0, NEGM)
    return out.astype(NPBF)


def build_l2():
    nc, P = new_prog()
    NQ = 1024
    d_q = P.dram("qaug", [74, 16 * NQ], BF16)
    d_cmpin = P.dram("cmpin", [2, 2, 64, S], BF16)
    d_kcrows = P.dram("kcrows", [10, 512], BF16)
    d_w1 = P.dram("w1", [2, 64, 32 * 64], F32)
    d_w2 = P.dram("w2", [2, 64, 64], F32)
    d_peT = P.dram("peT", [2, 64, 32], F32)
    d_b1 = P.dram("b1", [64, 2], F32)
    d_vcconst = P.dram("vcconst", [128, 4 * 129], BF16)
    d_cmpmask = P.dram("cmpmask", [128, 4 * NQ], BF16)
    d_tailmask = P.dram("tailmask", [128, 64 * 128], BF16)
    d_winmask = P.dram("winmask", [128, 2 * 512], BF16)
    d_expand = P.dram("expand", [128, S], BF16)
    d_force = P.dram("force", [128, 8 * 128], F32)
    d_ks = P.dram("ks", [2, 74, S], BF16)
    d_vs = P.dram("vs", [2, 128, 64 * 65], BF16)
    d_kw = P.dram("kw", [2, 8, 74, 640], BF16)
    d_vw = P.dram("vw", [2, 8, 128, 5 * 65], BF16)
    d_gates = P.dram("gates", [128, 8 * 48], F32)
    d_sz = P.dram("sz", [8, 128, 1024], F32)
    d_mzz = P.dram("mzz", [8, 128, 1024], F32)
    d_mqn = P.dram("mqn", [128, 8 * NQ], BF16)
    d_mqp = P.dram("mqp", [64, 8 * NQ], BF16)
    d_mkn = P.dram("mkn", [8, 128, S], BF16)
    d_mkp = P.dram("mkp", [64, S], BF16)
    d_mv = P.dram("mv", [8, 128, 64 * 129], BF16)
    d_idb = P.dram("idb", [128, 128], BF16)
    d_idf = P.dram("idf", [128, 128], F32)
    o_yz = P.dram("o_yz", [8, 128, 2048], BF16, "ExternalOutput")

    Qb = P.sb([128, 16, NQ], BF16, name="Qb")
    obuf = P.sb([128, 8, 1024], F32, name="obuf")
    imp = [P.sb([128, 8, 128], F32, name=f"imp{g}") for g in range(2)]
    negmt = [P.sb([128, NQ], BF16, name=f"negmt{g}") for g in range(2)]
    expand = P.sb([128, S], BF16, name="expand")
    tailm = P.sb([128, 64, 128], BF16, name="tailm")
    cmpm = P.sb([128, 4, NQ], BF16, name="cmpm")
    winm = P.sb([128, 2, 512], BF16, name="winm")
    force = P.sb([128, 8, 128], F32, name="force")
    E = [P.sb([128, NQ], BF16, name=f"E{i}") for i in range(4)]
    bufK = P.sb([128, S], BF16, name="bufK")
    bufV = P.sb([128, 64 * 129], BF16, name="bufV")
    kpe = expand
    gat = P.sb([128, 8, 48], F32, name="gat")
    idb = P.sb([128, 128], BF16, name="idb")
    idf = P.sb([128, 128], F32, name="idf")
    zst = [P.sb([128, 1024], F32, name=f"zst{i}") for i in range(2)]
    yst = [P.sb([128, 1024], BF16, name=f"yst{i}") for i in range(2)]
    kcmp = [P.sb([74, 512], BF16, name=f"kcmp{g}") for g in range(2)]
    vcmp = [P.sb([128, 4, 193], BF16, name=f"vcmp{g}") for g in range(2)]
    w1s = obuf.re("p j f -> p (j f)")
    w1b = P.sb([64, 32, 64], BF16, name="w1b")
    w2s = P.sb([64, 64], F32, name="w2s")
    w2b = P.sb([64, 64], BF16, name="w2b")
    peTs = P.sb([64, 32], F32, name="peTs")
    peTb = P.sb([64, 32], BF16, name="peTb")
    b1s = P.sb([64, 2], F32, name="b1s")
    cst = P.sb([64, 1], F32, name="cst")
    hid = P.sb([64, 512], BF16, name="hid")
    sm = [P.sb([128, 8], F32, name=f"sm{i}") for i in range(8)]
    wk = [P.sb([128, 128], F32, name=f"wk{i}") for i in range(3)]
    Sps = [P.ps([128, NQ], F32, name=f"S{i}") for i in range(2)]
    Aps = [P.ps([128, 512], F32, name=f"A{i}") for i in range(4)]

    P.dma('sp', Qb[0:74, :, :], d_q.re("r (h q) -> r h q", h=16)[:, :, :])
    P.dma('sp', idb[:, :], d_idb[:, :])
    P.dma('sp', idf[:, :], d_idf[:, :])
    P.dma('sp', cmpm[:, :, :], d_cmpmask.re("p (c q) -> p c q", c=4)[:, :, :])
    P.dma('sp', gat[:, :, :], d_gates.re("p (j c) -> p j c", j=8)[:, :, :])
    P.dma('sp', force[:, :, :], d_force.re("p (j c) -> p j c", j=8)[:, :, :])
    P.dma('sp', tailm[:, :, :], d_tailmask.re("p (c q) -> p c q", c=64)[:, :, :])
    P.dma('sp', winm[:, :, :], d_winmask.re("p (c q) -> p c q", c=2)[:, :, :])
    P.dma('sp', expand[:, :], d_expand[:, :])
    P.dma('sp', b1s[:, :], d_b1[:, :])
    smi = [0]

    def small():
        smi[0] += 1
        return sm[smi[0] % 8]
    Ei = [0]

    def nextE():
        Ei[0] += 1
        return E[Ei[0] % 4]
    Si = [0]

    def nextS():
        Si[0] += 1
        return Sps[Si[0] % 2]
    Ai = [0]

    def nextA():
        Ai[0] += 1
        return Aps[Ai[0] % 4]

    kcv = bufK.re("p (n r) -> p n r", r=16)
    for g in range(2):
        P.dma('sp', kcmp[g][64:74, :], d_kcrows[:, :])
        P.dma('sp', vcmp[g][:, :, 64:193], d_vcconst.re("p (c f) -> p c f", c=4)[:, :, :])
        P.memset('pool', vcmp[g][:, :, 0:64], 0.0)
        P.memset('pool', kcmp[g][0:64, :], 0.0)
        for kv in range(2):
            P.dma('sp', bufK[0:64, :], d_cmpin[g, kv, :, :])
            P.dma('sp', w1s[0:64, 0:2048], d_w1[kv, :, :])
            P.dma('sp', w2s[:, :], d_w2[kv, :, :])
            P.dma('sp', peTs[:, :], d_peT[kv, :, :])
            P.copy('pool', w1b[:, :, :], obuf[0:64, 0:2, :].tt.re("p j (a e) -> p (j a) e", e=64)[0:64, 0:32, :])
            P.copy('pool', w2b[:, :], w2s[:, :])
            P.copy('pool', peTb[:, :], peTs[:, :])
            a1 = nextA()
            for l in range(32):
                P.mm(a1[0:64, 0:1], w1b[:, l, :], peTb[:, l:l + 1], start=(l == 0), stop=(l == 31))
            P.tt('dve', cst[:, :], a1[0:64, 0:1], b1s[:, kv:kv + 1], ALU.add)
            a2 = nextA()
            for l in range(32):
                rhs = kcv[0:64, 0:511, l] if l < 16 else kcv[0:64, 1:512, l - 16]
                P.mm(a2[0:64, 0:511], w1b[:, l, :], rhs, start=(l == 0), stop=(l == 31))
            P.memset('pool', hid[:, :], 0.0)
            P.act(hid[:, 0:511], a2[0:64, 0:511], AF.Silu, bias=cst[:, 0:1])
            if kv == 0:
                a3 = nextA()
                P.mm(a3[0:64, 0:511], w2b[:, :], hid[:, 0:511])
                P.copy('dve', kcmp[g][0:64, 0:511], a3[0:64, 0:511])
            else:
                for c4 in range(4):
                    a3 = nextA()
                    P.mm(a3[:, 0:64], hid[:, c4 * 128:(c4 + 1) * 128], w2b[:, :])
                    P.copy('dve', vcmp[g][:, c4, 0:64], a3[:, 0:64])

    def finish(acc_v, zcol, gate_v, dst, first):
        s_ = small()
        P.ts('dve', s_[:, 0:1], zcol, 1e-30, None, ALU.max)
        P.recip(s_[:, 1:2], s_[:, 0:1])
        if gate_v is not None:
            P.tt('dve', s_[:, 2:3], s_[:, 1:2], gate_v, ALU.mult)
            sc_ = s_[:, 2:3]
        else:
            sc_ = s_[:, 1:2]
        if first:
            P.ts('dve', dst, acc_v, sc_, None, ALU.mult)
        else:
            P.stt('dve', dst, acc_v, sc_, dst, ALU.mult, ALU.add)
        return s_

    for h in range(16):
        g = h // 8
        Es = []
        for c4 in range(4):
            sp_ = nextS()
            for hf in range(2):
                lo, hi = hf * 512, (hf + 1) * 512
                P.mm(sp_[:, lo:hi], kcmp[g][0:74, c4 * 128:(c4 + 1) * 128], Qb[0:74, h, lo:hi], start=True, stop=False)
                P.mm(sp_[:, lo:hi], idb[:, :], cmpm[:, c4, lo:hi], start=False, stop=True)
            e_ = nextE()
            P.act(e_[:, :], sp_[:, :], AF.Exp)
            Es.append(e_)
        for j in range(8):
            acc = nextA()
            for c4 in range(4):
                P.mm(acc[:, 0:193], Es[c4][:, j * 128:(j + 1) * 128], vcmp[g][:, c4, :], start=(c4 == 0), stop=(c4 == 3))
            s_ = finish(acc[:, 0:64], acc[:, 64:65], gat[:, j, h * 3:h * 3 + 1], obuf[:, j, h * 64:(h + 1) * 64], True)
            if h % 8 == 0:
                P.ts('dve', imp[g][:, j, :], acc[:, 65:193], s_[:, 1:2], None, ALU.mult)
            else:
                P.stt('dve', imp[g][:, j, :], acc[:, 65:193], s_[:, 1:2], imp[g][:, j, :], ALU.mult, ALU.add)
        if h % 8 == 7:
            for j in range(8):
                w0, w1_, w2_ = wk
                s_ = small()
                P.tt('dve', w0[:, :], imp[g][:, j, :], force[:, j, :], ALU.add)
                P.op('dve', lambda e, m0=s_[:, 0:8].ap, i0=w0[:, :].ap: e.max(out=m0, in_=i0), [w0], [s_])
                P.op('dve', lambda e, o=w1_[:, :].ap, m0=s_[:, 0:8].ap, i0=w0[:, :].ap:
                     e.match_replace(out=o, in_to_replace=m0, in_values=i0, imm_value=-1e30), [w0, s_], [w1_])
                s2 = small()
                P.op('dve', lambda e, m0=s2[:, 0:8].ap, i0=w1_[:, :].ap: e.max(out=m0, in_=i0), [w1_], [s2])
                s3 = small()
                P.op('dve', lambda e, o=s3[:, 0:1].ap, i=s2[:, 0:8].ap:
                     e.tensor_reduce(out=o, in_=i, axis=AX.X, op=ALU.min), [s2], [s3])
                P.ts('dve', w2_[:, :], w0[:, :], s3[:, 0:1], None, ALU.is_ge)
                P.ts('dve', w2_[:, :], w2_[:, :], -1.0, -NEGM, ALU.add, ALU.mult)
                a_ = nextA()
                P.transpose(a_[:, 0:128], w2_[:, :], idf[:, :])
                P.copy('dve', negmt[g][:, j * 128:(j + 1) * 128], a_[:, 0:128])

    accpair = [(Aps[0], Aps[1]), (Aps[2], Aps[3])]
    for g in range(2):
        P.dma('sp', bufK[0:74, :], d_ks[g, :, :])
        P.dma('sp', bufV[:, 0:64 * 65], d_vs[g, :, :])
        vsv = bufV.re("p (c f) -> p c f", f=129)
        for hh in range(8):
            h = g * 8 + hh
            accA, accB = accpair[h % 2]
            P.memset('dve', accA[:, :], 0.0)
            P.memset('dve', accB[:, :], 0.0)

            def accv(j, w0_, w):
                t_ = accA if j < 4 else accB
                o_ = (j % 4) * 128
                return t_[:, o_ + w0_:o_ + w0_ + w]
            for c in range(64):
                j0 = c // 8
                sp_ = nextS()
                for hf in range(2):
                    lo, hi = max(j0 * 128, hf * 512), (hf + 1) * 512
                    if lo >= hi:
                        continue
                    tail_here = (j0 * 128 >= hf * 512) and (j0 * 128 < hi)
                    P.mm(sp_[:, lo:hi], bufK[0:74, c * 128:(c + 1) * 128], Qb[0:74, h, lo:hi], start=True, stop=False)
                    P.mm(sp_[:, lo:hi], expand[:, c * 128:(c + 1) * 128], negmt[g][:, lo:hi], start=False,
                         stop=(not tail_here))
                    if tail_here:
                        P.mm(sp_[:, j0 * 128:(j0 + 1) * 128], idb[:, :], tailm[:, c, :], start=False, stop=True)
                e_ = nextE()
                P.act(e_[:, j0 * 128:NQ], sp_[:, j0 * 128:NQ], AF.Exp)
                for j in range(j0, 8):
                    P.mm(accv(j, 0, 65), e_[:, j * 128:(j + 1) * 128], bufV[:, c * 65:(c + 1) * 65],
                         start=False, stop=False, skip=True)
            for j in range(8):
                finish(accv(j, 0, 64), accv(j, 64, 1),
                       gat[:, j, h * 3 + 1:h * 3 + 2], obuf[:, j, h * 64:(h + 1) * 64], False)

    for g in range(2):
        for j in range(8):
            P.dma('sp', bufK[0:74, 0:640], d_kw[g, j, :, :])
            P.dma('sp', bufV[:, 0:5 * 65], d_vw[g, j, :, :])
            accA, accB = accpair[(g * 8 + j) % 2]
            P.memset('dve', accA[:, :], 0.0)
            P.memset('dve', accB[:, :], 0.0)

            def accw(hh, w0_, w):
                t_ = accA if hh < 4 else accB
                o_ = (hh % 4) * 128
                return t_[:, o_ + w0_:o_ + w0_ + w]
            for wc in range(5):
                sp_ = nextS()
                for hf in range(2):
                    lo, hi = hf * 512, (hf + 1) * 512
                    msk = wc in (0, 4)
                    P.mm(sp_.re("p (h q) -> p h q", q=128)[:, hf * 4:(hf + 1) * 4, :],
                         bufK[0:74, wc * 128:(wc + 1) * 128], Qb[0:74, g * 8 + hf * 4:g * 8 + hf * 4 + 4, j * 128:(j + 1) * 128],
                         start=True, stop=(not msk))
                    if msk:
                        P.mm(sp_[:, lo:hi], idb[:, :], winm[:, 0 if wc == 0 else 1, :], start=False, stop=True)
                e_ = nextE()
                P.act(e_[:, :], sp_[:, :], AF.Exp)
                for hh in range(8):
                    P.mm(accw(hh, 0, 65), e_[:, hh * 128:(hh + 1) * 128], bufV[:, wc * 65:(wc + 1) * 65],
                         start=False, stop=False, skip=True)
            for hh in range(8):
                h = g * 8 + hh
                finish(accw(hh, 0, 64), accw(hh, 64, 1), gat[:, j, h * 3 + 2:h * 3 + 3],
                       obuf[:, j, h * 64:(h + 1) * 64], False)

    for j in range(8):
        z_ = zst[j % 2]
        y_ = yst[j % 2]
        P.dma('sp', z_[:, :], d_sz[j, :, :])
        P.tt('pool', y_[:, :], obuf[:, j, :], z_[:, :], ALU.mult)
        P.dma_out('sp', o_yz[j, :, 0:1024], y_[:, :])

    P.dma('sp', Qb[:, 0:8, :], d_mqn.re("p (h q) -> p h q", h=8)[:, :, :])
    P.dma('sp', Qb[0:64, 8:16, :], d_mqp.re("p (h q) -> p h q", h=8)[:, :, :])
    P.dma('sp', kpe[0:64, :], d_mkp[:, :])
    for h in range(8):
        P.dma('sp', bufK[:, :], d_mkn[h, :, :])
        P.dma('sp', bufV[:, :], d_mv[h, :, :])
        for a_ in Aps[0:3]:
            P.memset('dve', a_[:, :], 0.0)

        def accm(j, w0_, w):
            t_ = Aps[j // 3]
            o_ = (j % 3) * 160
            return t_[:, o_ + w0_:o_ + w0_ + w]
        for c in range(64):
            j0 = c // 8
            sp_ = nextS()
            for hf in range(2):
                lo, hi = max(j0 * 128, hf * 512), (hf + 1) * 512
                if lo >= hi:
                    continue
                tail_here = (j0 * 128 >= hf * 512) and (j0 * 128 < hi)
                P.mm(sp_[:, lo:hi], bufK[:, c * 128:(c + 1) * 128], Qb[:, h, lo:hi], start=True, stop=False)
                P.mm(sp_[:, lo:hi], kpe[0:64, c * 128:(c + 1) * 128], Qb[0:64, 8 + h, lo:hi], start=False,
                     stop=(not tail_here))
                if tail_here:
                    P.mm(sp_[:, j0 * 128:(j0 + 1) * 128], idb[:, :], tailm[:, c, :], start=False, stop=True)
            e_ = nextE()
            P.act(e_[:, j0 * 128:NQ], sp_[:, j0 * 128:NQ], AF.Exp)
            for j in range(j0, 8):
                P.mm(accm(j, 0, 129), e_[:, j * 128:(j + 1) * 128], bufV[:, c * 129:(c + 1) * 129],
                     start=False, stop=False, skip=True)
        for j in range(8):
            finish(accm(j, 0, 128), accm(j, 128, 1), None, obuf[:, j, h * 128:(h + 1) * 128], True)
    for j in range(8):
        z_ = zst[j % 2]
        y_ = yst[j % 2]
        P.dma('sp', z_[:, :], d_mzz[j, :, :])
        P.tt('pool', y_[:, :], obuf[:, j, :], z_[:, :], ALU.mult)
        P.dma_out('sp', o_yz[j, :, 1024:2048], y_[:, :])
    P.emit()
    return nc


def run_l2(inp, l1):
    nc = build_l2()
    o_q, o_kv = l1['o_q'], l1['o_kv']
    bf = lambda a: np.ascontiguousarray(np.asarray(a).astype(NPBF))
    n_cmp = 511
    cmp_end = np.arange(512) * 16 + 31
    kcrows = alibi_k_rows(cmp_end)
    vcconst = np.zeros((128, 4, 129), np.float32)
    vcconst[:, :, 0] = 1.0
    for n in range(n_cmp):
        for jb in range(128):
            if 4 * jb - 1 <= n <= 4 * jb + 3:
                vcconst[n % 128, n // 128, 1 + jb] = 1.0
    vcconst = bf(vcconst.reshape(128, 4 * 129))
    expand = np.zeros((128, S), np.float32)
    expand[np.arange(S) // 64, np.arange(S)] = 1.0
    expand = bf(expand)
    kl = np.arange(128)[:, None]
    tl = np.arange(128)[None, :]
    wlo = np.where(kl > tl, 0.0, NEGM).astype(np.float32)
    whi = np.where(kl <= tl, 0.0, NEGM).astype(np.float32)
    winmask = bf(np.stack([np.tile(wlo, (1, 4)), np.tile(whi, (1, 4))], axis=1).reshape(128, 2 * 512))
    idb = bf(np.eye(128, dtype=np.float32))
    idf = np.eye(128, dtype=np.float32)
    w1 = np.ascontiguousarray(np.stack([inp['nsa_w1_k'][0], inp['nsa_w1_v'][0]]).transpose(0, 2, 1, 3).reshape(2, 64, 32 * 64))
    w2 = np.ascontiguousarray(np.stack([inp['nsa_w2_k'][0], inp['nsa_w2_v'][0]]))
    peT = np.ascontiguousarray(np.stack([inp['nsa_pe_k'][0].T, inp['nsa_pe_v'][0].T]))
    b1 = np.ascontiguousarray(np.stack([inp['nsa_b1_k'][0], inp['nsa_b1_v'][0]], axis=1))
    kvr = o_kv.reshape(6, 2, 64, S)
    cmpin = np.ascontiguousarray(np.stack([np.stack([kvr[0, g], kvr[1, g]]) for g in range(2)]))
    krows_all = alibi_k_rows(np.arange(S))
    ks = np.ascontiguousarray(np.stack([np.concatenate([kvr[2, g], krows_all], axis=0) for g in range(2)]))
    ones_col = np.ones((S, 1), NPBF)

    def tokmajor(vT, width):
        a = np.concatenate([vT.T, ones_col], axis=1)
        return np.ascontiguousarray(a.reshape(64, 128, width).transpose(1, 0, 2).reshape(128, 64 * width))
    vs = np.stack([tokmajor(kvr[3, g], 65) for g in range(2)])
    mkn = np.ascontiguousarray(l1['o_mk'])
    mkp = np.ascontiguousarray(l1['o_kpe'])
    mv = np.stack([tokmajor(l1['o_mv'][h], 129) for h in range(8)])
    gT, zT, mzT = l1['o_g'], l1['o_z'], l1['o_mz']
    maps = []
    toks = []
    for i in range(NCORES):
        blks = [blk_of(i, j) for j in range(8)]
        tok = np.concatenate([np.arange(b * 128, (b + 1) * 128) for b in blks])
        toks.append(tok)
        qrows = alibi_q_rows(tok)
        qa = np.concatenate([o_q[:, tok].reshape(16, 64, 1024), qrows], axis=1)
        qaug = np.ascontiguousarray(qa.transpose(1, 0, 2).reshape(74, 16 * 1024))
        cm = np.where(cmp_end[:, None] <= tok[None, :], 0.0, NEGM).astype(np.float32)
        cm[511, :] = NEGM
        cmpmask = bf(cm.reshape(4, 128, 1024).transpose(1, 0, 2).reshape(128, 4 * 1024))
        tm = np.zeros((128, 64, 128), np.float32)
        for c in range(64):
            j = c // 8
            kpos = c * 128 + np.arange(128)
            tpos = blks[j] * 128 + np.arange(128)
            tm[:, c, :] = np.where(kpos[:, None] <= tpos[None, :], 0.0, NEGM)
        tailmask = bf(tm.reshape(128, 64 * 128))
        fo = np.zeros((128, 8, 128), np.float32)
        for j in range(8):
            tpos = blks[j] * 128 + np.arange(128)
            cur = tpos // 64
            jb = np.arange(128)[None, :]
            f = np.zeros((128, 128), np.float32)
            f[jb == (cur[:, None] - 1)] = 3e9
            f[jb == cur[:, None]] = 2e9
            f[np.broadcast_to(jb == 0, (128, 128))] = 1e9
            fo[:, j, :] = f
        kw = np.zeros((2, 8, 74, 640), NPBF)
        vw = np.zeros((2, 8, 128, 5, 65), NPBF)
        for j in range(8):
            pos = (blks[j] - 4) * 128 + np.arange(640)
            valid = pos >= 0
            pc = np.maximum(pos, 0)
            rows = alibi_k_rows(pos, valid)
            for g in range(2):
                kk = kvr[4, g][:, pc].copy()
                kk[:, ~valid] = 0
                kw[g, j] = np.concatenate([kk, rows], axis=0)
                vv = kvr[5, g][:, pc].T.copy()
                vv[~valid] = 0
                vv = np.concatenate([vv, np.ones((640, 1), NPBF)], axis=1)
                vw[g, j] = vv.reshape(5, 128, 65).transpose(1, 0, 2)
        gates = np.ascontiguousarray(gT[:, tok].T.reshape(8, 128, 48).transpose(1, 0, 2).reshape(128, 8 * 48))
        sz = np.ascontiguousarray(zT[:, tok].T.reshape(8, 128, 1024))
        mzz = np.ascontiguousarray(mzT[:, tok].T.reshape(8, 128, 1024))
        mq = l1['o_mq'][:, :, tok]
        mqn = np.ascontiguousarray(mq[:, 0:128].transpose(1, 0, 2).reshape(128, 8 * 1024))
        mqp = np.ascontiguousarray(mq[:, 128:192].transpose(1, 0, 2).reshape(64, 8 * 1024))
        maps.append(dict(qaug=qaug, cmpin=cmpin, kcrows=kcrows, w1=w1, w2=w2, peT=peT, b1=b1, vcconst=vcconst,
                         cmpmask=cmpmask, tailmask=tailmask, winmask=winmask, expand=expand,
                         force=np.ascontiguousarray(fo.reshape(128, 8 * 128)), ks=ks, vs=vs,
                         kw=kw, vw=np.ascontiguousarray(vw.reshape(2, 8, 128, 5 * 65)), gates=gates, sz=sz, mzz=mzz,
                         mqn=mqn, mqp=mqp, mkn=mkn, mkp=mkp, mv=mv, idb=idb, idf=idf))
    res = run(nc, maps)
    yz = np.zeros((S, 2048), NPBF)
    for i in range(NCORES):
        yz[toks[i]] = res[i]["o_yz"].reshape(1024, 2048)
    return yz


def build_l3(stop=99, part='a'):
    nc, P = new_prog()
    T = TPC
    T1 = T + 1
    TGX = [(0, 1), (1, 512), (513, 512)]
    TG = [(0, 512), (512, 512)]
    A_ = "ExternalInput" if part == 'a' else "Internal"
    B_ = "ExternalInput" if part == 'b' else "Internal"
    AO = "ExternalOutput" if part == 'a' else "Internal"
    BO = "ExternalOutput" if part == 'b' else "Internal"
    yzT = P.dram("yzT", [D, T1], BF16, A_)
    xT = P.dram("xT", [D, T1], F32, A_)
    w_out = P.dram("w_out", [D, D], F32, A_)
    h_io = P.dram("h_io", [D, T1], F32, "ExternalOutput" if part == 'a' else "ExternalInput")
    vecs = P.dram("vecs", [128, 10, 16], F32)
    mu_in = P.dram("mu", [128, 6, 16], F32)
    pm_in = P.dram("pm", [128, 1], F32)
    bones_in = P.dram("bones", [128, 128], F32)
    wr = P.dram("w_r", [D, D], F32, B_)
    wk_ = P.dram("w_k", [D, D], F32, A_)
    wv = P.dram("w_v", [D, D], F32, B_)
    wz = P.dram("w_z", [D, D], F32, B_)
    w1 = P.dram("w1", [D, 96], F32, A_)
    w2 = P.dram("w2", [96, D], F32, A_)
    a1 = P.dram("a1", [D, 96], F32, A_)
    a2 = P.dram("a2", [96, D], F32, A_)
    o_x1 = P.dram("o_x1", [D, T], F32, AO)
    o_r = P.dram("o_r", [D, T], BF16, BO)
    o_v = P.dram("o_v", [D, T], BF16, BO)
    o_kap = P.dram("o_kap", [D, T], BF16, AO)
    o_b = P.dram("o_b", [D, T], BF16, AO)
    o_km = P.dram("o_km", [D, T], BF16, AO)
    o_lw = P.dram("o_lw", [D, T], F32, AO)
    o_bonus = P.dram("o_bonus", [D, T], F32, BO)
    o_sz = P.dram("o_sz", [D, T], F32, BO)
    s_a = P.dram("s_a", [D, T], F32, "Internal")
    s_km = P.dram("s_km", [D, T], F32, "ExternalOutput" if part == 'a' else "ExternalInput")
    s_rk = P.dram("s_rk", [D, T], F32, "Internal")

    L = LinCtx(P)
    ones = P.sb([128, 128], F32, name="ones")
    P.memset('pool', ones[:, :], 1.0)
    bones = P.sb([128, 128], F32, name="bones")
    P.dma('sp', bones[:, :], bones_in[:, :])
    vs = P.sb([128, 10, 16], F32, name="vs")
    P.dma('sp', vs[:, :, :], vecs[:, :, :])
    mus = P.sb([128, 6, 16], F32, name="mus")
    omu = P.sb([128, 6, 16], F32, name="omu")
    P.dma('sp', mus[:, :, :], mu_in[:, :, :])
    P.ts('dve', omu[:, :, :], mus[:, :, :], -1.0, 1.0, ALU.mult, ALU.add)
    pm = P.sb([128, 1], F32, name="pm")
    P.dma('sp', pm[:, :], pm_in[:, :])
    gsc = P.sb([128, 16], F32, name="gsc")
    P.stt('dve', gsc[:, :], vs[:, 2, :], 1.0, vs[:, 1, :], ALU.add, ALU.mult)
    ab = P.sb([128, 16, T1], BF16, name="ab")
    hs = P.sb([128, 16, T1], F32, name="hs")
    xb = [P.sb([128, T1], F32, name=f"xb{i}") for i in range(2)]
    sqb = [P.sb([128, 512], F32, name=f"sq{i}") for i in range(2)]
    stat_ps = [P.ps([128, 512], F32, name=f"sps{i}") for i in range(3)]
    rstd = P.sb([128, T1], F32, name="rstd")
    tmp = P.sb([128, T1], F32, name="tmp")
    st32 = [P.sb([128, 1024], F32, name=f"st32_{i}") for i in range(4)]
    st16 = [P.sb([128, 1024], BF16, name=f"st16_{i}") for i in range(4)]
    ld32 = [P.sb([128, 1024], F32, name=f"ld32_{i}") for i in range(2)]
    cur = {}
    e32 = [P.sb([128, 512], F32, name=f"e32_{i}") for i in range(3)]
    lora = P.sb([96, 1, T], BF16, name="lora")
    cnt = {'a': 0, 'b': 0, 'c': 0, 'd': 0}

    def s32(key, g0):
        if g0 == 0:
            cnt['a'] += 1
            cur[key] = st32[cnt['a'] % 4]
        return cur[key]

    def s16(key, g0):
        if g0 == 0:
            cnt['b'] += 1
            cur[key] = st16[cnt['b'] % 4]
        return cur[key]

    def l32(src, f0, g0):
        if g0 == 0:
            cnt['c'] += 1
            t_ = ld32[cnt['c'] % 2]
            P.dma('act', t_[:, :], src[f0:f0 + 128, :])
            cur[('l', f0)] = t_
        return cur[('l', f0)]

    def t32():
        cnt['d'] += 1
        return e32[cnt['d'] % 3]

    if part == 'b':
        P.mute = True
    P.dma('sp', ab[:, :, :], yzT.re("(kc p) t -> p kc t", p=128)[:, :, :])
    xv = xT.re("(kc p) t -> p kc t", p=128)
    allch = [(f0, 128, None) for f0 in range(0, D, 128)]
    xcur = {}

    def evac_out(ps, f0, fs, tag, g0, gs):
        fc = f0 // 128
        if g0 == 0:
            b = xb[fc % 2]
            P.dma('sp', b[:, :], xv[:, fc, :])
            xcur[fc] = b
        b = xcur[fc]
        P.stt('dve', hs[:, fc, g0:g0 + gs], ps[:, 0:gs], vs[:, 0, fc:fc + 1], b[:, g0:g0 + gs], ALU.mult, ALU.add)
        if g0 == 513:
            P.dma_out('sp', o_x1[f0:f0 + 128, :], hs[:, fc, 1:T1])
    linear(L, ab, 128, 16, TGX, w_out, allch, evac_out)
    if stop == 1:
        P.emit()
        return nc

    et = eps_tile(P, EPS)
    for kc in range(16):
        for gi, (g0, gs) in enumerate(TGX):
            sq = sqb[gi % 2]
            P.act(sq[:, 0:gs], hs[:, kc, g0:g0 + gs], AF.Square)
            P.mm(stat_ps[gi][:, 0:gs], ones[:, :], sq[:, 0:gs], start=(kc == 0), stop=(kc == 15))
    for gi, (g0, gs) in enumerate(TGX):
        P.act(rstd[:, g0:g0 + gs], stat_ps[gi][:, 0:gs], AF.Sqrt, bias=et[:, 0:1], scale=1.0 / D)
        P.recip(rstd[:, g0:g0 + gs], rstd[:, g0:g0 + gs])
    P.ts('dve', rstd[:, 0:1], rstd[:, 0:1], pm[:, 0:1], None, ALU.mult)
    for kc in range(16):
        P.tt('pool', tmp[:, :], hs[:, kc, :], rstd[:, :], ALU.mult)
        P.ts('dve', hs[:, kc, :], tmp[:, :], gsc[:, kc:kc + 1], vs[:, 3, kc:kc + 1], ALU.mult, ALU.add)
    for kc in range(16):
        P.ts('dve', hs[:, kc, 0:1], hs[:, kc, 0:1], pm[:, 0:1], None, ALU.mult)

    def mix(m):
        for kc in range(16):
            P.ts('pool', tmp[:, 0:T], hs[:, kc, 0:T], mus[:, m, kc:kc + 1], None, ALU.mult)
            P.stt('dve', ab[:, kc, 0:T], hs[:, kc, 1:T1], omu[:, m, kc:kc + 1], tmp[:, 0:T], ALU.mult, ALU.add)

    def store(dst, f0, g0, gs, st, scratch=False):
        if g0 == 512:
            if scratch:
                P.dma('sp', dst[f0:f0 + 128, :], st[:, :], owner=st)
            else:
                P.dma_out('sp', dst[f0:f0 + 128, :], st[:, :])

    if stop == 2:
        P.emit()
        return nc
    mix(4)

    def evac_l1(func):
        def f(ps, f0, fs, tag, g0, gs):
            P.act(lora[0:96, 0, g0:g0 + gs], ps[0:96, 0:gs], func)
        return f
    linear(L, ab, 128, 16, TG, a1, [(0, 96, None)], evac_l1(AF.Copy))

    def evac_a(ps, f0, fs, tag, g0, gs):
        fc = f0 // 128
        s = s32('a', g0)
        P.act(s[:, g0:g0 + gs], ps[:, 0:gs], AF.Sigmoid, bias=vs[:, 5, fc:fc + 1])
        store(s_a, f0, g0, gs, s, scratch=True)
    linear(L, lora, 96, 1, TG, a2, allch, evac_a)

    if stop == 3:
        P.emit()
        return nc
    mix(1)
    linear(L, ab, 128, 16, TG, w1, [(0, 96, None)], evac_l1(AF.Tanh))

    def evac_w(ps, f0, fs, tag, g0, gs):
        fc = f0 // 128
        s = s32('w', g0)
        P.act(s[:, g0:g0 + gs], ps[:, 0:gs], AF.Sigmoid, bias=vs[:, 4, fc:fc + 1])
        P.ts('dve', s[:, g0:g0 + gs], s[:, g0:g0 + gs], -float(np.exp(-0.5)), None, ALU.mult)
        store(o_lw, f0, g0, gs, s)
    linear(L, lora, 96, 1, TG, w2, allch, evac_w)

    if stop == 4:
        P.emit()
        return nc
    mix(2)
    nka = P.sb([128, 16], F32, name="nka")
    P.ts('dve', nka[:, :], vs[:, 7, :], -1.0, None, ALU.mult)

    def evac_k(ps, f0, fs, tag, g0, gs):
        fc = f0 // 128
        laf = l32(s_a, f0, g0)
        la = TT_view(laf, laf.h[:, g0:g0 + gs])
        kkr = t32()
        P.ts('dve', kkr[:, 0:gs], ps[:, 0:gs], vs[:, 6, fc:fc + 1], None, ALU.mult)
        sq = t32()
        P.tt('pool', sq[:, 0:gs], kkr[:, 0:gs], kkr[:, 0:gs], ALU.mult)
        bp = stat_ps[(g0 // 512) % 2]
        P.mm(bp[:, 0:gs], bones[:, :], sq[:, 0:gs])
        nr = t32()
        P.act(nr[:, 0:gs], bp[:, 0:gs], AF.Sqrt)
        P.ts('dve', nr[:, 0:gs], nr[:, 0:gs], 1e-12, None, ALU.max)
        P.recip(nr[:, 0:gs], nr[:, 0:gs])
        P.tt('dve', kkr[:, 0:gs], kkr[:, 0:gs], nr[:, 0:gs], ALU.mult)
        s = s16('kap', g0)
        P.copy('pool', s[:, g0:g0 + gs], kkr[:, 0:gs])
        store(o_kap, f0, g0, gs, s)
        s = s16('b', g0)
        P.tt('dve', s[:, g0:g0 + gs], kkr[:, 0:gs], la[:, 0:gs], ALU.mult)
        store(o_b, f0, g0, gs, s)
        P.ts('dve', sq[:, 0:gs], la[:, 0:gs], vs[:, 7, fc:fc + 1], nka[:, fc:fc + 1], ALU.mult, ALU.add)
        km = s32('km', g0)
        P.stt('dve', km[:, g0:g0 + gs], sq[:, 0:gs], 1.0, ps[:, 0:gs], ALU.add, ALU.mult)
        s = s16('km16', g0)
        P.copy('pool', s[:, g0:g0 + gs], km[:, g0:g0 + gs])
        store(s_km, f0, g0, gs, km, scratch=True)
        store(o_km, f0, g0, gs, s)
    linear(L, ab, 128, 16, TG, wk_, allch, evac_k)

    if part == 'a':
        hv_ = h_io.re("(kc p) t -> p kc t", p=128)
        for kc in range(16):
            P.dma_out('sp', hv_[:, kc, :], hs[:, kc, :])
        P.emit()
        return nc
    P.mute = False
    for kc in range(16):
        P.dma('sp', hs[:, kc, :], h_io.re("(kc p) t -> p kc t", p=128)[:, kc, :])
    mix(0)

    def evac_r(ps, f0, fs, tag, g0, gs):
        fc = f0 // 128
        lkf = l32(s_km, f0, g0)
        lk = TT_view(lkf, lkf.h[:, g0:g0 + gs])
        s = s16('r', g0)
        P.copy('dve', s[:, g0:g0 + gs], ps[:, 0:gs])
        store(o_r, f0, g0, gs, s)
        t_ = t32()
        P.stt('dve', t_[:, 0:gs], ps[:, 0:gs], vs[:, 8, fc:fc + 1], lk[:, 0:gs], ALU.mult, ALU.mult)
        bp = stat_ps[(g0 // 512) % 2]
        P.mm(bp[:, 0:gs], bones[:, :], t_[:, 0:gs])
        s = s32('rk', g0)
        P.copy('act', s[:, g0:g0 + gs], bp[:, 0:gs])
        store(s_rk, f0, g0, gs, s, scratch=True)
    linear(L, ab, 128, 16, TG, wr, allch, evac_r)

    if stop == 6:
        P.emit()
        return nc
    mix(3)

    def evac_v(ps, f0, fs, tag, g0, gs):
        lkf = l32(s_rk, f0, g0)
        lk = TT_view(lkf, lkf.h[:, g0:g0 + gs])
        s = s16('v', g0)
        P.copy('dve', s[:, g0:g0 + gs], ps[:, 0:gs])
        store(o_v, f0, g0, gs, s)
        s = s32('bon', g0)
        P.tt('dve', s[:, g0:g0 + gs], ps[:, 0:gs], lk[:, 0:gs], ALU.mult)
        store(o_bonus, f0, g0, gs, s)
    linear(L, ab, 128, 16, TG, wv, allch, evac_v)

    mix(5)

    def evac_z(ps, f0, fs, tag, g0, gs):
        s = s32('z', g0)
        P.act(s[:, g0:g0 + gs], ps[:, 0:gs], AF.Silu)
        store(o_sz, f0, g0, gs, s)
    linear(L, ab, 128, 16, TG, wz, allch, evac_z)
    P.emit()
    return nc


def run_l3(inp, mod, yz, stop=99):
    gate0 = mod[0, 2 * D:3 * D]
    shift1, scale1 = mod[1, 0:D], mod[1, D:2 * D]
    vecs = np.ascontiguousarray(np.stack([pk(gate0), pk(inp['norm_g'][1]), pk(scale1), pk(shift1),
                                          pk(inp['r_w0'][0]), pk(inp['r_a0'][0]), pk(inp['r_k_k'][0]),
                                          pk(inp['r_k_a'][0]), pk(inp['r_r_k'][0]), pk(np.zeros(D, np.float32))], axis=1))
    mu = np.ascontiguousarray(np.stack([pk(inp['r_mu'][0][m]) for m in range(6)], axis=1))
    bones = np.zeros((128, 128), np.float32)
    bones[0:64, 0:64] = 1.0
    bones[64:128, 64:128] = 1.0
    xTf = np.ascontiguousarray(inp['x'][0].T)
    yzTf = np.ascontiguousarray(yz.T)
    xTp = np.concatenate([np.zeros((D, 1), np.float32), xTf], axis=1)
    yzTp = np.concatenate([np.zeros((D, 1), NPBF), yzTf], axis=1)
    nc = build_l3(stop, 'a')
    maps = []
    for i in range(NCORES):
        sl = slice(i * TPC, (i + 1) * TPC + 1)
        maps.append(dict(yzT=np.ascontiguousarray(yzTp[:, sl]), xT=np.ascontiguousarray(xTp[:, sl]),
                         w_out=inp['a_w_out'][0], vecs=vecs, mu=mu,
                         pm=np.full((128, 1), 0.0 if i == 0 else 1.0, np.float32), bones=bones,
                         w_k=inp['r_w_k'][0],
                         w1=inp['r_w1'][0], w2=inp['r_w2'][0], a1=inp['r_a1'][0], a2=inp['r_a2'][0]))
    resa = run(nc, maps)
    out = {}
    for k in ("o_x1", "o_kap", "o_b", "o_km", "o_lw"):
        out[k] = np.concatenate([r[k] for r in resa], axis=1)
    nc = build_l3(stop, 'b')
    maps = []
    for i in range(NCORES):
        maps.append(dict(h_io=resa[i]["h_io"], s_km=resa[i]["s_km"], vecs=vecs, mu=mu,
                         pm=np.full((128, 1), 0.0 if i == 0 else 1.0, np.float32), bones=bones,
                         w_r=inp['r_w_r'][0], w_v=inp['r_w_v'][0], w_z=inp['r_w_z'][0]))
    resb = run(nc, maps)
    for k in ("o_r", "o_v", "o_bonus", "o_sz"):
        out[k] = np.concatenate([r[k] for r in resb], axis=1)
    return out


LNX_EPS = 64e-5
CH = 64
NCH = S // CH


def build_l4():
    nc, P = new_prog()
    SCN = 8
    NSC = NCH // SCN
    d_tok = {k: P.dram("t_" + k, [64, NCH * 256], BF16) for k in ("v", "kap", "b", "km")}
    d_tlw = P.dram("t_lw", [64, NCH * 256], F32)
    d_f = {k: P.dram("f_" + k, [64, 4, S], BF16) for k in ("r", "kap", "b", "km")}
    d_flw = P.dram("f_lw", [64, 4, S], F32)
    d_c = P.dram("consts", [64, 6, 256], F32)
    o_yn = P.dram("o_yn", [64, NCH * 256], F32, "ExternalOutput")

    cs_ = P.sb([64, 6, 256], F32, name="consts")
    P.dma('sp', cs_[:, :, :], d_c[:, :, :])
    tri = cs_[:, 0, 0:64]
    ones64 = cs_[:, 1, 0:64]
    id64 = cs_[:, 2, 0:64]
    I4 = cs_[:, 2, :]
    mUs = cs_[:, 3, :]
    mUi = cs_[:, 4, :]
    mLs = cs_[:, 5, :]
    Tt = {k: P.sb([64, SCN, 256], BF16, name="T_" + k) for k in ("v", "kap", "b", "km")}
    Tlw = P.sb([64, SCN, 256], F32, name="T_lw")
    Ft = {k: P.sb([64, 4, 512], BF16, name="F_" + k) for k in ("r", "kap", "b", "km")}
    Flw = P.sb([64, 4, 512], F32, name="F_lw")
    At = P.sb([64, SCN, 256], F32, name="At")
    Bh = P.sb([64, SCN, 256], F32, name="Bh")
    Kh = P.sb([64, SCN, 256], F32, name="Kh")
    Vt = P.sb([64, SCN, 256], F32, name="Vt")
    Rt = P.sb([64, 4, 512], F32, name="Rt")
    AtT = P.sb([64, 4, 512], F32, name="AtT")
    BtT = P.sb([64, 4, 512], F32, name="BtT")
    KtT = P.sb([64, 4, 512], F32, name="KtT")
    gC = P.sb([64, 4, SCN], F32, name="gC")
    lg = P.sb([64, SCN, 256], F32, name="lg")
    d1 = P.sb([64, SCN, 256], F32, name="d1")
    d2 = P.sb([64, SCN, 256], F32, name="d2")
    lgf = P.sb([64, 4, 512], F32, name="lgf")
    ef = P.sb([64, 4, 512], F32, name="ef")
    ef2 = P.sb([64, 4, 512], F32, name="ef2")
    banks = [P.ps([128, 512], F32, name=f"bk{i}") for i in range(8)]
    bi = [0]

    def nb():
        bi[0] += 1
        return banks[bi[0] % 8]
    tmps = {}

    def tm(name, n=2, shape=(64, 256)):
        if name not in tmps:
            tmps[name] = [[P.sb(list(shape), F32, name=f"{name}{i}") for i in range(n)], 0]
        l = tmps[name]
        l[1] += 1
        return l[0][l[1] % n]
    Hs = [P.sb([64, 256], F32, name=f"H{i}") for i in range(2)]
    P.memset('pool', Hs[0][:, :], 0.0)
    hsl = lambda h: slice(h * 64, (h + 1) * 64)

    def mm4(fn_l, fn_r):
        ps = nb()
        for h in range(4):
            P.mm(ps[0:64, hsl(h)], fn_l(h), fn_r(h))
        return ps

    for sc in range(NSC):
        c0 = sc * SCN
        for k in ("v", "kap", "b", "km"):
            P.dma('sp', Tt[k][:, :, :], d_tok[k].re("p (c f) -> p c f", f=256)[:, c0:c0 + SCN, :])
        for k in ("r", "kap", "b", "km"):
            P.dma('sp', Ft[k][:, :, :], d_f[k][:, :, sc * 512:(sc + 1) * 512])
        P.dma('sp', Tlw[:, :, :], d_tlw.re("p (c f) -> p c f", f=256)[:, c0:c0 + SCN, :])
        P.dma('sp', Flw[:, :, :], d_flw[:, :, sc * 512:(sc + 1) * 512])
        for p_ in range(SCN // 2):
            psL = nb()
            psC = nb()
            for cc in range(2):
                c = 2 * p_ + cc
                P.mm(psL[0:64, cc * 256:(cc + 1) * 256], tri, Tlw[:, c, :])
                P.mm(psC[0:64, cc * 256:(cc + 1) * 256], ones64, Tlw[:, c, :])
            P.copy('act', lg.re("p c f -> p (c f)")[:, p_ * 512:(p_ + 1) * 512], psL[0:64, :])
            P.tt('dve', d2.re("p c f -> p (c f)")[:, p_ * 512:(p_ + 1) * 512], psC[0:64, :],
                 lg.re("p c f -> p (c f)")[:, p_ * 512:(p_ + 1) * 512], ALU.subtract)
        P.tt('pool', d1[:, :, :], lg[:, :, :], Tlw[:, :, :], ALU.subtract)
        P.act(d1[:, :, :], d1[:, :, :], AF.Exp)
        P.act(d2[:, :, :], d2[:, :, :], AF.Exp)
        P.stt('dve', At[:, :, :], Tt["kap"][:, :, :], -1.0, d1[:, :, :], ALU.mult, ALU.mult)
        P.tt('pool', Bh[:, :, :], Tt["b"][:, :, :], d2[:, :, :], ALU.mult)
        P.tt('dve', Kh[:, :, :], Tt["km"][:, :, :], d2[:, :, :], ALU.mult)
        P.copy('pool', Vt[:, :, :], Tt["v"][:, :, :])
        for h in range(4):
            psF = nb()
            for c in range(SCN):
                P.mm(psF[0:64, c * 64:(c + 1) * 64], Tlw[:, c, hsl(h)], tri)
            P.copy('act', lgf[:, h, :], psF[0:64, :])
        P.act(ef[:, :, :], lgf[:, :, :], AF.Exp)
        P.tt('dve', Rt[:, :, :], Ft["r"][:, :, :], ef[:, :, :], ALU.mult)
        P.copy('pool', gC[:, :, :], ef.re("p h (c t) -> p h c t", t=64)[:, :, :, 63])
        P.act(ef2[:, :, :], lgf[:, :, :], AF.Exp, scale=-1.0)
        P.tt('dve', BtT[:, :, :], Ft["b"][:, :, :], ef2[:, :, :], ALU.mult)
        P.tt('pool', KtT[:, :, :], Ft["km"][:, :, :], ef2[:, :, :], ALU.mult)
        P.tt('pool', lgf[:, :, :], lgf[:, :, :], Flw[:, :, :], ALU.subtract)
        P.act(ef2[:, :, :], lgf[:, :, :], AF.Exp)
        P.stt('dve', AtT[:, :, :], Ft["kap"][:, :, :], -1.0, ef2[:, :, :], ALU.mult, ALU.mult)

        for c in range(SCN):
            cs = slice(c * 64, (c + 1) * 64)
            cg = c0 + c
            ps = mm4(lambda h: BtT[:, h, cs], lambda h: AtT[:, h, cs])
            XT = tm("XT", 3)
            P.tt('dve', XT[:, :], ps[0:64, 0:256], mUs, ALU.mult)
            ps = mm4(lambda h: AtT[:, h, cs], lambda h: BtT[:, h, cs])
            X = tm("X", 3)
            P.tt('dve', X[:, :], ps[0:64, 0:256], mLs, ALU.mult)
            ps = mm4(lambda h: KtT[:, h, cs], lambda h: AtT[:, h, cs])
            AakT = tm("AakT")
            P.tt('dve', AakT[:, :], ps[0:64, 0:256], mUs, ALU.mult)
            ps = mm4(lambda h: BtT[:, h, cs], lambda h: Rt[:, h, cs])
            ArbT = tm("ArbT")
            P.tt('dve', ArbT[:, :], ps[0:64, 0:256], mUi, ALU.mult)
            ps = mm4(lambda h: KtT[:, h, cs], lambda h: Rt[:, h, cs])
            ArkT = tm("ArkT")
            P.tt('dve', ArkT[:, :], ps[0:64, 0:256], mUi, ALU.mult)
            W = tm("W", 3)
            P.tt('pool', W[:, :], XT[:, :], I4, ALU.add)
            for it in range(5):
                psa = mm4(lambda h: XT[:, hsl(h)], lambda h: X[:, hsl(h)])
                Xn = tm("X", 3)
                P.copy('act', Xn[:, :], psa[0:64, 0:256])
                if it < 4:
                    psb = mm4(lambda h: X[:, hsl(h)], lambda h: XT[:, hsl(h)])
                    XTn = tm("XT", 3)
                    P.copy('act', XTn[:, :], psb[0:64, 0:256])
                psc = mm4(lambda h: Xn[:, hsl(h)], lambda h: W[:, hsl(h)])
                Wn = tm("W", 3)
                P.tt('dve', Wn[:, :], psc[0:64, 0:256], W[:, :], ALU.add)
                X, W = Xn, Wn
                if it < 4:
                    XT = XTn
            ps = mm4(lambda h: AakT[:, hsl(h)], lambda h: Vt[:, c, hsl(h)])
            X2 = tm("X2")
            P.copy('act', X2[:, :], ps[0:64, 0:256])
            ps = mm4(lambda h: W[:, hsl(h)], lambda h: X2[:, hsl(h)])
            U2 = tm("U2")
            P.copy('act', U2[:, :], ps[0:64, 0:256])
            ps = mm4(lambda h: W[:, hsl(h)], lambda h: At[:, c, hsl(h)])
            Ap = tm("Ap")
            P.copy('act', Ap[:, :], ps[0:64, 0:256])
            ps = mm4(lambda h: Ap[:, hsl(h)], lambda h: ArbT[:, hsl(h)])
            RpT = tm("RpT")
            P.tt('dve', RpT.re("p (h t) -> p h t", h=4)[:, :, :], ps.re("p (h t) -> p h t", t=64)[0:64, 0:4, :],
                 Rt[:, :, cs], ALU.add)
            ps = mm4(lambda h: Ap[:, hsl(h)], lambda h: Bh[:, c, hsl(h)])
            PhiT = tm("PhiT")
            for h in range(4):
                P.stt('dve', PhiT[:, hsl(h)], id64, gC[:, h, c:c + 1], ps[0:64, hsl(h)], ALU.mult, ALU.add)
            Hc, Hn = Hs[cg % 2], Hs[(cg + 1) % 2]
            psY = nb()
            psH = nb()
            for h in range(4):
                P.mm(psY[0:64, hsl(h)], RpT[:, hsl(h)], Hc[:, hsl(h)], start=True, stop=False)
                P.mm(psY[0:64, hsl(h)], ArbT[:, hsl(h)], U2[:, hsl(h)], start=False, stop=False)
                P.mm(psY[0:64, hsl(h)], ArkT[:, hsl(h)], Vt[:, c, hsl(h)], start=False, stop=True)
            for h in range(4):
                P.mm(psH[0:64, hsl(h)], PhiT[:, hsl(h)], Hc[:, hsl(h)], start=True, stop=False)
                P.mm(psH[0:64, hsl(h)], Bh[:, c, hsl(h)], U2[:, hsl(h)], start=False, stop=False)
                P.mm(psH[0:64, hsl(h)], Kh[:, c, hsl(h)], Vt[:, c, hsl(h)], start=False, stop=True)
            P.copy('act', Hn[:, :], psH[0:64, 0:256])
            ysb = tm("ysb")
            P.copy('act', ysb[:, :], psY[0:64, 0:256])
            st = tm("st", 2, (64, 16))
            ysb3 = ysb.re("p (h v) -> p h v", h=4)
            P.op('dve', lambda e, o=st[:, 0:4].ap, i=ysb3[:, :, :].ap: e.tensor_reduce(out=o, in_=i, axis=AX.X, op=ALU.add),
                 [ysb], [st])
            P.ts('dve', st[:, 4:8], st[:, 0:4], -1.0 / 64, None, ALU.mult)
            cen = tm("cen")
            for h in range(4):
                P.ts('pool' if h % 2 else 'dve', cen[:, hsl(h)], ysb[:, hsl(h)], st[:, 4 + h:5 + h], None, ALU.add)
            sq = tm("sqq")
            P.tt('pool', sq[:, :], cen[:, :], cen[:, :], ALU.mult)
            st2 = tm("st", 2, (64, 16))
            P.op('dve', lambda e, o=st2[:, 0:4].ap, i=sq.re("p (h v) -> p h v", h=4)[:, :, :].ap:
                 e.tensor_reduce(out=o, in_=i, axis=AX.X, op=ALU.add), [sq], [st2])
            P.ts('dve', st2[:, 4:8], st2[:, 0:4], 1.0 / 64, LNX_EPS, ALU.mult, ALU.add)
            P.act(st2[:, 8:12], st2[:, 4:8], AF.Sqrt)
            P.recip(st2[:, 12:16], st2[:, 8:12])
            yo = tm("yo", 4)
            for h in range(4):
                P.ts('pool' if h % 2 else 'dve', yo[:, hsl(h)], cen[:, hsl(h)], st2[:, 12 + h:13 + h], None, ALU.mult)
            P.dma_out('sp', o_yn[:, cg * 256:(cg + 1) * 256], yo[:, :])
    P.emit()
    return nc


def run_l4(l3):
    nc = build_l4()
    consts = np.zeros((64, 6, 256), np.float32)
    s_ = np.arange(64)[:, None]
    t_ = np.arange(64)[None, :]
    consts[:, 0, 0:64] = (s_ <= t_)
    consts[:, 1, 0:64] = 1.0
    consts[:, 2, :] = np.tile(np.eye(64, dtype=np.float32), (1, 4))
    consts[:, 3, :] = np.tile((t_ > s_).astype(np.float32), (1, 4))
    consts[:, 4, :] = np.tile((t_ >= s_).astype(np.float32), (1, 4))
    consts[:, 5, :] = np.tile((t_ < s_).astype(np.float32), (1, 4))
    maps = []
    for i in range(NCORES):
        chs = slice(i * 256, (i + 1) * 256)
        m = {"consts": consts}

        def tokl(a):
            return np.ascontiguousarray(a[chs, :].T.reshape(NCH, 64, 256).transpose(1, 0, 2).reshape(64, NCH * 256))

        def featl(a):
            return np.ascontiguousarray(a[chs, :].reshape(4, 64, S).transpose(1, 0, 2))
        for k, src in (("v", "o_v"), ("kap", "o_kap"), ("b", "o_b"), ("km", "o_km")):
            m["t_" + k] = tokl(l3[src])
        m["t_lw"] = tokl(l3["o_lw"])
        for k, src in (("r", "o_r"), ("kap", "o_kap"), ("b", "o_b"), ("km", "o_km")):
            m["f_" + k] = featl(l3[src])
        m["f_lw"] = featl(l3["o_lw"])
        maps.append(m)
    res = run(nc, maps)
    yn = np.zeros((S, D), np.float32)
    for i in range(NCORES):
        a = res[i]["o_yn"].reshape(64, NCH, 256).transpose(1, 0, 2).reshape(S, 256)
        yn[:, i * 256:(i + 1) * 256] = a
    return yn


def build_l5():
    nc, P = new_prog()
    T = TPC
    TG = [(0, 512), (512, 512)]
    ynT = P.dram("ynT", [D, T], F32)
    bonus = P.dram("bonus", [D, T], F32)
    szT = P.dram("szT", [D, T], F32)
    x1T = P.dram("x1T", [D, T], F32)
    w_o = P.dram("w_o", [D, D], F32)
    vecs = P.dram("vecs", [128, 4, 16], F32)
    o_out = P.dram("o_out", [D, T], F32, "ExternalOutput")
    L = LinCtx(P)
    ones = P.sb([128, 128], F32, name="ones")
    P.memset('pool', ones[:, :], 1.0)
    vs = P.sb([128, 4, 16], F32, name="vs")
    P.dma('sp', vs[:, :, :], vecs[:, :, :])
    yb = P.sb([128, 16, T], BF16, name="yb")
    xs = P.sb([128, 16, T], F32, name="xs")
    lb = [[P.sb([128, T], F32, name=f"lb{j}_{i}") for i in range(2)] for j in range(3)]
    sqb = [P.sb([128, 512], F32, name=f"sq{i}") for i in range(2)]
    stat_ps = [P.ps([128, 512], F32, name=f"sps{i}") for i in range(2)]
    rstd = P.sb([128, T], F32, name="rstd")
    x1b = [P.sb([128, T], F32, name=f"x1b{i}") for i in range(2)]
    ost = [P.sb([128, T], F32, name=f"ost{i}") for i in range(2)]
    for kc in range(16):
        a, b, c = lb[0][kc % 2], lb[1][kc % 2], lb[2][kc % 2]
        P.dma('sp', a[:, :], ynT.re("(kc p) t -> p kc t", p=128)[:, kc, :])
        P.dma('sp', b[:, :], bonus.re("(kc p) t -> p kc t", p=128)[:, kc, :])
        P.dma('sp', c[:, :], szT.re("(kc p) t -> p kc t", p=128)[:, kc, :])
        P.ts('dve', a[:, :], a[:, :], vs[:, 0, kc:kc + 1], vs[:, 1, kc:kc + 1], ALU.mult, ALU.add)
        P.tt('pool', a[:, :], a[:, :], b[:, :], ALU.add)
        P.tt('dve', yb[:, kc, :], a[:, :], c[:, :], ALU.mult)
    xcur = {}

    def evac(ps, f0, fs, tag, g0, gs):
        fc = f0 // 128
        if g0 == 0:
            b = x1b[fc % 2]
            P.dma('sp', b[:, :], x1T.re("(kc p) t -> p kc t", p=128)[:, fc, :])
            xcur[fc] = b
        b = xcur[fc]
        P.stt('dve', xs[:, fc, g0:g0 + gs], ps[:, 0:gs], vs[:, 2, fc:fc + 1], b[:, g0:g0 + gs], ALU.mult, ALU.add)
    linear(L, yb, 128, 16, TG, w_o, [(f0, 128, None) for f0 in range(0, D, 128)], evac)
    et = eps_tile(P, EPS)
    for kc in range(16):
        for gi, (g0, gs) in enumerate(TG):
            sq = sqb[gi]
            P.act(sq[:, 0:gs], xs[:, kc, g0:g0 + gs], AF.Square)
            P.mm(stat_ps[gi][:, 0:gs], ones[:, :], sq[:, 0:gs], start=(kc == 0), stop=(kc == 15))
    for gi, (g0, gs) in enumerate(TG):
        P.act(rstd[:, g0:g0 + gs], stat_ps[gi][:, 0:gs], AF.Sqrt, bias=et[:, 0:1], scale=1.0 / D)
        P.recip(rstd[:, g0:g0 + gs], rstd[:, g0:g0 + gs])
    for kc in range(16):
        o = ost[kc % 2]
        P.stt('dve', o[:, :], xs[:, kc, :], vs[:, 3, kc:kc + 1], rstd[:, :], ALU.mult, ALU.mult)
        P.dma_out('sp', o_out[kc * 128:(kc + 1) * 128, :], o[:, :])
    P.emit()
    return nc


def run_l5(inp, mod, l3, yn):
    nc = build_l5()
    gate1 = mod[1, 2 * D:3 * D]
    vecs = np.ascontiguousarray(np.stack([pk(inp['r_lnx_g'][0]), pk(inp['r_lnx_b'][0]), pk(gate1),
                                          pk(inp['final_g'])], axis=1))
    ynT = np.ascontiguousarray(yn.T)
    maps = []
    for i in range(NCORES):
        sl = slice(i * TPC, (i + 1) * TPC)
        maps.append(dict(ynT=np.ascontiguousarray(ynT[:, sl]), bonus=np.ascontiguousarray(l3["o_bonus"][:, sl]),
                         szT=np.ascontiguousarray(l3["o_sz"][:, sl]), x1T=np.ascontiguousarray(l3["o_x1"][:, sl]),
                         w_o=inp['r_w_o'][0], vecs=vecs))
    res = run(nc, maps)
    outT = np.concatenate([r["o_out"] for r in res], axis=1)
    return np.ascontiguousarray(outT.T)[None].astype(np.float32)


def kernel(**inputs):
    inp = {k: np.asarray(v) for k, v in inputs.items()}
    mod = run_l0(inp)
    l1 = run_l1(inp, mod)
    yz = run_l2(inp, l1)
    del l1
    l3 = run_l3(inp, mod, yz)
    yn = run_l4(l3)
    return run_l5(inp, mod, l3, yn)
```

```python
from contextlib import ExitStack
import numpy as np
import ml_dtypes
import concourse.bass as bass
import concourse.mybir as mybir
from concourse.bass_utils import run_bass_kernel_spmd

F32 = mybir.dt.float32
BF16 = mybir.dt.bfloat16
AF = mybir.ActivationFunctionType
ALU = mybir.AluOpType
AX = mybir.AxisListType
NPBF = ml_dtypes.bfloat16
NCORES = 8

ENGS = ('pe', 'act', 'dve', 'pool', 'sp')


class V:
    __slots__ = ('tt', 'ap')

    def __init__(self, tt, ap):
        self.tt = tt
        self.ap = ap


class TT:
    def __init__(self, h, name):
        self.h = h
        self.name = name
        self.w = None
        self.r = {}
        self.dsem = None
        self.dval = 0

    def __getitem__(self, idx):
        return V(self, self.h[idx])

    def re(self, pattern, **kw):
        return TT_view(self, self.h.rearrange(pattern, **kw))


class TT_view:
    def __init__(self, tt, ap):
        self.tt = tt
        self.apx = ap

    def __getitem__(self, idx):
        return V(self.tt, self.apx[idx])


class Prog:
    def __init__(self, nc):
        self.nc = nc
        self.es = ExitStack()
        self.q = {e: [] for e in ENGS}
        self.seq = {e: 0 for e in ENGS}
        self.sems = {}
        self.known = {e: {} for e in ENGS}
        for e in ENGS:
            self._sem('E_' + e)
        self.ntile = 0
        self.outs = []
        self.outsem = {}
        self.mute = False

    def _sem(self, key):
        s = self.es.enter_context(self.nc.semaphore(key))
        self.sems[key] = s
        return key

    def sb(self, shape, dt, name=None):
        self.ntile += 1
        name = "sb_" + (name or f"t{self.ntile}")
        h = self.es.enter_context(self.nc.sbuf_tensor(name, list(shape), dt))
        return TT(h, name)

    def ps(self, shape, dt, name=None):
        self.ntile += 1
        name = "ps_" + (name or f"p{self.ntile}")
        h = self.es.enter_context(self.nc.psum_tensor(name, list(shape), dt))
        return TT(h, name)

    def dram(self, name, shape, dt, kind="ExternalInput"):
        h = self.nc.dram_tensor(name, list(shape), dt, kind=kind)
        t = TT(h.ap(), name)
        if kind == "ExternalOutput":
            self.outs.append(t)
        return t

    def _collect(self, e, reads, writes):
        waits = {}

        def need(ev, war=False):
            if ev is None:
                return
            key, val, eng = ev
            if eng == e:
                if e == 'pe':
                    return
            if self.known[e].get(key, 0) >= val:
                return
            if waits.get(key, 0) < val:
                waits[key] = val
        for t in reads:
            need(t.w)
        for t in writes:
            need(t.w)
            for r in t.r.values():
                need(r, war=True)
        for key, val in waits.items():
            self.known[e][key] = val
            self.q[e].append(('wait', key, val))

    def op(self, e, fn, reads=(), writes=()):
        if self.mute:
            return None
        reads = [t for t in reads if t is not None]
        self._collect(e, reads, writes)
        self.seq[e] += 1
        ev = ('E_' + e, self.seq[e], e)
        self.q[e].append(('op', fn, 'E_' + e))
        for t in reads:
            t.r[ev[0]] = ev
        for t in writes:
            t.w = ev
            t.r = {}
        return ev

    def dma(self, e, out, in_, owner=None):
        if self.mute:
            return None
        pairs = out if isinstance(out, list) else [(out, in_)]
        reads = list({id(i.tt): i.tt for (_, i) in pairs}.values())
        writes = list({id(o.tt): o.tt for (o, _) in pairs}.values())
        if owner is None:
            owner = writes[0]
        if owner.dsem is None:
            owner.dsem = self._sem('D_' + owner.name)
        self._collect(e, reads, writes)
        for (o, i) in pairs:
            owner.dval += 16
            self.q[e].append(('dma', o.ap, i.ap, owner.dsem))
        ev = (owner.dsem, owner.dval, 'dma')
        for t in reads:
            t.r[ev[0]] = ev
        for t in writes:
            t.w = ev
            t.r = {}
        return ev

    def mm(self, out, lhsT, rhs, start=True, stop=True, skip=False):
        o, l, r = out.ap, lhsT.ap, rhs.ap
        if skip:
            fn = lambda e: e.matmul(o, lhsT=l, rhs=r, start=start, stop=stop, skip_group_check=True)
        else:
            fn = lambda e: e.matmul(o, lhsT=l, rhs=r, start=start, stop=stop)
        return self.op('pe', fn, [lhsT.tt, rhs.tt], [out.tt])

    def transpose(self, out, in_, ident):
        o, i, d = out.ap, in_.ap, ident.ap
        return self.op('pe', lambda e: e.transpose(o, i, d), [in_.tt, ident.tt], [out.tt])

    def act(self, out, in_, func, bias=None, scale=None, accum=None, eng='act'):
        kw = {}
        rd = [in_.tt]
        wr = [out.tt]
        if bias is not None:
            if isinstance(bias, V):
                kw['bias'] = bias.ap
                rd.append(bias.tt)
            else:
                kw['bias'] = bias
        if scale is not None:
            if isinstance(scale, V):
                kw['scale'] = scale.ap
                rd.append(scale.tt)
            else:
                kw['scale'] = scale
        if accum is not None:
            kw['accum_out'] = accum.ap
            wr.append(accum.tt)
        o, i = out.ap, in_.ap
        return self.op(eng, lambda e: e.activation(out=o, in_=i, func=func, **kw), rd, wr)

    def tt(self, eng, out, in0, in1, op):
        o, a, b = out.ap, in0.ap, in1.ap
        return self.op(eng, lambda e: e.tensor_tensor(out=o, in0=a, in1=b, op=op), [in0.tt, in1.tt], [out.tt])

    def ts(self, eng, out, in0, s1, s2=None, op0=ALU.mult, op1=None):
        rd = [in0.tt]
        a1 = s1
        if isinstance(s1, V):
            a1 = s1.ap
            rd.append(s1.tt)
        a2 = s2
        if isinstance(s2, V):
            a2 = s2.ap
            rd.append(s2.tt)
        o, a = out.ap, in0.ap
        if op1 is None:
            fn = lambda e: e.tensor_scalar(out=o, in0=a, scalar1=a1, scalar2=None, op0=op0)
        else:
            fn = lambda e: e.tensor_scalar(out=o, in0=a, scalar1=a1, scalar2=a2, op0=op0, op1=op1)
        return self.op(eng, fn, rd, [out.tt])

    def stt(self, eng, out, in0, scalar, in1, op0, op1):
        rd = [in0.tt, in1.tt]
        s = scalar
        if isinstance(scalar, V):
            s = scalar.ap
            rd.append(scalar.tt)
        o, a, b = out.ap, in0.ap, in1.ap
        return self.op(eng, lambda e: e.scalar_tensor_tensor(out=o, in0=a, scalar=s, in1=b, op0=op0, op1=op1),
                       rd, [out.tt])

    def copy(self, eng, out, in_):
        o, i = out.ap, in_.ap
        if eng == 'act':
            return self.op(eng, lambda e: e.activation(out=o, in_=i, func=AF.Copy), [in_.tt], [out.tt])
        return self.op(eng, lambda e: e.tensor_copy(out=o, in_=i), [in_.tt], [out.tt])

    def memset(self, eng, out, val):
        o = out.ap
        return self.op(eng, lambda e: e.memset(o, val), [], [out.tt])

    def recip(self, out, in_):
        o, i = out.ap, in_.ap
        return self.op('dve', lambda e: e.reciprocal(out=o, in_=i), [in_.tt], [out.tt])

    def emit(self):
        nc = self.nc
        sems = self.sems
        q = self.q
        waits = {}
        for t in self.outs:
            if t.w is not None:
                key, val, _ = t.w
                waits[key] = max(waits.get(key, 0), val)
        for key, val in waits.items():
            q['sp'].append(('wait', key, val))
        for key, val in self.outsem.items():
            q['sp'].append(('wait', key, val))

        def replay(eng, items):
            for it in items:
                if it[0] == 'wait':
                    eng.wait_ge(sems[it[1]], it[2])
                elif it[0] == 'op':
                    it[1](eng).then_inc(sems[it[2]], 1)
                elif it[0] == 'dma':
                    eng.dma_start(out=it[1], in_=it[2]).then_inc(sems[it[3]], 16)
        with nc.Block() as block:
            @block.sync
            def _(eng):
                replay(eng, q['sp'])

            @block.tensor
            def _(eng):
                replay(eng, q['pe'])

            @block.scalar
            def _(eng):
                replay(eng, q['act'])

            @block.vector
            def _(eng):
                replay(eng, q['dve'])

            @block.gpsimd
            def _(eng):
                replay(eng, q['pool'])
        self.es.close()

    def dma_out(self, e, out, in_):
        ev = self.dma(e, out, in_, owner=in_.tt)
        if ev is None:
            return None
        self.outsem[ev[0]] = max(self.outsem.get(ev[0], 0), ev[1])
        return ev


def new_prog():
    nc = bass.Bass("TRN2", target_bir_lowering=False)
    return nc, Prog(nc)


def run(nc, in_maps):
    res = run_bass_kernel_spmd(nc, in_maps, core_ids=list(range(NCORES)))
    return res.results


D = 2048
S = 8192
TPC = S // NCORES
EPS = 1e-6
IN0_SEGS = [('nsa_q', 1024), ('nsa_kv', 768), ('nsa_g', 48), ('nsa_z', 1024),
            ('mla_qa', 512), ('mla_ckv', 512), ('mla_kpe', 64), ('mla_z', 1024)]
IN0_W = 4976


def seg_offsets():
    o = {}
    acc = 0
    for n, s in IN0_SEGS:
        o[n] = (acc, s)
        acc += s
    return o


def build_l0():
    nc, P = new_prog()
    CW = 768
    c_in = P.dram("c", [128, 16], F32)
    w_in = P.dram("w", [2, D, CW], F32)
    b_in = P.dram("b", [2, CW], F32)
    o = P.dram("o", [2, CW], F32, "ExternalOutput")
    cs = P.sb([128, 16], F32)
    sc = P.sb([128, 16], F32)
    ws = [P.sb([128, 16, CW], F32, name=f"w{l}") for l in range(2)]
    bs = P.sb([1, 2, CW], F32)
    os_ = P.sb([1, 2, CW], F32)
    P.dma('sp', cs[:, :], c_in[:, :])
    P.dma('sp', bs[0:1, :, :], b_in.re("(o l) c -> o l c", o=1)[:, :, :])
    for l in range(2):
        wv = w_in.re("l (kc p) c -> l p kc c", p=128)
        P.dma('sp' if l == 0 else 'pool', ws[l][:, :, :], wv[l])
    P.act(sc[:, :], cs[:, :], AF.Silu)
    pss = [P.ps([128, 512], F32, name=f"ps{i}") for i in range(4)]
    for l in range(2):
        for hf in range(2):
            ps = pss[l * 2 + hf]
            for kc in range(16):
                P.mm(ps[0:1, 0:384], sc[:, kc:kc + 1], ws[l][:, kc, hf * 384:(hf + 1) * 384],
                     start=(kc == 0), stop=(kc == 15))
            P.tt('dve', os_[0:1, l, hf * 384:(hf + 1) * 384], ps[0:1, 0:384], bs[0:1, l, hf * 384:(hf + 1) * 384],
                 ALU.add)
    P.dma_out('sp', o.re("(o l) c -> o l c", o=1)[:, :, :], os_[0:1, :, :])
    P.emit()
    return nc


def run_l0(inp):
    nc = build_l0()
    CW = 768
    c = np.ascontiguousarray(inp['c'][0].reshape(16, 128).T)
    maps = []
    for i in range(NCORES):
        maps.append({"c": c,
                     "w": np.ascontiguousarray(inp['ada_w'][:, :, i * CW:(i + 1) * CW]),
                     "b": np.ascontiguousarray(inp['ada_b'][:, i * CW:(i + 1) * CW])})
    res = run(nc, maps)
    mod = np.concatenate([r["o"] for r in res], axis=1)
    return mod


class LinCtx:
    def __init__(self, P, kcmax=16, npsum=4):
        self.P = P
        self.wst = [P.sb([128, kcmax, 256], F32, name=f"wst{i}") for i in range(2)]
        self.wbf = [P.sb([128, kcmax, 256], BF16, name=f"wbf{i}") for i in range(2)]
        self.pss = [P.ps([128, 512], F32, name=f"lps{i}") for i in range(npsum)]
        self.wi = 0
        self.pi = 0

    def next_ps(self):
        p = self.pss[self.pi % len(self.pss)]
        self.pi += 1
        return p


def group_chunks(chunks, maxw=256):
    groups = []
    cur = []
    for (f0, fs, tag) in chunks:
        if cur and cur[-1][0] + cur[-1][1] == f0 and (f0 + fs - cur[0][0]) <= maxw:
            cur.append((f0, fs, tag))
        else:
            if cur:
                groups.append(cur)
            cur = [(f0, fs, tag)]
    if cur:
        groups.append(cur)
    return groups


def linear(L, actT, KP, KC, tgroups, w_dram, chunks, evac, cast_eng='pool'):
    P = L.P
    wv = w_dram.re("(kc p) f -> p kc f", p=KP)
    groups = group_chunks(chunks)
    loaded = {}

    def load(gi):
        g = groups[gi]
        c0 = g[0][0]
        fw = g[-1][0] + g[-1][1] - c0
        b = L.wi % 2
        L.wi += 1
        P.dma('sp', L.wst[b][0:KP, 0:KC, 0:fw], wv[:, :, c0:c0 + fw])
        P.copy(cast_eng, L.wbf[b][0:KP, 0:KC, 0:fw], L.wst[b][0:KP, 0:KC, 0:fw])
        loaded[gi] = (b, c0)
    load(0)
    for gi, g in enumerate(groups):
        if gi + 1 < len(groups):
            load(gi + 1)
        b, c0 = loaded[gi]
        for (f0, fs, tag) in g:
            for (g0, gs) in tgroups:
                ps = L.next_ps()
                for kc in range(KC):
                    P.mm(ps[0:fs, 0:gs], L.wbf[b][0:KP, kc, f0 - c0:f0 - c0 + fs], actT[0:KP, kc, g0:g0 + gs],
                         start=(kc == 0), stop=(kc == KC - 1))
                evac(ps, f0, fs, tag, g0, gs)


def fm_rstd(P, src, KC, T, Dn, eps, rstd, ones, sqb, pss):
    gi = 0
    for g0 in range(0, T, 512):
        gs = min(512, T - g0)
        ps = pss[gi % len(pss)]
        gi += 1
        for kc in range(KC):
            sq = sqb[kc % len(sqb)]
            P.act(sq[:, 0:gs], src(kc, g0, gs), AF.Square)
            P.mm(ps[:, 0:gs], ones[:, :], sq[:, 0:gs], start=(kc == 0), stop=(kc == KC - 1))
        P.act(rstd[:, g0:g0 + gs], ps[:, 0:gs], AF.Sqrt, bias=eps_tile(P, eps)[:, 0:1], scale=1.0 / Dn)
        P.recip(rstd[:, g0:g0 + gs], rstd[:, g0:g0 + gs])


_eps_tiles = {}


def eps_tile(P, eps):
    key = (id(P), eps)
    if key not in _eps_tiles:
        t = P.sb([128, 1], F32, name=f"eps{len(_eps_tiles)}")
        P.memset('pool', t[:, :], eps)
        _eps_tiles[key] = t
    return _eps_tiles[key]


MLA_SCALE = 192 ** -0.5
NSA_SCALE = 64 ** -0.5


def build_l1():
    nc, P = new_prog()
    T = TPC
    TG = [(0, 512), (512, 512)]
    so = seg_offsets()
    xT = P.dram("xT", [D, T], F32)
    vecs = P.dram("vecs", [128, 3, 16], F32)
    w_in = P.dram("w_in", [D, IN0_W], F32)
    mg = P.dram("mg", [128, 2, 4], F32)
    w_qb = P.dram("w_qb", [512, 1536], F32)
    w_kvb = P.dram("w_kvb", [512, 2048], F32)
    cs_in = P.dram("cs", [32, 2, T], F32)
    o_q = P.dram("o_q", [1024, T], BF16, "ExternalOutput")
    o_kv = P.dram("o_kv", [768, T], BF16, "ExternalOutput")
    o_g = P.dram("o_g", [48, T], F32, "ExternalOutput")
    o_z = P.dram("o_z", [1024, T], F32, "ExternalOutput")
    o_mz = P.dram("o_mz", [1024, T], F32, "ExternalOutput")
    o_mq = P.dram("o_mq", [8, 192, T], BF16, "ExternalOutput")
    o_mk = P.dram("o_mk", [8, 128, T], BF16, "ExternalOutput")
    o_mv = P.dram("o_mv", [8, 128, T], BF16, "ExternalOutput")
    o_kpe = P.dram("o_kpe", [64, T], BF16, "ExternalOutput")

    L = LinCtx(P)
    ones = P.sb([128, 128], F32, name="ones")
    P.memset('pool', ones[:, :], 1.0)
    vs = P.sb([128, 3, 16], F32, name="vs")
    gsc = P.sb([128, 16], F32, name="gsc")
    mgs = P.sb([128, 2, 4], F32, name="mgs")
    cs = P.sb([32, 2, T], F32, name="cs")
    P.dma('sp', vs[:, :, :], vecs[:, :, :])
    P.dma('sp', mgs[:, :, :], mg[:, :, :])
    P.dma('sp', cs[:, :, :], cs_in[:, :, :])
    P.stt('dve', gsc[:, :], vs[:, 1, :], 1.0, vs[:, 0, :], ALU.add, ALU.mult)
    xb = [P.sb([128, T], F32, name=f"xb{i}") for i in range(3)]
    sqb = [P.sb([128, 512], F32, name=f"sq{i}") for i in range(2)]
    stat_ps = [P.ps([128, 512], F32, name=f"sps{i}") for i in range(2)]
    rstd = P.sb([128, T], F32, name="rstd")
    hT = P.sb([128, 16, T], BF16, name="hT")
    tmp = P.sb([128, T], F32, name="tmp")
    xv = xT.re("(kc p) t -> p kc t", p=128)

    xi = [0]

    def src1(kc, g0, gs):
        if g0 == 0:
            b = xb[xi[0] % 3]
            xi[0] += 1
            P.dma('sp', b[:, :], xv[:, kc, :])
            src1.cur[kc] = b
        return src1.cur[kc][:, g0:g0 + gs]
    src1.cur = {}
    ps0, ps1 = stat_ps
    for kc in range(16):
        b = xb[kc % 3]
        P.dma('sp', b[:, :], xv[:, kc, :])
        for gi, (g0, gs) in enumerate(TG):
            sq = sqb[gi]
            P.act(sq[:, 0:gs], b[:, g0:g0 + gs], AF.Square)
            P.mm(stat_ps[gi][:, 0:gs], ones[:, :], sq[:, 0:gs], start=(kc == 0), stop=(kc == 15))
    et = eps_tile(P, EPS)
    for gi, (g0, gs) in enumerate(TG):
        P.act(rstd[:, g0:g0 + gs], stat_ps[gi][:, 0:gs], AF.Sqrt, bias=et[:, 0:1], scale=1.0 / D)
        P.recip(rstd[:, g0:g0 + gs], rstd[:, g0:g0 + gs])
    for kc in range(16):
        b = xb[(kc + 1) % 3]
        P.dma('sp', b[:, :], xv[:, kc, :])
        P.tt('pool', tmp[:, :], b[:, :], rstd[:, :], ALU.mult)
        P.ts('dve', hT[:, kc, :], tmp[:, :], gsc[:, kc:kc + 1], vs[:, 2, kc:kc + 1], ALU.mult, ALU.add)

    chunks = []
    for name, (o0, sz) in so.items():
        step = 32 if name == 'mla_kpe' else 128
        for f0 in range(o0, o0 + sz, step):
            chunks.append((f0, min(step, o0 + sz - f0), name))
    qa = P.sb([128, 4, T], F32, name="qa")
    ckv = P.sb([128, 4, T], F32, name="ckv")
    kpeA = P.sb([32, T], F32, name="kpeA")
    st32 = [P.sb([128, 512], F32, name=f"st32_{i}") for i in range(4)]
    st16 = [P.sb([128, 512], BF16, name=f"st16_{i}") for i in range(4)]
    r1 = P.sb([32, 512], F32, name="r1")
    r2 = P.sb([32, 512], F32, name="r2")
    cnt = {'a': 0, 'b': 0}

    def s32():
        cnt['a'] += 1
        return st32[cnt['a'] % 4]

    def s16():
        cnt['b'] += 1
        return st16[cnt['b'] % 4]

    def rope(Av, Bv, g0, gs, scale, out1, out2):
        cosv = cs[0:32, 0, g0:g0 + gs]
        sinv = cs[0:32, 1, g0:g0 + gs]
        P.tt('dve', r1[:, 0:gs], Av, cosv, ALU.mult)
        P.tt('dve', r2[:, 0:gs], Bv, sinv, ALU.mult)
        P.tt('dve', r1[:, 0:gs], r1[:, 0:gs], r2[:, 0:gs], ALU.subtract)
        s = s16()
        P.act(s[0:32, 0:gs], r1[:, 0:gs], AF.Copy, scale=scale)
        P.dma_out('sp', out1, s[0:32, 0:gs])
        P.tt('dve', r1[:, 0:gs], Av, sinv, ALU.mult)
        P.tt('dve', r2[:, 0:gs], Bv, cosv, ALU.mult)
        P.tt('dve', r1[:, 0:gs], r1[:, 0:gs], r2[:, 0:gs], ALU.add)
        s = s16()
        P.act(s[0:32, 0:gs], r1[:, 0:gs], AF.Copy, scale=scale)
        P.dma_out('sp', out2, s[0:32, 0:gs])

    def evac_in(ps, f0, fs, name, g0, gs):
        o0, sz = so[name]
        r0 = f0 - o0
        if name == 'nsa_q':
            s = s16()
            P.act(s[0:fs, 0:gs], ps[0:fs, 0:gs], AF.Copy, scale=NSA_SCALE)
            P.dma_out('sp', o_q[r0:r0 + fs, g0:g0 + gs], s[0:fs, 0:gs])
        elif name == 'nsa_kv':
            s = s16()
            P.copy('dve', s[0:fs, 0:gs], ps[0:fs, 0:gs])
            P.dma_out('sp', o_kv[r0:r0 + fs, g0:g0 + gs], s[0:fs, 0:gs])
        elif name == 'nsa_g':
            s = s32()
            P.act(s[0:fs, 0:gs], ps[0:fs, 0:gs], AF.Sigmoid)
            P.dma_out('sp', o_g[r0:r0 + fs, g0:g0 + gs], s[0:fs, 0:gs])
        elif name in ('nsa_z', 'mla_z'):
            s = s32()
            P.act(s[0:fs, 0:gs], ps[0:fs, 0:gs], AF.Silu)
            dst = o_z if name == 'nsa_z' else o_mz
            P.dma_out('sp', dst[r0:r0 + fs, g0:g0 + gs], s[0:fs, 0:gs])
        elif name == 'mla_qa':
            P.copy('dve', qa[:, r0 // 128, g0:g0 + gs], ps[0:fs, 0:gs])
        elif name == 'mla_ckv':
            P.copy('dve', ckv[:, r0 // 128, g0:g0 + gs], ps[0:fs, 0:gs])
        elif name == 'mla_kpe':
            if r0 == 0:
                P.copy('dve', kpeA[:, g0:g0 + gs], ps[0:32, 0:gs])
            else:
                rope(kpeA[:, g0:g0 + gs], ps[0:32, 0:gs], g0, gs, 1.0,
                     o_kpe[0:32, g0:g0 + gs], o_kpe[32:64, g0:g0 + gs])
    linear(L, hT, 128, 16, TG, w_in, chunks, evac_in)

    qn = P.sb([128, 4, T], BF16, name="qn")
    cn = P.sb([128, 4, T], BF16, name="cn")
    for (srcT, dstT, gi_) in ((qa, qn, 0), (ckv, cn, 1)):
        for gi, (g0, gs) in enumerate(TG):
            for kc in range(4):
                sq = sqb[kc % 2]
                P.act(sq[:, 0:gs], srcT[:, kc, g0:g0 + gs], AF.Square)
                P.mm(stat_ps[gi][:, 0:gs], ones[:, :], sq[:, 0:gs], start=(kc == 0), stop=(kc == 3))
            P.act(rstd[:, g0:g0 + gs], stat_ps[gi][:, 0:gs], AF.Sqrt, bias=et[:, 0:1], scale=1.0 / 512)
            P.recip(rstd[:, g0:g0 + gs], rstd[:, g0:g0 + gs])
        for kc in range(4):
            P.stt('dve', dstT[:, kc, :], srcT[:, kc, :], mgs[:, gi_, kc:kc + 1], rstd[:, :], ALU.mult, ALU.mult)

    qchunks = []
    for h in range(8):
        qchunks.append((h * 192, 128, ('n', h)))
        qchunks.append((h * 192 + 128, 32, ('a', h)))
        qchunks.append((h * 192 + 160, 32, ('b', h)))
    qA = [P.sb([32, 512], F32, name=f"qA{i}") for i in range(2)]

    def evac_q(ps, f0, fs, tag, g0, gs):
        kind, h = tag
        if kind == 'n':
            s = s16()
            P.act(s[0:128, 0:gs], ps[0:128, 0:gs], AF.Copy, scale=MLA_SCALE)
            P.dma_out('sp', o_mq[h, 0:128, g0:g0 + gs], s[0:128, 0:gs])
        elif kind == 'a':
            P.copy('dve', qA[g0 // 512][:, 0:gs], ps[0:32, 0:gs])
        else:
            rope(qA[g0 // 512][:, 0:gs], ps[0:32, 0:gs], g0, gs, MLA_SCALE,
                 o_mq[h, 128:160, g0:g0 + gs], o_mq[h, 160:192, g0:g0 + gs])
    linear(L, qn, 128, 4, TG, w_qb, qchunks, evac_q)

    kvchunks = []
    for h in range(8):
        kvchunks.append((h * 256, 128, ('k', h)))
        kvchunks.append((h * 256 + 128, 128, ('v', h)))

    def evac_kv(ps, f0, fs, tag, g0, gs):
        kind, h = tag
        s = s16()
        P.copy('dve', s[0:128, 0:gs], ps[0:128, 0:gs])
        dst = o_mk if kind == 'k' else o_mv
        P.dma_out('sp', dst[h, :, g0:g0 + gs], s[0:128, 0:gs])
    linear(L, cn, 128, 4, TG, w_kvb, kvchunks, evac_kv)
    P.emit()
    return nc


def rope_tables():
    inv = (10000.0 ** (-np.arange(0, 64, 2, dtype=np.float32) / np.float32(64))).astype(np.float32)
    ang = np.arange(S, dtype=np.float32)[:, None] * inv[None]
    return np.cos(ang).astype(np.float32), np.sin(ang).astype(np.float32)


def pk(v):
    return np.ascontiguousarray(v.reshape(-1, 128).T)


def run_l1(inp, mod):
    nc = build_l1()
    shift, scale = mod[0, 0:D], mod[0, D:2 * D]
    vecs = np.ascontiguousarray(np.stack([pk(inp['norm_g'][0]), pk(scale), pk(shift)], axis=1))
    mg = np.ascontiguousarray(np.stack([pk(inp['mla_qa_g'][0]), pk(inp['mla_kva_g'][0])], axis=1))
    cos, sin = rope_tables()
    xTfull = np.ascontiguousarray(inp['x'][0].T)
    maps = []
    for i in range(NCORES):
        sl = slice(i * TPC, (i + 1) * TPC)
        maps.append({"xT": np.ascontiguousarray(xTfull[:, sl]), "vecs": vecs,
                     "w_in": inp['a_w_in'][0], "mg": mg,
                     "w_qb": inp['mla_w_qb'][0], "w_kvb": inp['mla_w_kvb'][0],
                     "cs": np.ascontiguousarray(np.stack([cos[sl].T, sin[sl].T], axis=1))})
    res = run(nc, maps)
    out = {}
    for k in ("o_q", "o_kv", "o_g", "o_z", "o_mz", "o_kpe"):
        out[k] = np.concatenate([r[k] for r in res], axis=-1)
    for k in ("o_mq", "o_mk", "o_mv"):
        out[k] = np.concatenate([r[k] for r in res], axis=-1)
    return out


NEGM = -30000.0


def blk_of(i, j):
    return 8 * j + (i if j % 2 == 0 else 7 - i)


def split3(x):
    x = np.asarray(x, dtype=np.float64)
    x1 = x.astype(NPBF).astype(np.float64)
    x2 = (x - x1).astype(NPBF).astype(np.float64)
    x3 = (x - x1 - x2).astype(NPBF).astype(np.float64)
    return x1, x2, x3


def alibi_q_rows(tok):
    slopes = (2.0 ** (-8.0 * np.arange(1, 17, dtype=np.float32) / np.float32(16))).astype(np.float32)
    out = np.zeros((16, 10, len(tok)), np.float64)
    for h in range(16):
        s1, s2, s3 = split3(slopes[h])
        t1, t2, t3 = split3(np.float64(slopes[h]) * tok.astype(np.float64))
        out[h, 0] = s1; out[h, 1] = s2; out[h, 2] = s3
        out[h, 3] = s1; out[h, 4] = s2; out[h, 5] = s3
        out[h, 6] = -t1; out[h, 7] = -t2; out[h, 8] = -t3
        out[h, 9] = 1.0
    return out.astype(NPBF)


def alibi_k_rows(pos, valid=None):
    pos = np.asarray(pos, dtype=np.int64)
    out = np.zeros((10, len(pos)), np.float64)
    p = np.maximum(pos, 0)
    hi = (p // 128) * 128
    lo = p % 128
    out[0:3] = hi
    out[3:6] = lo
    out[6:9] = 1.0
    if valid is not None:
        out[9] = np.where(valid, 0.0, NEGM)
    return out.astype(NPBF)


def build_l2():
    nc, P = new_prog()
    NQ = 1024
    d_q = P.dram("qaug", [74, 16 * NQ], BF16)
    d_cmpin = P.dram("cmpin", [2, 2, 64, S], BF16)
    d_kcrows = P.dram("kcrows", [10, 512], BF16)
    d_w1 = P.dram("w1", [2, 64, 32 * 64], F32)
    d_w2 = P.dram("w2", [2, 64, 64], F32)
    d_peT = P.dram("peT", [2, 64, 32], F32)
    d_b1 = P.dram("b1", [64, 2], F32)
    d_vcconst = P.dram("vcconst", [128, 4 * 129], BF16)
    d_cmpmask = P.dram("cmpmask", [128, 4 * NQ], BF16)
    d_tailmask = P.dram("tailmask", [128, 64 * 128], BF16)
    d_winmask = P.dram("winmask", [128, 2 * 512], BF16)
    d_expand = P.dram("expand", [128, S], BF16)
    d_force = P.dram("force", [128, 8 * 128], F32)
    d_ks = P.dram("ks", [2, 74, S], BF16)
    d_vs = P.dram("vs", [2, 128, 64 * 65], BF16)
    d_kw = P.dram("kw", [2, 8, 74, 640], BF16)
    d_vw = P.dram("vw", [2, 8, 128, 5 * 65], BF16)
    d_gates = P.dram("gates", [128, 8 * 48], F32)
    d_sz = P.dram("sz", [8, 128, 1024], F32)
    d_mzz = P.dram("mzz", [8, 128, 1024], F32)
    d_mqn = P.dram("mqn", [128, 8 * NQ], BF16)
    d_mqp = P.dram("mqp", [64, 8 * NQ], BF16)
    d_mkn = P.dram("mkn", [8, 128, S], BF16)
    d_mkp = P.dram("mkp", [64, S], BF16)
    d_mv = P.dram("mv", [8, 128, 64 * 129], BF16)
    d_idb = P.dram("idb", [128, 128], BF16)
    d_idf = P.dram("idf", [128, 128], F32)
    o_yz = P.dram("o_yz", [8, 128, 2048], BF16, "ExternalOutput")

    Qb = P.sb([128, 16, NQ], BF16, name="Qb")
    obuf = P.sb([128, 8, 1024], F32, name="obuf")
    imp = [P.sb([128, 8, 128], F32, name=f"imp{g}") for g in range(2)]
    negmt = [P.sb([128, NQ], BF16, name=f"negmt{g}") for g in range(2)]
    expand = P.sb([128, S], BF16, name="expand")
    tailm = P.sb([128, 64, 128], BF16, name="tailm")
    cmpm = P.sb([128, 4, NQ], BF16, name="cmpm")
    winm = P.sb([128, 2, 512], BF16, name="winm")
    force = P.sb([128, 8, 128], F32, name="force")
    E = [P.sb([128, NQ], BF16, name=f"E{i}") for i in range(4)]
    bufK = P.sb([128, S], BF16, name="bufK")
    bufV = P.sb([128, 64 * 129], BF16, name="bufV")
    kpe = expand
    gat = P.sb([128, 8, 48], F32, name="gat")
    idb = P.sb([128, 128], BF16, name="idb")
    idf = P.sb([128, 128], F32, name="idf")
    zst = [P.sb([128, 1024], F32, name=f"zst{i}") for i in range(2)]
    yst = [P.sb([128, 1024], BF16, name=f"yst{i}") for i in range(2)]
    kcmp = [P.sb([74, 512], BF16, name=f"kcmp{g}") for g in range(2)]
    vcmp = [P.sb([128, 4, 193], BF16, name=f"vcmp{g}") for g in range(2)]
    w1s = obuf.re("p j f -> p (j f)")
    w1b = P.sb([64, 32, 64], BF16, name="w1b")
    w2s = P.sb([64, 64], F32, name="w2s")
    w2b = P.sb([64, 64], BF16, name="w2b")
    peTs = P.sb([64, 32], F32, name="peTs")
    peTb = P.sb([64, 32], BF16, name="peTb")
    b1s = P.sb([64, 2], F32, name="b1s")
    cst = P.sb([64, 1], F32, name="cst")
    hid = P.sb([64, 512], BF16, name="hid")
    sm = [P.sb([128, 8], F32, name=f"sm{i}") for i in range(8)]
    wk = [P.sb([128, 128], F32, name=f"wk{i}") for i in range(3)]
    Sps = [P.ps([128, NQ], F32, name=f"S{i}") for i in range(2)]
    Aps = [P.ps([128, 512], F32, name=f"A{i}") for i in range(4)]

    P.dma('sp', Qb[0:74, :, :], d_q.re("r (h q) -> r h q", h=16)[:, :, :])
    P.dma('sp', idb[:, :], d_idb[:, :])
    P.dma('sp', idf[:, :], d_idf[:, :])
    P.dma('sp', cmpm[:, :, :], d_cmpmask.re("p (c q) -> p c q", c=4)[:, :, :])
    P.dma('sp', gat[:, :, :], d_gates.re("p (j c) -> p j c", j=8)[:, :, :])
    P.dma('sp', force[:, :, :], d_force.re("p (j c) -> p j c", j=8)[:, :, :])
    P.dma('sp', tailm[:, :, :], d_tailmask.re("p (c q) -> p c q", c=64)[:, :, :])
    P.dma('sp', winm[:, :, :], d_winmask.re("p (c q) -> p c q", c=2)[:, :, :])
    P.dma('sp', expand[:, :], d_expand[:, :])
    P.dma('sp', b1s[:, :], d_b1[:, :])
    smi = [0]

    def small():
        smi[0] += 1
        return sm[smi[0] % 8]
    Ei = [0]

    def nextE():
        Ei[0] += 1
        return E[Ei[0] % 4]
    Si = [0]

    def nextS():
        Si[0] += 1
        return Sps[Si[0] % 2]
    Ai = [0]

    def nextA():
        Ai[0] += 1
        return Aps[Ai[0] % 4]

    kcv = bufK.re("p (n r) -> p n r", r=16)
    for g in range(2):
        P.dma('sp', kcmp[g][64:74, :], d_kcrows[:, :])
        P.dma('sp', vcmp[g][:, :, 64:193], d_vcconst.re("p (c f) -> p c f", c=4)[:, :, :])
        P.memset('pool', vcmp[g][:, :, 0:64], 0.0)
        P.memset('pool', kcmp[g][0:64, :], 0.0)
        for kv in range(2):
            P.dma('sp', bufK[0:64, :], d_cmpin[g, kv, :, :])
            P.dma('sp', w1s[0:64, 0:2048], d_w1[kv, :, :])
            P.dma('sp', w2s[:, :], d_w2[kv, :, :])
            P.dma('sp', peTs[:, :], d_peT[kv, :, :])
            P.copy('pool', w1b[:, :, :], obuf[0:64, 0:2, :].tt.re("p j (a e) -> p (j a) e", e=64)[0:64, 0:32, :])
            P.copy('pool', w2b[:, :], w2s[:, :])
            P.copy('pool', peTb[:, :], peTs[:, :])
            a1 = nextA()
            for l in range(32):
                P.mm(a1[0:64, 0:1], w1b[:, l, :], peTb[:, l:l + 1], start=(l == 0), stop=(l == 31))
            P.tt('dve', cst[:, :], a1[0:64, 0:1], b1s[:, kv:kv + 1], ALU.add)
            a2 = nextA()
            for l in range(32):
                rhs = kcv[0:64, 0:511, l] if l < 16 else kcv[0:64, 1:512, l - 16]
                P.mm(a2[0:64, 0:511], w1b[:, l, :], rhs, start=(l == 0), stop=(l == 31))
            P.memset('pool', hid[:, :], 0.0)
            P.act(hid[:, 0:511], a2[0:64, 0:511], AF.Silu, bias=cst[:, 0:1])
            if kv == 0:
                a3 = nextA()
                P.mm(a3[0:64, 0:511], w2b[:, :], hid[:, 0:511])
                P.copy('dve', kcmp[g][0:64, 0:511], a3[0:64, 0:511])
            else:
                for c4 in range(4):
                    a3 = nextA()
                    P.mm(a3[:, 0:64], hid[:, c4 * 128:(c4 + 1) * 128], w2b[:, :])
                    P.copy('dve', vcmp[g][:, c4, 0:64], a3[:, 0:64])

    def finish(acc_v, zcol, gate_v, dst, first):
        s_ = small()
        P.ts('dve', s_[:, 0:1], zcol, 1e-30, None, ALU.max)
        P.recip(s_[:, 1:2], s_[:, 0:1])
        if gate_v is not None:
            P.tt('dve', s_[:, 2:3], s_[:, 1:2], gate_v, ALU.mult)
            sc_ = s_[:, 2:3]
        else:
            sc_ = s_[:, 1:2]
        if first:
            P.ts('dve', dst, acc_v, sc_, None, ALU.mult)
        else:
            P.stt('dve', dst, acc_v, sc_, dst, ALU.mult, ALU.add)
        return s_

    for h in range(16):
        g = h // 8
        Es = []
        for c4 in range(4):
            sp_ = nextS()
            for hf in range(2):
                lo, hi = hf * 512, (hf + 1) * 512
                P.mm(sp_[:, lo:hi], kcmp[g][0:74, c4 * 128:(c4 + 1) * 128], Qb[0:74, h, lo:hi], start=True, stop=False)
                P.mm(sp_[:, lo:hi], idb[:, :], cmpm[:, c4, lo:hi], start=False, stop=True)
            e_ = nextE()
            P.act(e_[:, :], sp_[:, :], AF.Exp)
            Es.append(e_)
        for j in range(8):
            acc = nextA()
            for c4 in range(4):
                P.mm(acc[:, 0:193], Es[c4][:, j * 128:(j + 1) * 128], vcmp[g][:, c4, :], start=(c4 == 0), stop=(c4 == 3))
            s_ = finish(acc[:, 0:64], acc[:, 64:65], gat[:, j, h * 3:h * 3 + 1], obuf[:, j, h * 64:(h + 1) * 64], True)
            if h % 8 == 0:
                P.ts('dve', imp[g][:, j, :], acc[:, 65:193], s_[:, 1:2], None, ALU.mult)
            else:
                P.stt('dve', imp[g][:, j, :], acc[:, 65:193], s_[:, 1:2], imp[g][:, j, :], ALU.mult, ALU.add)
        if h % 8 == 7:
            for j in range(8):
                w0, w1_, w2_ = wk
                s_ = small()
                P.tt('dve', w0[:, :], imp[g][:, j, :], force[:, j, :], ALU.add)
                P.op('dve', lambda e, m0=s_[:, 0:8].ap, i0=w0[:, :].ap: e.max(out=m0, in_=i0), [w0], [s_])
                P.op('dve', lambda e, o=w1_[:, :].ap, m0=s_[:, 0:8].ap, i0=w0[:, :].ap:
                     e.match_replace(out=o, in_to_replace=m0, in_values=i0, imm_value=-1e30), [w0, s_], [w1_])
                s2 = small()
                P.op('dve', lambda e, m0=s2[:, 0:8].ap, i0=w1_[:, :].ap: e.max(out=m0, in_=i0), [w1_], [s2])
                s3 = small()
                P.op('dve', lambda e, o=s3[:, 0:1].ap, i=s2[:, 0:8].ap:
                     e.tensor_reduce(out=o, in_=i, axis=AX.X, op=ALU.min), [s2], [s3])
                P.ts('dve', w2_[:, :], w0[:, :], s3[:, 0:1], None, ALU.is_ge)
                P.ts('dve', w2_[:, :], w2_[:, :], -1.0, -NEGM, ALU.add, ALU.mult)
                a_ = nextA()
                P.transpose(a_[:, 0:128], w2_[:, :], idf[:, :])
                P.copy('dve', negmt[g][:, j * 128:(j + 1) * 128], a_[:, 0:128])

    accpair = [(Aps[0], Aps[1]), (Aps[2], Aps[3])]
    for g in range(2):
        P.dma('sp', bufK[0:74, :], d_ks[g, :, :])
        P.dma('sp', bufV[:, 0:64 * 65], d_vs[g, :, :])
        vsv = bufV.re("p (c f) -> p c f", f=129)
        for hh in range(8):
            h = g * 8 + hh
            accA, accB = accpair[h % 2]
            P.memset('dve', accA[:, :], 0.0)
            P.memset('dve', accB[:, :], 0.0)

            def accv(j, w0_, w):
                t_ = accA if j < 4 else accB
                o_ = (j % 4) * 128
                return t_[:, o_ + w0_:o_ + w0_ + w]
            for c in range(64):
                j0 = c // 8
                sp_ = nextS()
                for hf in range(2):
                    lo, hi = max(j0 * 128, hf * 512), (hf + 1) * 512
                    if lo >= hi:
                        continue
                    tail_here = (j0 * 128 >= hf * 512) and (j0 * 128 < hi)
                    P.mm(sp_[:, lo:hi], bufK[0:74, c * 128:(c + 1) * 128], Qb[0:74, h, lo:hi], start=True, stop=False)
                    P.mm(sp_[:, lo:hi], expand[:, c * 128:(c + 1) * 128], negmt[g][:, lo:hi], start=False,
                         stop=(not tail_here))
                    if tail_here:
                        P.mm(sp_[:, j0 * 128:(j0 + 1) * 128], idb[:, :], tailm[:, c, :], start=False, stop=True)
                e_ = nextE()
                P.act(e_[:, j0 * 128:NQ], sp_[:, j0 * 128:NQ], AF.Exp)
                for j in range(j0, 8):
                    P.mm(accv(j, 0, 65), e_[:, j * 128:(j + 1) * 128], bufV[:, c * 65:(c + 1) * 65],
                         start=False, stop=False, skip=True)
            for j in range(8):
                finish(accv(j, 0, 64), accv(j, 64, 1),
                       gat[:, j, h * 3 + 1:h * 3 + 2], obuf[:, j, h * 64:(h + 1) * 64], False)

    for g in range(2):
        for j in range(8):
            P.dma('sp', bufK[0:74, 0:640], d_kw[g, j, :, :])
            P.dma('sp', bufV[:, 0:5 * 65], d_vw[g, j, :, :])
            accA, accB = accpair[(g * 8 + j) % 2]
            P.memset('dve', accA[:, :], 0.0)
            P.memset('dve', accB[:, :], 0.0)

            def accw(hh, w0_, w):
                t_ = accA if hh < 4 else accB
                o_ = (hh % 4) * 128
                return t_[:, o_ + w0_:o_ + w0_ + w]
            for wc in range(5):
                sp_ = nextS()
                for hf in range(2):
                    lo, hi = hf * 512, (hf + 1) * 512
                    msk = wc in (0, 4)
                    P.mm(sp_.re("p (h q) -> p h q", q=128)[:, hf * 4:(hf + 1) * 4, :],
                         bufK[0:74, wc * 128:(wc + 1) * 128], Qb[0:74, g * 8 + hf * 4:g * 8 + hf * 4 + 4, j * 128:(j + 1) * 128],
                         start=True, stop=(not msk))
                    if msk:
                        P.mm(sp_[:, lo:hi], idb[:, :], winm[:, 0 if wc == 0 else 1, :], start=False, stop=True)
                e_ = nextE()
                P.act(e_[:, :], sp_[:, :], AF.Exp)
                for hh in range(8):
                    P.mm(accw(hh, 0, 65), e_[:, hh * 128:(hh + 1) * 128], bufV[:, wc * 65:(wc + 1) * 65],
                         start=False, stop=False, skip=True)
            for hh in range(8):
                h = g * 8 + hh
                finish(accw(hh, 0, 64), accw(hh, 64, 1), gat[:, j, h * 3 + 2:h * 3 + 3],
                       obuf[:, j, h * 64:(h + 1) * 64], False)

    for j in range(8):
        z_ = zst[j % 2]
        y_ = yst[j % 2]
        P.dma('sp', z_[:, :], d_sz[j, :, :])
        P.tt('pool', y_[:, :], obuf[:, j, :], z_[:, :], ALU.mult)
        P.dma_out('sp', o_yz[j, :, 0:1024], y_[:, :])

    P.dma('sp', Qb[:, 0:8, :], d_mqn.re("p (h q) -> p h q", h=8)[:, :, :])
    P.dma('sp', Qb[0:64, 8:16, :], d_mqp.re("p (h q) -> p h q", h=8)[:, :, :])
    P.dma('sp', kpe[0:64, :], d_mkp[:, :])
    for h in range(8):
        P.dma('sp', bufK[:, :], d_mkn[h, :, :])
        P.dma('sp', bufV[:, :], d_mv[h, :, :])
        for a_ in Aps[0:3]:
            P.memset('dve', a_[:, :], 0.0)

        def accm(j, w0_, w):
            t_ = Aps[j // 3]
            o_ = (j % 3) * 160
            return t_[:, o_ + w0_:o_ + w0_ + w]
        for c in range(64):
            j0 = c // 8
            sp_ = nextS()
            for hf in range(2):
                lo, hi = max(j0 * 128, hf * 512), (hf + 1) * 512
                if lo >= hi:
                    continue
                tail_here = (j0 * 128 >= hf * 512) and (j0 * 128 < hi)
                P.mm(sp_[:, lo:hi], bufK[:, c * 128:(c + 1) * 128], Qb[:, h, lo:hi], start=True, stop=False)
                P.mm(sp_[:, lo:hi], kpe[0:64, c * 128:(c + 1) * 128], Qb[0:64, 8 + h, lo:hi], start=False,
                     stop=(not tail_here))
                if tail_here:
                    P.mm(sp_[:, j0 * 128:(j0 + 1) * 128], idb[:, :], tailm[:, c, :], start=False, stop=True)
            e_ = nextE()
            P.act(e_[:, j0 * 128:NQ], sp_[:, j0 * 128:NQ], AF.Exp)
            for j in range(j0, 8):
                P.mm(accm(j, 0, 129), e_[:, j * 128:(j + 1) * 128], bufV[:, c * 129:(c + 1) * 129],
                     start=False, stop=False, skip=True)
        for j in range(8):
            finish(accm(j, 0, 128), accm(j, 128, 1), None, obuf[:, j, h * 128:(h + 1) * 128], True)
    for j in range(8):
        z_ = zst[j % 2]
        y_ = yst[j % 2]
        P.dma('sp', z_[:, :], d_mzz[j, :, :])
        P.tt('pool', y_[:, :], obuf[:, j, :], z_[:, :], ALU.mult)
        P.dma_out('sp', o_yz[j, :, 1024:2048], y_[:, :])
    P.emit()
    return nc


def run_l2(inp, l1):
    nc = build_l2()
    o_q, o_kv = l1['o_q'], l1['o_kv']
    bf = lambda a: np.ascontiguousarray(np.asarray(a).astype(NPBF))
    n_cmp = 511
    cmp_end = np.arange(512) * 16 + 31
    kcrows = alibi_k_rows(cmp_end)
    vcconst = np.zeros((128, 4, 129), np.float32)
    vcconst[:, :, 0] = 1.0
    for n in range(n_cmp):
        for jb in range(128):
            if 4 * jb - 1 <= n <= 4 * jb + 3:
                vcconst[n % 128, n // 128, 1 + jb] = 1.0
    vcconst = bf(vcconst.reshape(128, 4 * 129))
    expand = np.zeros((128, S), np.float32)
    expand[np.arange(S) // 64, np.arange(S)] = 1.0
    expand = bf(expand)
    kl = np.arange(128)[:, None]
    tl = np.arange(128)[None, :]
    wlo = np.where(kl > tl, 0.0, NEGM).astype(np.float32)
    whi = np.where(kl <= tl, 0.0, NEGM).astype(np.float32)
    winmask = bf(np.stack([np.tile(wlo, (1, 4)), np.tile(whi, (1, 4))], axis=1).reshape(128, 2 * 512))
    idb = bf(np.eye(128, dtype=np.float32))
    idf = np.eye(128, dtype=np.float32)
    w1 = np.ascontiguousarray(np.stack([inp['nsa_w1_k'][0], inp['nsa_w1_v'][0]]).transpose(0, 2, 1, 3).reshape(2, 64, 32 * 64))
    w2 = np.ascontiguousarray(np.stack([inp['nsa_w2_k'][0], inp['nsa_w2_v'][0]]))
    peT = np.ascontiguousarray(np.stack([inp['nsa_pe_k'][0].T, inp['nsa_pe_v'][0].T]))
    b1 = np.ascontiguousarray(np.stack([inp['nsa_b1_k'][0], inp['nsa_b1_v'][0]], axis=1))
    kvr = o_kv.reshape(6, 2, 64, S)
    cmpin = np.ascontiguousarray(np.stack([np.stack([kvr[0, g], kvr[1, g]]) for g in range(2)]))
    krows_all = alibi_k_rows(np.arange(S))
    ks = np.ascontiguousarray(np.stack([np.concatenate([kvr[2, g], krows_all], axis=0) for g in range(2)]))
    ones_col = np.ones((S, 1), NPBF)

    def tokmajor(vT, width):
        a = np.concatenate([vT.T, ones_col], axis=1)
        return np.ascontiguousarray(a.reshape(64, 128, width).transpose(1, 0, 2).reshape(128, 64 * width))
    vs = np.stack([tokmajor(kvr[3, g], 65) for g in range(2)])
    mkn = np.ascontiguousarray(l1['o_mk'])
    mkp = np.ascontiguousarray(l1['o_kpe'])
    mv = np.stack([tokmajor(l1['o_mv'][h], 129) for h in range(8)])
    gT, zT, mzT = l1['o_g'], l1['o_z'], l1['o_mz']
    maps = []
    toks = []
    for i in range(NCORES):
        blks = [blk_of(i, j) for j in range(8)]
        tok = np.concatenate([np.arange(b * 128, (b + 1) * 128) for b in blks])
        toks.append(tok)
        qrows = alibi_q_rows(tok)
        qa = np.concatenate([o_q[:, tok].reshape(16, 64, 1024), qrows], axis=1)
        qaug = np.ascontiguousarray(qa.transpose(1, 0, 2).reshape(74, 16 * 1024))
        cm = np.where(cmp_end[:, None] <= tok[None, :], 0.0, NEGM).astype(np.float32)
        cm[511, :] = NEGM
        cmpmask = bf(cm.reshape(4, 128, 1024).transpose(1, 0, 2).reshape(128, 4 * 1024))
        tm = np.zeros((128, 64, 128), np.float32)
        for c in range(64):
            j = c // 8
            kpos = c * 128 + np.arange(128)
            tpos = blks[j] * 128 + np.arange(128)
            tm[:, c, :] = np.where(kpos[:, None] <= tpos[None, :], 0.0, NEGM)
        tailmask = bf(tm.reshape(128, 64 * 128))
        fo = np.zeros((128, 8, 128), np.float32)
        for j in range(8):
            tpos = blks[j] * 128 + np.arange(128)
            cur = tpos // 64
            jb = np.arange(128)[None, :]
            f = np.zeros((128, 128), np.float32)
            f[jb == (cur[:, None] - 1)] = 3e9
            f[jb == cur[:, None]] = 2e9
            f[np.broadcast_to(jb == 0, (128, 128))] = 1e9
            fo[:, j, :] = f
        kw = np.zeros((2, 8, 74, 640), NPBF)
        vw = np.zeros((2, 8, 128, 5, 65), NPBF)
        for j in range(8):
            pos = (blks[j] - 4) * 128 + np.arange(640)
            valid = pos >= 0
            pc = np.maximum(pos, 0)
            rows = alibi_k_rows(pos, valid)
            for g in range(2):
                kk = kvr[4, g][:, pc].copy()
                kk[:, ~valid] = 0
                kw[g, j] = np.concatenate([kk, rows], axis=0)
                vv = kvr[5, g][:, pc].T.copy()
                vv[~valid] = 0
                vv = np.concatenate([vv, np.ones((640, 1), NPBF)], axis=1)
                vw[g, j] = vv.reshape(5, 128, 65).transpose(1, 0, 2)
        gates = np.ascontiguousarray(gT[:, tok].T.reshape(8, 128, 48).transpose(1, 0, 2).reshape(128, 8 * 48))
        sz = np.ascontiguousarray(zT[:, tok].T.reshape(8, 128, 1024))
        mzz = np.ascontiguousarray(mzT[:, tok].T.reshape(8, 128, 1024))
        mq = l1['o_mq'][:, :, tok]
        mqn = np.ascontiguousarray(mq[:, 0:128].transpose(1, 0, 2).reshape(128, 8 * 1024))
        mqp = np.ascontiguousarray(mq[:, 128:192].transpose(1, 0, 2).reshape(64, 8 * 1024))
        maps.append(dict(qaug=qaug, cmpin=cmpin, kcrows=kcrows, w1=w1, w2=w2, peT=peT, b1=b1, vcconst=vcconst,
                         cmpmask=cmpmask, tailmask=tailmask, winmask=winmask, expand=expand,
                         force=np.ascontiguousarray(fo.reshape(128, 8 * 128)), ks=ks, vs=vs,
                         kw=kw, vw=np.ascontiguousarray(vw.reshape(2, 8, 128, 5 * 65)), gates=gates, sz=sz, mzz=mzz,
                         mqn=mqn, mqp=mqp, mkn=mkn, mkp=mkp, mv=mv, idb=idb, idf=idf))
    res = run(nc, maps)
    yz = np.zeros((S, 2048), NPBF)
    for i in range(NCORES):
        yz[toks[i]] = res[i]["o_yz"].reshape(1024, 2048)
    return yz


def build_l3(stop=99, part='a'):
    nc, P = new_prog()
    T = TPC
    T1 = T + 1
    TGX = [(0, 1), (1, 512), (513, 512)]
    TG = [(0, 512), (512, 512)]
    A_ = "ExternalInput" if part == 'a' else "Internal"
    B_ = "ExternalInput" if part == 'b' else "Internal"
    AO = "ExternalOutput" if part == 'a' else "Internal"
    BO = "ExternalOutput" if part == 'b' else "Internal"
    yzT = P.dram("yzT", [D, T1], BF16, A_)
    xT = P.dram("xT", [D, T1], F32, A_)
    w_out = P.dram("w_out", [D, D], F32, A_)
    h_io = P.dram("h_io", [D, T1], F32, "ExternalOutput" if part == 'a' else "ExternalInput")
    vecs = P.dram("vecs", [128, 10, 16], F32)
    mu_in = P.dram("mu", [128, 6, 16], F32)
    pm_in = P.dram("pm", [128, 1], F32)
    bones_in = P.dram("bones", [128, 128], F32)
    wr = P.dram("w_r", [D, D], F32, B_)
    wk_ = P.dram("w_k", [D, D], F32, A_)
    wv = P.dram("w_v", [D, D], F32, B_)
    wz = P.dram("w_z", [D, D], F32, B_)
    w1 = P.dram("w1", [D, 96], F32, A_)
    w2 = P.dram("w2", [96, D], F32, A_)
    a1 = P.dram("a1", [D, 96], F32, A_)
    a2 = P.dram("a2", [96, D], F32, A_)
    o_x1 = P.dram("o_x1", [D, T], F32, AO)
    o_r = P.dram("o_r", [D, T], BF16, BO)
    o_v = P.dram("o_v", [D, T], BF16, BO)
    o_kap = P.dram("o_kap", [D, T], BF16, AO)
    o_b = P.dram("o_b", [D, T], BF16, AO)
    o_km = P.dram("o_km", [D, T], BF16, AO)
    o_lw = P.dram("o_lw", [D, T], F32, AO)
    o_bonus = P.dram("o_bonus", [D, T], F32, BO)
    o_sz = P.dram("o_sz", [D, T], F32, BO)
    s_a = P.dram("s_a", [D, T], F32, "Internal")
    s_km = P.dram("s_km", [D, T], F32, "ExternalOutput" if part == 'a' else "ExternalInput")
    s_rk = P.dram("s_rk", [D, T], F32, "Internal")

    L = LinCtx(P)
    ones = P.sb([128, 128], F32, name="ones")
    P.memset('pool', ones[:, :], 1.0)
    bones = P.sb([128, 128], F32, name="bones")
    P.dma('sp', bones[:, :], bones_in[:, :])
    vs = P.sb([128, 10, 16], F32, name="vs")
    P.dma('sp', vs[:, :, :], vecs[:, :, :])
    mus = P.sb([128, 6, 16], F32, name="mus")
    omu = P.sb([128, 6, 16], F32, name="omu")
    P.dma('sp', mus[:, :, :], mu_in[:, :, :])
    P.ts('dve', omu[:, :, :], mus[:, :, :], -1.0, 1.0, ALU.mult, ALU.add)
    pm = P.sb([128, 1], F32, name="pm")
    P.dma('sp', pm[:, :], pm_in[:, :])
    gsc = P.sb([128, 16], F32, name="gsc")
    P.stt('dve', gsc[:, :], vs[:, 2, :], 1.0, vs[:, 1, :], ALU.add, ALU.mult)
    ab = P.sb([128, 16, T1], BF16, name="ab")
    hs = P.sb([128, 16, T1], F32, name="hs")
    xb = [P.sb([128, T1], F32, name=f"xb{i}") for i in range(2)]
    sqb = [P.sb([128, 512], F32, name=f"sq{i}") for i in range(2)]
    stat_ps = [P.ps([128, 512], F32, name=f"sps{i}") for i in range(3)]
    rstd = P.sb([128, T1], F32, name="rstd")
    tmp = P.sb([128, T1], F32, name="tmp")
    st32 = [P.sb([128, 1024], F32, name=f"st32_{i}") for i in range(4)]
    st16 = [P.sb([128, 1024], BF16, name=f"st16_{i}") for i in range(4)]
    ld32 = [P.sb([128, 1024], F32, name=f"ld32_{i}") for i in range(2)]
    cur = {}
    e32 = [P.sb([128, 512], F32, name=f"e32_{i}") for i in range(3)]
    lora = P.sb([96, 1, T], BF16, name="lora")
    cnt = {'a': 0, 'b': 0, 'c': 0, 'd': 0}

    def s32(key, g0):
        if g0 == 0:
            cnt['a'] += 1
            cur[key] = st32[cnt['a'] % 4]
        return cur[key]

    def s16(key, g0):
        if g0 == 0:
            cnt['b'] += 1
            cur[key] = st16[cnt['b'] % 4]
        return cur[key]

    def l32(src, f0, g0):
        if g0 == 0:
            cnt['c'] += 1
            t_ = ld32[cnt['c'] % 2]
            P.dma('act', t_[:, :], src[f0:f0 + 128, :])
            cur[('l', f0)] = t_
        return cur[('l', f0)]

    def t32():
        cnt['d'] += 1
        return e32[cnt['d'] % 3]

    if part == 'b':
        P.mute = True
    P.dma('sp', ab[:, :, :], yzT.re("(kc p) t -> p kc t", p=128)[:, :, :])
    xv = xT.re("(kc p) t -> p kc t", p=128)
    allch = [(f0, 128, None) for f0 in range(0, D, 128)]
    xcur = {}

    def evac_out(ps, f0, fs, tag, g0, gs):
        fc = f0 // 128
        if g0 == 0:
            b = xb[fc % 2]
            P.dma('sp', b[:, :], xv[:, fc, :])
            xcur[fc] = b
        b = xcur[fc]
        P.stt('dve', hs[:, fc, g0:g0 + gs], ps[:, 0:gs], vs[:, 0, fc:fc + 1], b[:, g0:g0 + gs], ALU.mult, ALU.add)
        if g0 == 513:
            P.dma_out('sp', o_x1[f0:f0 + 128, :], hs[:, fc, 1:T1])
    linear(L, ab, 128, 16, TGX, w_out, allch, evac_out)
    if stop == 1:
        P.emit()
        return nc

    et = eps_tile(P, EPS)
    for kc in range(16):
        for gi, (g0, gs) in enumerate(TGX):
            sq = sqb[gi % 2]
            P.act(sq[:, 0:gs], hs[:, kc, g0:g0 + gs], AF.Square)
            P.mm(stat_ps[gi][:, 0:gs], ones[:, :], sq[:, 0:gs], start=(kc == 0), stop=(kc == 15))
    for gi, (g0, gs) in enumerate(TGX):
        P.act(rstd[:, g0:g0 + gs], stat_ps[gi][:, 0:gs], AF.Sqrt, bias=et[:, 0:1], scale=1.0 / D)
        P.recip(rstd[:, g0:g0 + gs], rstd[:, g0:g0 + gs])
    P.ts('dve', rstd[:, 0:1], rstd[:, 0:1], pm[:, 0:1], None, ALU.mult)
    for kc in range(16):
        P.tt('pool', tmp[:, :], hs[:, kc, :], rstd[:, :], ALU.mult)
        P.ts('dve', hs[:, kc, :], tmp[:, :], gsc[:, kc:kc + 1], vs[:, 3, kc:kc + 1], ALU.mult, ALU.add)
    for kc in range(16):
        P.ts('dve', hs[:, kc, 0:1], hs[:, kc, 0:1], pm[:, 0:1], None, ALU.mult)

    def mix(m):
        for kc in range(16):
            P.ts('pool', tmp[:, 0:T], hs[:, kc, 0:T], mus[:, m, kc:kc + 1], None, ALU.mult)
            P.stt('dve', ab[:, kc, 0:T], hs[:, kc, 1:T1], omu[:, m, kc:kc + 1], tmp[:, 0:T], ALU.mult, ALU.add)

    def store(dst, f0, g0, gs, st, scratch=False):
        if g0 == 512:
            if scratch:
                P.dma('sp', dst[f0:f0 + 128, :], st[:, :], owner=st)
            else:
                P.dma_out('sp', dst[f0:f0 + 128, :], st[:, :])

    if stop == 2:
        P.emit()
        return nc
    mix(4)

    def evac_l1(func):
        def f(ps, f0, fs, tag, g0, gs):
            P.act(lora[0:96, 0, g0:g0 + gs], ps[0:96, 0:gs], func)
        return f
    linear(L, ab, 128, 16, TG, a1, [(0, 96, None)], evac_l1(AF.Copy))

    def evac_a(ps, f0, fs, tag, g0, gs):
        fc = f0 // 128
        s = s32('a', g0)
        P.act(s[:, g0:g0 + gs], ps[:, 0:gs], AF.Sigmoid, bias=vs[:, 5, fc:fc + 1])
        store(s_a, f0, g0, gs, s, scratch=True)
    linear(L, lora, 96, 1, TG, a2, allch, evac_a)

    if stop == 3:
        P.emit()
        return nc
    mix(1)
    linear(L, ab, 128, 16, TG, w1, [(0, 96, None)], evac_l1(AF.Tanh))

    def evac_w(ps, f0, fs, tag, g0, gs):
        fc = f0 // 128
        s = s32('w', g0)
        P.act(s[:, g0:g0 + gs], ps[:, 0:gs], AF.Sigmoid, bias=vs[:, 4, fc:fc + 1])
        P.ts('dve', s[:, g0:g0 + gs], s[:, g0:g0 + gs], -float(np.exp(-0.5)), None, ALU.mult)
        store(o_lw, f0, g0, gs, s)
    linear(L, lora, 96, 1, TG, w2, allch, evac_w)

    if stop == 4:
        P.emit()
        return nc
    mix(2)
    nka = P.sb([128, 16], F32, name="nka")
    P.ts('dve', nka[:, :], vs[:, 7, :], -1.0, None, ALU.mult)

    def evac_k(ps, f0, fs, tag, g0, gs):
        fc = f0 // 128
        laf = l32(s_a, f0, g0)
        la = TT_view(laf, laf.h[:, g0:g0 + gs])
        kkr = t32()
        P.ts('dve', kkr[:, 0:gs], ps[:, 0:gs], vs[:, 6, fc:fc + 1], None, ALU.mult)
        sq = t32()
        P.tt('pool', sq[:, 0:gs], kkr[:, 0:gs], kkr[:, 0:gs], ALU.mult)
        bp = stat_ps[(g0 // 512) % 2]
        P.mm(bp[:, 0:gs], bones[:, :], sq[:, 0:gs])
        nr = t32()
        P.act(nr[:, 0:gs], bp[:, 0:gs], AF.Sqrt)
        P.ts('dve', nr[:, 0:gs], nr[:, 0:gs], 1e-12, None, ALU.max)
        P.recip(nr[:, 0:gs], nr[:, 0:gs])
        P.tt('dve', kkr[:, 0:gs], kkr[:, 0:gs], nr[:, 0:gs], ALU.mult)
        s = s16('kap', g0)
        P.copy('pool', s[:, g0:g0 + gs], kkr[:, 0:gs])
        store(o_kap, f0, g0, gs, s)
        s = s16('b', g0)
        P.tt('dve', s[:, g0:g0 + gs], kkr[:, 0:gs], la[:, 0:gs], ALU.mult)
        store(o_b, f0, g0, gs, s)
        P.ts('dve', sq[:, 0:gs], la[:, 0:gs], vs[:, 7, fc:fc + 1], nka[:, fc:fc + 1], ALU.mult, ALU.add)
        km = s32('km', g0)
        P.stt('dve', km[:, g0:g0 + gs], sq[:, 0:gs], 1.0, ps[:, 0:gs], ALU.add, ALU.mult)
        s = s16('km16', g0)
        P.copy('pool', s[:, g0:g0 + gs], km[:, g0:g0 + gs])
        store(s_km, f0, g0, gs, km, scratch=True)
        store(o_km, f0, g0, gs, s)
    linear(L, ab, 128, 16, TG, wk_, allch, evac_k)

    if part == 'a':
        hv_ = h_io.re("(kc p) t -> p kc t", p=128)
        for kc in range(16):
            P.dma_out('sp', hv_[:, kc, :], hs[:, kc, :])
        P.emit()
        return nc
    P.mute = False
    for kc in range(16):
        P.dma('sp', hs[:, kc, :], h_io.re("(kc p) t -> p kc t", p=128)[:, kc, :])
    mix(0)

    def evac_r(ps, f0, fs, tag, g0, gs):
        fc = f0 // 128
        lkf = l32(s_km, f0, g0)
        lk = TT_view(lkf, lkf.h[:, g0:g0 + gs])
        s = s16('r', g0)
        P.copy('dve', s[:, g0:g0 + gs], ps[:, 0:gs])
        store(o_r, f0, g0, gs, s)
        t_ = t32()
        P.stt('dve', t_[:, 0:gs], ps[:, 0:gs], vs[:, 8, fc:fc + 1], lk[:, 0:gs], ALU.mult, ALU.mult)
        bp = stat_ps[(g0 // 512) % 2]
        P.mm(bp[:, 0:gs], bones[:, :], t_[:, 0:gs])
        s = s32('rk', g0)
        P.copy('act', s[:, g0:g0 + gs], bp[:, 0:gs])
        store(s_rk, f0, g0, gs, s, scratch=True)
    linear(L, ab, 128, 16, TG, wr, allch, evac_r)

    if stop == 6:
        P.emit()
        return nc
    mix(3)

    def evac_v(ps, f0, fs, tag, g0, gs):
        lkf = l32(s_rk, f0, g0)
        lk = TT_view(lkf, lkf.h[:, g0:g0 + gs])
        s = s16('v', g0)
        P.copy('dve', s[:, g0:g0 + gs], ps[:, 0:gs])
        store(o_v, f0, g0, gs, s)
        s = s32('bon', g0)
        P.tt('dve', s[:, g0:g0 + gs], ps[:, 0:gs], lk[:, 0:gs], ALU.mult)
        store(o_bonus, f0, g0, gs, s)
    linear(L, ab, 128, 16, TG, wv, allch, evac_v)

    mix(5)

    def evac_z(ps, f0, fs, tag, g0, gs):
        s = s32('z', g0)
        P.act(s[:, g0:g0 + gs], ps[:, 0:gs], AF.Silu)
        store(o_sz, f0, g0, gs, s)
    linear(L, ab, 128, 16, TG, wz, allch, evac_z)
    P.emit()
    return nc


def run_l3(inp, mod, yz, stop=99):
    gate0 = mod[0, 2 * D:3 * D]
    shift1, scale1 = mod[1, 0:D], mod[1, D:2 * D]
    vecs = np.ascontiguousarray(np.stack([pk(gate0), pk(inp['norm_g'][1]), pk(scale1), pk(shift1),
                                          pk(inp['r_w0'][0]), pk(inp['r_a0'][0]), pk(inp['r_k_k'][0]),
                                          pk(inp['r_k_a'][0]), pk(inp['r_r_k'][0]), pk(np.zeros(D, np.float32))], axis=1))
    mu = np.ascontiguousarray(np.stack([pk(inp['r_mu'][0][m]) for m in range(6)], axis=1))
    bones = np.zeros((128, 128), np.float32)
    bones[0:64, 0:64] = 1.0
    bones[64:128, 64:128] = 1.0
    xTf = np.ascontiguousarray(inp['x'][0].T)
    yzTf = np.ascontiguousarray(yz.T)
    xTp = np.concatenate([np.zeros((D, 1), np.float32), xTf], axis=1)
    yzTp = np.concatenate([np.zeros((D, 1), NPBF), yzTf], axis=1)
    nc = build_l3(stop, 'a')
    maps = []
    for i in range(NCORES):
        sl = slice(i * TPC, (i + 1) * TPC + 1)
        maps.append(dict(yzT=np.ascontiguousarray(yzTp[:, sl]), xT=np.ascontiguousarray(xTp[:, sl]),
                         w_out=inp['a_w_out'][0], vecs=vecs, mu=mu,
                         pm=np.full((128, 1), 0.0 if i == 0 else 1.0, np.float32), bones=bones,
                         w_k=inp['r_w_k'][0],
                         w1=inp['r_w1'][0], w2=inp['r_w2'][0], a1=inp['r_a1'][0], a2=inp['r_a2'][0]))
    resa = run(nc, maps)
    out = {}
    for k in ("o_x1", "o_kap", "o_b", "o_km", "o_lw"):
        out[k] = np.concatenate([r[k] for r in resa], axis=1)
    nc = build_l3(stop, 'b')
    maps = []
    for i in range(NCORES):
        maps.append(dict(h_io=resa[i]["h_io"], s_km=resa[i]["s_km"], vecs=vecs, mu=mu,
                         pm=np.full((128, 1), 0.0 if i == 0 else 1.0, np.float32), bones=bones,
                         w_r=inp['r_w_r'][0], w_v=inp['r_w_v'][0], w_z=inp['r_w_z'][0]))
    resb = run(nc, maps)
    for k in ("o_r", "o_v", "o_bonus", "o_sz"):
        out[k] = np.concatenate([r[k] for r in resb], axis=1)
    return out


LNX_EPS = 64e-5
CH = 64
NCH = S // CH


def build_l4():
    nc, P = new_prog()
    SCN = 8
    NSC = NCH // SCN
    d_tok = {k: P.dram("t_" + k, [64, NCH * 256], BF16) for k in ("v", "kap", "b", "km")}
    d_tlw = P.dram("t_lw", [64, NCH * 256], F32)
    d_f = {k: P.dram("f_" + k, [64, 4, S], BF16) for k in ("r", "kap", "b", "km")}
    d_flw = P.dram("f_lw", [64, 4, S], F32)
    d_c = P.dram("consts", [64, 6, 256], F32)
    o_yn = P.dram("o_yn", [64, NCH * 256], F32, "ExternalOutput")

    cs_ = P.sb([64, 6, 256], F32, name="consts")
    P.dma('sp', cs_[:, :, :], d_c[:, :, :])
    tri = cs_[:, 0, 0:64]
    ones64 = cs_[:, 1, 0:64]
    id64 = cs_[:, 2, 0:64]
    I4 = cs_[:, 2, :]
    mUs = cs_[:, 3, :]
    mUi = cs_[:, 4, :]
    mLs = cs_[:, 5, :]
    Tt = {k: P.sb([64, SCN, 256], BF16, name="T_" + k) for k in ("v", "kap", "b", "km")}
    Tlw = P.sb([64, SCN, 256], F32, name="T_lw")
    Ft = {k: P.sb([64, 4, 512], BF16, name="F_" + k) for k in ("r", "kap", "b", "km")}
    Flw = P.sb([64, 4, 512], F32, name="F_lw")
    At = P.sb([64, SCN, 256], F32, name="At")
    Bh = P.sb([64, SCN, 256], F32, name="Bh")
    Kh = P.sb([64, SCN, 256], F32, name="Kh")
    Vt = P.sb([64, SCN, 256], F32, name="Vt")
    Rt = P.sb([64, 4, 512], F32, name="Rt")
    AtT = P.sb([64, 4, 512], F32, name="AtT")
    BtT = P.sb([64, 4, 512], F32, name="BtT")
    KtT = P.sb([64, 4, 512], F32, name="KtT")
    gC = P.sb([64, 4, SCN], F32, name="gC")
    lg = P.sb([64, SCN, 256], F32, name="lg")
    d1 = P.sb([64, SCN, 256], F32, name="d1")
    d2 = P.sb([64, SCN, 256], F32, name="d2")
    lgf = P.sb([64, 4, 512], F32, name="lgf")
    ef = P.sb([64, 4, 512], F32, name="ef")
    ef2 = P.sb([64, 4, 512], F32, name="ef2")
    banks = [P.ps([128, 512], F32, name=f"bk{i}") for i in range(8)]
    bi = [0]

    def nb():
        bi[0] += 1
        return banks[bi[0] % 8]
    tmps = {}

    def tm(name, n=2, shape=(64, 256)):
        if name not in tmps:
            tmps[name] = [[P.sb(list(shape), F32, name=f"{name}{i}") for i in range(n)], 0]
        l = tmps[name]
        l[1] += 1
        return l[0][l[1] % n]
    Hs = [P.sb([64, 256], F32, name=f"H{i}") for i in range(2)]
    P.memset('pool', Hs[0][:, :], 0.0)
    hsl = lambda h: slice(h * 64, (h + 1) * 64)

    def mm4(fn_l, fn_r):
        ps = nb()
        for h in range(4):
            P.mm(ps[0:64, hsl(h)], fn_l(h), fn_r(h))
        return ps

    for sc in range(NSC):
        c0 = sc * SCN
        for k in ("v", "kap", "b", "km"):
            P.dma('sp', Tt[k][:, :, :], d_tok[k].re("p (c f) -> p c f", f=256)[:, c0:c0 + SCN, :])
        for k in ("r", "kap", "b", "km"):
            P.dma('sp', Ft[k][:, :, :], d_f[k][:, :, sc * 512:(sc + 1) * 512])
        P.dma('sp', Tlw[:, :, :], d_tlw.re("p (c f) -> p c f", f=256)[:, c0:c0 + SCN, :])
        P.dma('sp', Flw[:, :, :], d_flw[:, :, sc * 512:(sc + 1) * 512])
        for p_ in range(SCN // 2):
            psL = nb()
            psC = nb()
            for cc in range(2):
                c = 2 * p_ + cc
                P.mm(psL[0:64, cc * 256:(cc + 1) * 256], tri, Tlw[:, c, :])
                P.mm(psC[0:64, cc * 256:(cc + 1) * 256], ones64, Tlw[:, c, :])
            P.copy('act', lg.re("p c f -> p (c f)")[:, p_ * 512:(p_ + 1) * 512], psL[0:64, :])
            P.tt('dve', d2.re("p c f -> p (c f)")[:, p_ * 512:(p_ + 1) * 512], psC[0:64, :],
                 lg.re("p c f -> p (c f)")[:, p_ * 512:(p_ + 1) * 512], ALU.subtract)
        P.tt('pool', d1[:, :, :], lg[:, :, :], Tlw[:, :, :], ALU.subtract)
        P.act(d1[:, :, :], d1[:, :, :], AF.Exp)
        P.act(d2[:, :, :], d2[:, :, :], AF.Exp)
        P.stt('dve', At[:, :, :], Tt["kap"][:, :, :], -1.0, d1[:, :, :], ALU.mult, ALU.mult)
        P.tt('pool', Bh[:, :, :], Tt["b"][:, :, :], d2[:, :, :], ALU.mult)
        P.tt('dve', Kh[:, :, :], Tt["km"][:, :, :], d2[:, :, :], ALU.mult)
        P.copy('pool', Vt[:, :, :], Tt["v"][:, :, :])
        for h in range(4):
            psF = nb()
            for c in range(SCN):
                P.mm(psF[0:64, c * 64:(c + 1) * 64], Tlw[:, c, hsl(h)], tri)
            P.copy('act', lgf[:, h, :], psF[0:64, :])
        P.act(ef[:, :, :], lgf[:, :, :], AF.Exp)
        P.tt('dve', Rt[:, :, :], Ft["r"][:, :, :], ef[:, :, :], ALU.mult)
        P.copy('pool', gC[:, :, :], ef.re("p h (c t) -> p h c t", t=64)[:, :, :, 63])
        P.act(ef2[:, :, :], lgf[:, :, :], AF.Exp, scale=-1.0)
        P.tt('dve', BtT[:, :, :], Ft["b"][:, :, :], ef2[:, :, :], ALU.mult)
        P.tt('pool', KtT[:, :, :], Ft["km"][:, :, :], ef2[:, :, :], ALU.mult)
        P.tt('pool', lgf[:, :, :], lgf[:, :, :], Flw[:, :, :], ALU.subtract)
        P.act(ef2[:, :, :], lgf[:, :, :], AF.Exp)
        P.stt('dve', AtT[:, :, :], Ft["kap"][:, :, :], -1.0, ef2[:, :, :], ALU.mult, ALU.mult)

        def chunk_gen(c):
            cs = slice(c * 64, (c + 1) * 64)
            cg = c0 + c
            ps = mm4(lambda h: BtT[:, h, cs], lambda h: AtT[:, h, cs])
            XT = tm("XT", 4)
            P.tt('dve', XT[:, :], ps[0:64, 0:256], mUs, ALU.mult)
            ps = mm4(lambda h: AtT[:, h, cs], lambda h: BtT[:, h, cs])
            X = tm("X", 4)
            P.tt('dve', X[:, :], ps[0:64, 0:256], mLs, ALU.mult)
            ps = mm4(lambda h: KtT[:, h, cs], lambda h: AtT[:, h, cs])
            AakT = tm("AakT")
            P.tt('dve', AakT[:, :], ps[0:64, 0:256], mUs, ALU.mult)
            ps = mm4(lambda h: BtT[:, h, cs], lambda h: Rt[:, h, cs])
            ArbT = tm("ArbT")
            P.tt('dve', ArbT[:, :], ps[0:64, 0:256], mUi, ALU.mult)
            ps = mm4(lambda h: KtT[:, h, cs], lambda h: Rt[:, h, cs])
            ArkT = tm("ArkT")
            P.tt('dve', ArkT[:, :], ps[0:64, 0:256], mUi, ALU.mult)
            W = tm("W", 4)
            P.tt('pool', W[:, :], XT[:, :], I4, ALU.add)
            yield
            for it in range(5):
                psa = mm4(lambda h: XT[:, hsl(h)], lambda h: X[:, hsl(h)])
                Xn = tm("X", 4)
                P.copy('act', Xn[:, :], psa[0:64, 0:256])
                if it < 4:
                    psb = mm4(lambda h: X[:, hsl(h)], lambda h: XT[:, hsl(h)])
                    XTn = tm("XT", 4)
                    P.copy('act', XTn[:, :], psb[0:64, 0:256])
                psc = mm4(lambda h: Xn[:, hsl(h)], lambda h: W[:, hsl(h)])
                Wn = tm("W", 4)
                P.tt('dve', Wn[:, :], psc[0:64, 0:256], W[:, :], ALU.add)
                X, W = Xn, Wn
                if it < 4:
                    XT = XTn
                yield
            ps = mm4(lambda h: AakT[:, hsl(h)], lambda h: Vt[:, c, hsl(h)])
            X2 = tm("X2")
            P.copy('act', X2[:, :], ps[0:64, 0:256])
            yield
            ps = mm4(lambda h: W[:, hsl(h)], lambda h: X2[:, hsl(h)])
            U2 = tm("U2")
            P.copy('act', U2[:, :], ps[0:64, 0:256])
            ps = mm4(lambda h: W[:, hsl(h)], lambda h: At[:, c, hsl(h)])
            Ap = tm("Ap")
            P.copy('act', Ap[:, :], ps[0:64, 0:256])
            yield
            ps = mm4(lambda h: Ap[:, hsl(h)], lambda h: ArbT[:, hsl(h)])
            RpT = tm("RpT")
            P.tt('dve', RpT.re("p (h t) -> p h t", h=4)[:, :, :], ps.re("p (h t) -> p h t", t=64)[0:64, 0:4, :],
                 Rt[:, :, cs], ALU.add)
            ps = mm4(lambda h: Ap[:, hsl(h)], lambda h: Bh[:, c, hsl(h)])
            PhiT = tm("PhiT")
            for h in range(4):
                P.stt('dve', PhiT[:, hsl(h)], id64, gC[:, h, c:c + 1], ps[0:64, hsl(h)], ALU.mult, ALU.add)
            yield
            Hc, Hn = Hs[cg % 2], Hs[(cg + 1) % 2]
            psY = nb()
            psH = nb()
            for h in range(4):
                P.mm(psY[0:64, hsl(h)], RpT[:, hsl(h)], Hc[:, hsl(h)], start=True, stop=False)
                P.mm(psY[0:64, hsl(h)], ArbT[:, hsl(h)], U2[:, hsl(h)], start=False, stop=False)
                P.mm(psY[0:64, hsl(h)], ArkT[:, hsl(h)], Vt[:, c, hsl(h)], start=False, stop=True)
            for h in range(4):
                P.mm(psH[0:64, hsl(h)], PhiT[:, hsl(h)], Hc[:, hsl(h)], start=True, stop=False)
                P.mm(psH[0:64, hsl(h)], Bh[:, c, hsl(h)], U2[:, hsl(h)], start=False, stop=False)
                P.mm(psH[0:64, hsl(h)], Kh[:, c, hsl(h)], Vt[:, c, hsl(h)], start=False, stop=True)
            P.copy('act', Hn[:, :], psH[0:64, 0:256])
            yield
            ysb = tm("ysb")
            P.copy('act', ysb[:, :], psY[0:64, 0:256])
            st = tm("st", 4, (64, 16))
            ysb3 = ysb.re("p (h v) -> p h v", h=4)
            P.op('dve', lambda e, o=st[:, 0:4].ap, i=ysb3[:, :, :].ap: e.tensor_reduce(out=o, in_=i, axis=AX.X, op=ALU.add),
                 [ysb], [st])
            P.ts('dve', st[:, 4:8], st[:, 0:4], -1.0 / 64, None, ALU.mult)
            cen = tm("cen")
            for h in range(4):
                P.ts('pool' if h % 2 else 'dve', cen[:, hsl(h)], ysb[:, hsl(h)], st[:, 4 + h:5 + h], None, ALU.add)
            sq = tm("sqq")
            P.tt('pool', sq[:, :], cen[:, :], cen[:, :], ALU.mult)
            st2 = tm("st", 4, (64, 16))
            P.op('dve', lambda e, o=st2[:, 0:4].ap, i=sq.re("p (h v) -> p h v", h=4)[:, :, :].ap:
                 e.tensor_reduce(out=o, in_=i, axis=AX.X, op=ALU.add), [sq], [st2])
            P.ts('dve', st2[:, 4:8], st2[:, 0:4], 1.0 / 64, LNX_EPS, ALU.mult, ALU.add)
            P.act(st2[:, 8:12], st2[:, 4:8], AF.Sqrt)
            P.recip(st2[:, 12:16], st2[:, 8:12])
            yo = tm("yo", 4)
            for h in range(4):
                P.ts('pool' if h % 2 else 'dve', yo[:, hsl(h)], cen[:, hsl(h)], st2[:, 12 + h:13 + h], None, ALU.mult)
            P.dma_out('sp', o_yn[:, cg * 256:(cg + 1) * 256], yo[:, :])

        for c2 in range(0, SCN, 2):
            gens = [chunk_gen(c2), chunk_gen(c2 + 1)]
            alive = True
            while alive:
                alive = False
                for g_ in gens:
                    try:
                        next(g_)
                        alive = True
                    except StopIteration:
                        pass
    P.emit()
    return nc


def run_l4(l3):
    nc = build_l4()
    consts = np.zeros((64, 6, 256), np.float32)
    s_ = np.arange(64)[:, None]
    t_ = np.arange(64)[None, :]
    consts[:, 0, 0:64] = (s_ <= t_)
    consts[:, 1, 0:64] = 1.0
    consts[:, 2, :] = np.tile(np.eye(64, dtype=np.float32), (1, 4))
    consts[:, 3, :] = np.tile((t_ > s_).astype(np.float32), (1, 4))
    consts[:, 4, :] = np.tile((t_ >= s_).astype(np.float32), (1, 4))
    consts[:, 5, :] = np.tile((t_ < s_).astype(np.float32), (1, 4))
    maps = []
    for i in range(NCORES):
        chs = slice(i * 256, (i + 1) * 256)
        m = {"consts": consts}

        def tokl(a):
            return np.ascontiguousarray(a[chs, :].T.reshape(NCH, 64, 256).transpose(1, 0, 2).reshape(64, NCH * 256))

        def featl(a):
            return np.ascontiguousarray(a[chs, :].reshape(4, 64, S).transpose(1, 0, 2))
        for k, src in (("v", "o_v"), ("kap", "o_kap"), ("b", "o_b"), ("km", "o_km")):
            m["t_" + k] = tokl(l3[src])
        m["t_lw"] = tokl(l3["o_lw"])
        for k, src in (("r", "o_r"), ("kap", "o_kap"), ("b", "o_b"), ("km", "o_km")):
            m["f_" + k] = featl(l3[src])
        m["f_lw"] = featl(l3["o_lw"])
        maps.append(m)
    res = run(nc, maps)
    yn = np.zeros((S, D), np.float32)
    for i in range(NCORES):
        a = res[i]["o_yn"].reshape(64, NCH, 256).transpose(1, 0, 2).reshape(S, 256)
        yn[:, i * 256:(i + 1) * 256] = a
    return yn


def build_l5():
    nc, P = new_prog()
    T = TPC
    TG = [(0, 512), (512, 512)]
    ynT = P.dram("ynT", [D, T], F32)
    bonus = P.dram("bonus", [D, T], F32)
    szT = P.dram("szT", [D, T], F32)
    x1T = P.dram("x1T", [D, T], F32)
    w_o = P.dram("w_o", [D, D], F32)
    vecs = P.dram("vecs", [128, 4, 16], F32)
    o_out = P.dram("o_out", [D, T], F32, "ExternalOutput")
    L = LinCtx(P)
    ones = P.sb([128, 128], F32, name="ones")
    P.memset('pool', ones[:, :], 1.0)
    vs = P.sb([128, 4, 16], F32, name="vs")
    P.dma('sp', vs[:, :, :], vecs[:, :, :])
    yb = P.sb([128, 16, T], BF16, name="yb")
    xs = P.sb([128, 16, T], F32, name="xs")
    lb = [[P.sb([128, T], F32, name=f"lb{j}_{i}") for i in range(2)] for j in range(3)]
    sqb = [P.sb([128, 512], F32, name=f"sq{i}") for i in range(2)]
    stat_ps = [P.ps([128, 512], F32, name=f"sps{i}") for i in range(2)]
    rstd = P.sb([128, T], F32, name="rstd")
    x1b = [P.sb([128, T], F32, name=f"x1b{i}") for i in range(2)]
    ost = [P.sb([128, T], F32, name=f"ost{i}") for i in range(2)]
    for kc in range(16):
        a, b, c = lb[0][kc % 2], lb[1][kc % 2], lb[2][kc % 2]
        P.dma('sp', a[:, :], ynT.re("(kc p) t -> p kc t", p=128)[:, kc, :])
        P.dma('sp', b[:, :], bonus.re("(kc p) t -> p kc t", p=128)[:, kc, :])
        P.dma('sp', c[:, :], szT.re("(kc p) t -> p kc t", p=128)[:, kc, :])
        P.ts('dve', a[:, :], a[:, :], vs[:, 0, kc:kc + 1], vs[:, 1, kc:kc + 1], ALU.mult, ALU.add)
        P.tt('pool', a[:, :], a[:, :], b[:, :], ALU.add)
        P.tt('dve', yb[:, kc, :], a[:, :], c[:, :], ALU.mult)
    xcur = {}

    def evac(ps, f0, fs, tag, g0, gs):
        fc = f0 // 128
        if g0 == 0:
            b = x1b[fc % 2]
            P.dma('sp', b[:, :], x1T.re("(kc p) t -> p kc t", p=128)[:, fc, :])
            xcur[fc] = b
        b = xcur[fc]
        P.stt('dve', xs[:, fc, g0:g0 + gs], ps[:, 0:gs], vs[:, 2, fc:fc + 1], b[:, g0:g0 + gs], ALU.mult, ALU.add)
    linear(L, yb, 128, 16, TG, w_o, [(f0, 128, None) for f0 in range(0, D, 128)], evac)
    et = eps_tile(P, EPS)
    for kc in range(16):
        for gi, (g0, gs) in enumerate(TG):
            sq = sqb[gi]
            P.act(sq[:, 0:gs], xs[:, kc, g0:g0 + gs], AF.Square)
            P.mm(stat_ps[gi][:, 0:gs], ones[:, :], sq[:, 0:gs], start=(kc == 0), stop=(kc == 15))
    for gi, (g0, gs) in enumerate(TG):
        P.act(rstd[:, g0:g0 + gs], stat_ps[gi][:, 0:gs], AF.Sqrt, bias=et[:, 0:1], scale=1.0 / D)
        P.recip(rstd[:, g0:g0 + gs], rstd[:, g0:g0 + gs])
    for kc in range(16):
        o = ost[kc % 2]
        P.stt('dve', o[:, :], xs[:, kc, :], vs[:, 3, kc:kc + 1], rstd[:, :], ALU.mult, ALU.mult)
        P.dma_out('sp', o_out[kc * 128:(kc + 1) * 128, :], o[:, :])
    P.emit()
    return nc


def run_l5(inp, mod, l3, yn):
    nc = build_l5()
    gate1 = mod[1, 2 * D:3 * D]
    vecs = np.ascontiguousarray(np.stack([pk(inp['r_lnx_g'][0]), pk(inp['r_lnx_b'][0]), pk(gate1),
                                          pk(inp['final_g'])], axis=1))
    ynT = np.ascontiguousarray(yn.T)
    maps = []
    for i in range(NCORES):
        sl = slice(i * TPC, (i + 1) * TPC)
        maps.append(dict(ynT=np.ascontiguousarray(ynT[:, sl]), bonus=np.ascontiguousarray(l3["o_bonus"][:, sl]),
                         szT=np.ascontiguousarray(l3["o_sz"][:, sl]), x1T=np.ascontiguousarray(l3["o_x1"][:, sl]),
                         w_o=inp['r_w_o'][0], vecs=vecs))
    res = run(nc, maps)
    outT = np.concatenate([r["o_out"] for r in res], axis=1)
    return np.ascontiguousarray(outT.T)[None].astype(np.float32)


def kernel(**inputs):
    inp = {k: np.asarray(v) for k, v in inputs.items()}
    mod = run_l0(inp)
    l1 = run_l1(inp, mod)
    yz = run_l2(inp, l1)
    del l1
    l3 = run_l3(inp, mod, yz)
    yn = run_l4(l3)
    return run_l5(inp, mod, l3, yn)
```

```python
from contextlib import ExitStack
import numpy as np
import ml_dtypes
import concourse.bass as bass
import concourse.mybir as mybir
from concourse.bass_utils import run_bass_kernel_spmd

F32 = mybir.dt.float32
BF16 = mybir.dt.bfloat16
AF = mybir.ActivationFunctionType
ALU = mybir.AluOpType
AX = mybir.AxisListType
NPBF = ml_dtypes.bfloat16
NCORES = 8

ENGS = ('pe', 'act', 'dve', 'pool', 'sp')


class V:
    __slots__ = ('tt', 'ap')

    def __init__(self, tt, ap):
        self.tt = tt
        self.ap = ap


class TT:
    def __init__(self, h, name):
        self.h = h
        self.name = name
        self.w = None
        self.r = {}
        self.dsem = None
        self.dval = 0

    def __getitem__(self, idx):
        return V(self, self.h[idx])

    def re(self, pattern, **kw):
        return TT_view(self, self.h.rearrange(pattern, **kw))


class TT_view:
    def __init__(self, tt, ap):
        self.tt = tt
        self.apx = ap

    def __getitem__(self, idx):
        return V(self.tt, self.apx[idx])


class Prog:
    def __init__(self, nc):
        self.nc = nc
        self.es = ExitStack()
        self.q = {e: [] for e in ENGS}
        self.seq = {e: 0 for e in ENGS}
        self.sems = {}
        self.known = {e: {} for e in ENGS}
        for e in ENGS:
            self._sem('E_' + e)
        self.ntile = 0
        self.outs = []
        self.outsem = {}
        self.mute = False

    def _sem(self, key):
        s = self.es.enter_context(self.nc.semaphore(key))
        self.sems[key] = s
        return key

    def sb(self, shape, dt, name=None):
        self.ntile += 1
        name = "sb_" + (name or f"t{self.ntile}")
        h = self.es.enter_context(self.nc.sbuf_tensor(name, list(shape), dt))
        return TT(h, name)

    def ps(self, shape, dt, name=None):
        self.ntile += 1
        name = "ps_" + (name or f"p{self.ntile}")
        h = self.es.enter_context(self.nc.psum_tensor(name, list(shape), dt))
        return TT(h, name)

    def dram(self, name, shape, dt, kind="ExternalInput"):
        h = self.nc.dram_tensor(name, list(shape), dt, kind=kind)
        t = TT(h.ap(), name)
        if kind == "ExternalOutput":
            self.outs.append(t)
        return t

    def _collect(self, e, reads, writes):
        waits = {}

        def need(ev, war=False):
            if ev is None:
                return
            key, val, eng = ev
            if eng == e:
                if e == 'pe':
                    return
            if self.known[e].get(key, 0) >= val:
                return
            if waits.get(key, 0) < val:
                waits[key] = val
        for t in reads:
            need(t.w)
        for t in writes:
            need(t.w)
            for r in t.r.values():
                need(r, war=True)
        for key, val in waits.items():
            self.known[e][key] = val
            self.q[e].append(('wait', key, val))

    def op(self, e, fn, reads=(), writes=()):
        if self.mute:
            return None
        reads = [t for t in reads if t is not None]
        self._collect(e, reads, writes)
        self.seq[e] += 1
        ev = ('E_' + e, self.seq[e], e)
        self.q[e].append(('op', fn, 'E_' + e))
        for t in reads:
            t.r[ev[0]] = ev
        for t in writes:
            t.w = ev
            t.r = {}
        return ev

    def dma(self, e, out, in_, owner=None):
        if self.mute:
            return None
        pairs = out if isinstance(out, list) else [(out, in_)]
        reads = list({id(i.tt): i.tt for (_, i) in pairs}.values())
        writes = list({id(o.tt): o.tt for (o, _) in pairs}.values())
        if owner is None:
            owner = writes[0]
        if owner.dsem is None:
            owner.dsem = self._sem('D_' + owner.name)
        self._collect(e, reads, writes)
        for (o, i) in pairs:
            owner.dval += 16
            self.q[e].append(('dma', o.ap, i.ap, owner.dsem))
        ev = (owner.dsem, owner.dval, 'dma')
        for t in reads:
            t.r[ev[0]] = ev
        for t in writes:
            t.w = ev
            t.r = {}
        return ev

    def mm(self, out, lhsT, rhs, start=True, stop=True, skip=False):
        o, l, r = out.ap, lhsT.ap, rhs.ap
        if skip:
            fn = lambda e: e.matmul(o, lhsT=l, rhs=r, start=start, stop=stop, skip_group_check=True)
        else:
            fn = lambda e: e.matmul(o, lhsT=l, rhs=r, start=start, stop=stop)
        return self.op('pe', fn, [lhsT.tt, rhs.tt], [out.tt])

    def transpose(self, out, in_, ident):
        o, i, d = out.ap, in_.ap, ident.ap
        return self.op('pe', lambda e: e.transpose(o, i, d), [in_.tt, ident.tt], [out.tt])

    def act(self, out, in_, func, bias=None, scale=None, accum=None, eng='act'):
        kw = {}
        rd = [in_.tt]
        wr = [out.tt]
        if bias is not None:
            if isinstance(bias, V):
                kw['bias'] = bias.ap
                rd.append(bias.tt)
            else:
                kw['bias'] = bias
        if scale is not None:
            if isinstance(scale, V):
                kw['scale'] = scale.ap
                rd.append(scale.tt)
            else:
                kw['scale'] = scale
        if accum is not None:
            kw['accum_out'] = accum.ap
            wr.append(accum.tt)
        o, i = out.ap, in_.ap
        return self.op(eng, lambda e: e.activation(out=o, in_=i, func=func, **kw), rd, wr)

    def tt(self, eng, out, in0, in1, op):
        o, a, b = out.ap, in0.ap, in1.ap
        return self.op(eng, lambda e: e.tensor_tensor(out=o, in0=a, in1=b, op=op), [in0.tt, in1.tt], [out.tt])

    def ts(self, eng, out, in0, s1, s2=None, op0=ALU.mult, op1=None):
        rd = [in0.tt]
        a1 = s1
        if isinstance(s1, V):
            a1 = s1.ap
            rd.append(s1.tt)
        a2 = s2
        if isinstance(s2, V):
            a2 = s2.ap
            rd.append(s2.tt)
        o, a = out.ap, in0.ap
        if op1 is None:
            fn = lambda e: e.tensor_scalar(out=o, in0=a, scalar1=a1, scalar2=None, op0=op0)
        else:
            fn = lambda e: e.tensor_scalar(out=o, in0=a, scalar1=a1, scalar2=a2, op0=op0, op1=op1)
        return self.op(eng, fn, rd, [out.tt])

    def stt(self, eng, out, in0, scalar, in1, op0, op1):
        rd = [in0.tt, in1.tt]
        s = scalar
        if isinstance(scalar, V):
            s = scalar.ap
            rd.append(scalar.tt)
        o, a, b = out.ap, in0.ap, in1.ap
        return self.op(eng, lambda e: e.scalar_tensor_tensor(out=o, in0=a, scalar=s, in1=b, op0=op0, op1=op1),
                       rd, [out.tt])

    def copy(self, eng, out, in_):
        o, i = out.ap, in_.ap
        if eng == 'act':
            return self.op(eng, lambda e: e.activation(out=o, in_=i, func=AF.Copy), [in_.tt], [out.tt])
        return self.op(eng, lambda e: e.tensor_copy(out=o, in_=i), [in_.tt], [out.tt])

    def memset(self, eng, out, val):
        o = out.ap
        return self.op(eng, lambda e: e.memset(o, val), [], [out.tt])

    def recip(self, out, in_):
        o, i = out.ap, in_.ap
        return self.op('dve', lambda e: e.reciprocal(out=o, in_=i), [in_.tt], [out.tt])

    def emit(self):
        nc = self.nc
        sems = self.sems
        q = self.q
        waits = {}
        for t in self.outs:
            if t.w is not None:
                key, val, _ = t.w
                waits[key] = max(waits.get(key, 0), val)
        for key, val in waits.items():
            q['sp'].append(('wait', key, val))
        for key, val in self.outsem.items():
            q['sp'].append(('wait', key, val))

        def replay(eng, items):
            for it in items:
                if it[0] == 'wait':
                    eng.wait_ge(sems[it[1]], it[2])
                elif it[0] == 'op':
                    it[1](eng).then_inc(sems[it[2]], 1)
                elif it[0] == 'dma':
                    eng.dma_start(out=it[1], in_=it[2]).then_inc(sems[it[3]], 16)
        with nc.Block() as block:
            @block.sync
            def _(eng):
                replay(eng, q['sp'])

            @block.tensor
            def _(eng):
                replay(eng, q['pe'])

            @block.scalar
            def _(eng):
                replay(eng, q['act'])

            @block.vector
            def _(eng):
                replay(eng, q['dve'])

            @block.gpsimd
            def _(eng):
                replay(eng, q['pool'])
        self.es.close()

    def dma_out(self, e, out, in_):
        ev = self.dma(e, out, in_, owner=in_.tt)
        if ev is None:
            return None
        self.outsem[ev[0]] = max(self.outsem.get(ev[0], 0), ev[1])
        return ev


def new_prog():
    nc = bass.Bass("TRN2", target_bir_lowering=False)
    return nc, Prog(nc)


def run(nc, in_maps):
    res = run_bass_kernel_spmd(nc, in_maps, core_ids=list(range(NCORES)))
    return res.results


D = 2048
S = 8192
TPC = S // NCORES
EPS = 1e-6
IN0_SEGS = [('nsa_q', 1024), ('nsa_kv', 768), ('nsa_g', 48), ('nsa_z', 1024),
            ('mla_qa', 512), ('mla_ckv', 512), ('mla_kpe', 64), ('mla_z', 1024)]
IN0_W = 4976


def seg_offsets():
    o = {}
    acc = 0
    for n, s in IN0_SEGS:
        o[n] = (acc, s)
        acc += s
    return o


def build_l0():
    nc, P = new_prog()
    CW = 768
    c_in = P.dram("c", [128, 16], F32)
    w_in = P.dram("w", [2, D, CW], F32)
    b_in = P.dram("b", [2, CW], F32)
    o = P.dram("o", [2, CW], F32, "ExternalOutput")
    cs = P.sb([128, 16], F32)
    sc = P.sb([128, 16], F32)
    ws = [P.sb([128, 16, CW], F32, name=f"w{l}") for l in range(2)]
    bs = P.sb([1, 2, CW], F32)
    os_ = P.sb([1, 2, CW], F32)
    P.dma('sp', cs[:, :], c_in[:, :])
    P.dma('sp', bs[0:1, :, :], b_in.re("(o l) c -> o l c", o=1)[:, :, :])
    for l in range(2):
        wv = w_in.re("l (kc p) c -> l p kc c", p=128)
        P.dma('sp' if l == 0 else 'pool', ws[l][:, :, :], wv[l])
    P.act(sc[:, :], cs[:, :], AF.Silu)
    pss = [P.ps([128, 512], F32, name=f"ps{i}") for i in range(4)]
    for l in range(2):
        for hf in range(2):
            ps = pss[l * 2 + hf]
            for kc in range(16):
                P.mm(ps[0:1, 0:384], sc[:, kc:kc + 1], ws[l][:, kc, hf * 384:(hf + 1) * 384],
                     start=(kc == 0), stop=(kc == 15))
            P.tt('dve', os_[0:1, l, hf * 384:(hf + 1) * 384], ps[0:1, 0:384], bs[0:1, l, hf * 384:(hf + 1) * 384],
                 ALU.add)
    P.dma_out('sp', o.re("(o l) c -> o l c", o=1)[:, :, :], os_[0:1, :, :])
    P.emit()
    return nc


def run_l0(inp):
    nc = build_l0()
    CW = 768
    c = np.ascontiguousarray(inp['c'][0].reshape(16, 128).T)
    maps = []
    for i in range(NCORES):
        maps.append({"c": c,
                     "w": np.ascontiguousarray(inp['ada_w'][:, :, i * CW:(i + 1) * CW]),
                     "b": np.ascontiguousarray(inp['ada_b'][:, i * CW:(i + 1) * CW])})
    res = run(nc, maps)
    mod = np.concatenate([r["o"] for r in res], axis=1)
    return mod


class LinCtx:
    def __init__(self, P, kcmax=16, npsum=4):
        self.P = P
        self.wst = [P.sb([128, kcmax, 256], F32, name=f"wst{i}") for i in range(2)]
        self.wbf = [P.sb([128, kcmax, 256], BF16, name=f"wbf{i}") for i in range(2)]
        self.pss = [P.ps([128, 512], F32, name=f"lps{i}") for i in range(npsum)]
        self.wi = 0
        self.pi = 0

    def next_ps(self):
        p = self.pss[self.pi % len(self.pss)]
        self.pi += 1
        return p


def group_chunks(chunks, maxw=256):
    groups = []
    cur = []
    for (f0, fs, tag) in chunks:
        if cur and cur[-1][0] + cur[-1][1] == f0 and (f0 + fs - cur[0][0]) <= maxw:
            cur.append((f0, fs, tag))
        else:
            if cur:
                groups.append(cur)
            cur = [(f0, fs, tag)]
    if cur:
        groups.append(cur)
    return groups


def linear(L, actT, KP, KC, tgroups, w_dram, chunks, evac, cast_eng='pool'):
    P = L.P
    wv = w_dram.re("(kc p) f -> p kc f", p=KP)
    groups = group_chunks(chunks)
    loaded = {}

    def load(gi):
        g = groups[gi]
        c0 = g[0][0]
        fw = g[-1][0] + g[-1][1] - c0
        b = L.wi % 2
        L.wi += 1
        P.dma('sp', L.wst[b][0:KP, 0:KC, 0:fw], wv[:, :, c0:c0 + fw])
        P.copy(cast_eng, L.wbf[b][0:KP, 0:KC, 0:fw], L.wst[b][0:KP, 0:KC, 0:fw])
        loaded[gi] = (b, c0)
    load(0)
    for gi, g in enumerate(groups):
        if gi + 1 < len(groups):
            load(gi + 1)
        b, c0 = loaded[gi]
        for (f0, fs, tag) in g:
            for (g0, gs) in tgroups:
                ps = L.next_ps()
                for kc in range(KC):
                    P.mm(ps[0:fs, 0:gs], L.wbf[b][0:KP, kc, f0 - c0:f0 - c0 + fs], actT[0:KP, kc, g0:g0 + gs],
                         start=(kc == 0), stop=(kc == KC - 1))
                evac(ps, f0, fs, tag, g0, gs)


def fm_rstd(P, src, KC, T, Dn, eps, rstd, ones, sqb, pss):
    gi = 0
    for g0 in range(0, T, 512):
        gs = min(512, T - g0)
        ps = pss[gi % len(pss)]
        gi += 1
        for kc in range(KC):
            sq = sqb[kc % len(sqb)]
            P.act(sq[:, 0:gs], src(kc, g0, gs), AF.Square)
            P.mm(ps[:, 0:gs], ones[:, :], sq[:, 0:gs], start=(kc == 0), stop=(kc == KC - 1))
        P.act(rstd[:, g0:g0 + gs], ps[:, 0:gs], AF.Sqrt, bias=eps_tile(P, eps)[:, 0:1], scale=1.0 / Dn)
        P.recip(rstd[:, g0:g0 + gs], rstd[:, g0:g0 + gs])


_eps_tiles = {}


def eps_tile(P, eps):
    key = (id(P), eps)
    if key not in _eps_tiles:
        t = P.sb([128, 1], F32, name=f"eps{len(_eps_tiles)}")
        P.memset('pool', t[:, :], eps)
        _eps_tiles[key] = t
    return _eps_tiles[key]


MLA_SCALE = 192 ** -0.5
NSA_SCALE = 64 ** -0.5


def build_l1():
    nc, P = new_prog()
    T = TPC
    TG = [(0, 512), (512, 512)]
    so = seg_offsets()
    xT = P.dram("xT", [D, T], F32)
    vecs = P.dram("vecs", [128, 3, 16], F32)
    w_in = P.dram("w_in", [D, IN0_W], F32)
    mg = P.dram("mg", [128, 2, 4], F32)
    w_qb = P.dram("w_qb", [512, 1536], F32)
    w_kvb = P.dram("w_kvb", [512, 2048], F32)
    cs_in = P.dram("cs", [32, 2, T], F32)
    o_q = P.dram("o_q", [1024, T], BF16, "ExternalOutput")
    o_kv = P.dram("o_kv", [768, T], BF16, "ExternalOutput")
    o_g = P.dram("o_g", [48, T], F32, "ExternalOutput")
    o_z = P.dram("o_z", [1024, T], F32, "ExternalOutput")
    o_mz = P.dram("o_mz", [1024, T], F32, "ExternalOutput")
    o_mq = P.dram("o_mq", [8, 192, T], BF16, "ExternalOutput")
    o_mk = P.dram("o_mk", [8, 128, T], BF16, "ExternalOutput")
    o_mv = P.dram("o_mv", [8, 128, T], BF16, "ExternalOutput")
    o_kpe = P.dram("o_kpe", [64, T], BF16, "ExternalOutput")

    L = LinCtx(P)
    ones = P.sb([128, 128], F32, name="ones")
    P.memset('pool', ones[:, :], 1.0)
    vs = P.sb([128, 3, 16], F32, name="vs")
    gsc = P.sb([128, 16], F32, name="gsc")
    mgs = P.sb([128, 2, 4], F32, name="mgs")
    cs = P.sb([32, 2, T], F32, name="cs")
    P.dma('sp', vs[:, :, :], vecs[:, :, :])
    P.dma('sp', mgs[:, :, :], mg[:, :, :])
    P.dma('sp', cs[:, :, :], cs_in[:, :, :])
    P.stt('dve', gsc[:, :], vs[:, 1, :], 1.0, vs[:, 0, :], ALU.add, ALU.mult)
    xb = [P.sb([128, T], F32, name=f"xb{i}") for i in range(3)]
    sqb = [P.sb([128, 512], F32, name=f"sq{i}") for i in range(2)]
    stat_ps = [P.ps([128, 512], F32, name=f"sps{i}") for i in range(2)]
    rstd = P.sb([128, T], F32, name="rstd")
    hT = P.sb([128, 16, T], BF16, name="hT")
    tmp = P.sb([128, T], F32, name="tmp")
    xv = xT.re("(kc p) t -> p kc t", p=128)

    xi = [0]

    def src1(kc, g0, gs):
        if g0 == 0:
            b = xb[xi[0] % 3]
            xi[0] += 1
            P.dma('sp', b[:, :], xv[:, kc, :])
            src1.cur[kc] = b
        return src1.cur[kc][:, g0:g0 + gs]
    src1.cur = {}
    ps0, ps1 = stat_ps
    for kc in range(16):
        b = xb[kc % 3]
        P.dma('sp', b[:, :], xv[:, kc, :])
        for gi, (g0, gs) in enumerate(TG):
            sq = sqb[gi]
            P.act(sq[:, 0:gs], b[:, g0:g0 + gs], AF.Square)
            P.mm(stat_ps[gi][:, 0:gs], ones[:, :], sq[:, 0:gs], start=(kc == 0), stop=(kc == 15))
    et = eps_tile(P, EPS)
    for gi, (g0, gs) in enumerate(TG):
        P.act(rstd[:, g0:g0 + gs], stat_ps[gi][:, 0:gs], AF.Sqrt, bias=et[:, 0:1], scale=1.0 / D)
        P.recip(rstd[:, g0:g0 + gs], rstd[:, g0:g0 + gs])
    for kc in range(16):
        b = xb[(kc + 1) % 3]
        P.dma('sp', b[:, :], xv[:, kc, :])
        P.tt('pool', tmp[:, :], b[:, :], rstd[:, :], ALU.mult)
        P.ts('dve', hT[:, kc, :], tmp[:, :], gsc[:, kc:kc + 1], vs[:, 2, kc:kc + 1], ALU.mult, ALU.add)

    chunks = []
    for name, (o0, sz) in so.items():
        step = 32 if name == 'mla_kpe' else 128
        for f0 in range(o0, o0 + sz, step):
            chunks.append((f0, min(step, o0 + sz - f0), name))
    qa = P.sb([128, 4, T], F32, name="qa")
    ckv = P.sb([128, 4, T], F32, name="ckv")
    kpeA = P.sb([32, T], F32, name="kpeA")
    st32 = [P.sb([128, 512], F32, name=f"st32_{i}") for i in range(4)]
    st16 = [P.sb([128, 512], BF16, name=f"st16_{i}") for i in range(4)]
    r1 = P.sb([32, 512], F32, name="r1")
    r2 = P.sb([32, 512], F32, name="r2")
    cnt = {'a': 0, 'b': 0}

    def s32():
        cnt['a'] += 1
        return st32[cnt['a'] % 4]

    def s16():
        cnt['b'] += 1
        return st16[cnt['b'] % 4]

    def rope(Av, Bv, g0, gs, scale, out1, out2):
        cosv = cs[0:32, 0, g0:g0 + gs]
        sinv = cs[0:32, 1, g0:g0 + gs]
        P.tt('dve', r1[:, 0:gs], Av, cosv, ALU.mult)
        P.tt('dve', r2[:, 0:gs], Bv, sinv, ALU.mult)
        P.tt('dve', r1[:, 0:gs], r1[:, 0:gs], r2[:, 0:gs], ALU.subtract)
        s = s16()
        P.act(s[0:32, 0:gs], r1[:, 0:gs], AF.Copy, scale=scale)
        P.dma_out('sp', out1, s[0:32, 0:gs])
        P.tt('dve', r1[:, 0:gs], Av, sinv, ALU.mult)
        P.tt('dve', r2[:, 0:gs], Bv, cosv, ALU.mult)
        P.tt('dve', r1[:, 0:gs], r1[:, 0:gs], r2[:, 0:gs], ALU.add)
        s = s16()
        P.act(s[0:32, 0:gs], r1[:, 0:gs], AF.Copy, scale=scale)
        P.dma_out('sp', out2, s[0:32, 0:gs])

    def evac_in(ps, f0, fs, name, g0, gs):
        o0, sz = so[name]
        r0 = f0 - o0
        if name == 'nsa_q':
            s = s16()
            P.act(s[0:fs, 0:gs], ps[0:fs, 0:gs], AF.Copy, scale=NSA_SCALE)
            P.dma_out('sp', o_q[r0:r0 + fs, g0:g0 + gs], s[0:fs, 0:gs])
        elif name == 'nsa_kv':
            s = s16()
            P.copy('dve', s[0:fs, 0:gs], ps[0:fs, 0:gs])
            P.dma_out('sp', o_kv[r0:r0 + fs, g0:g0 + gs], s[0:fs, 0:gs])
        elif name == 'nsa_g':
            s = s32()
            P.act(s[0:fs, 0:gs], ps[0:fs, 0:gs], AF.Sigmoid)
            P.dma_out('sp', o_g[r0:r0 + fs, g0:g0 + gs], s[0:fs, 0:gs])
        elif name in ('nsa_z', 'mla_z'):
            s = s32()
            P.act(s[0:fs, 0:gs], ps[0:fs, 0:gs], AF.Silu)
            dst = o_z if name == 'nsa_z' else o_mz
            P.dma_out('sp', dst[r0:r0 + fs, g0:g0 + gs], s[0:fs, 0:gs])
        elif name == 'mla_qa':
            P.copy('dve', qa[:, r0 // 128, g0:g0 + gs], ps[0:fs, 0:gs])
        elif name == 'mla_ckv':
            P.copy('dve', ckv[:, r0 // 128, g0:g0 + gs], ps[0:fs, 0:gs])
        elif name == 'mla_kpe':
            if r0 == 0:
                P.copy('dve', kpeA[:, g0:g0 + gs], ps[0:32, 0:gs])
            else:
                rope(kpeA[:, g0:g0 + gs], ps[0:32, 0:gs], g0, gs, 1.0,
                     o_kpe[0:32, g0:g0 + gs], o_kpe[32:64, g0:g0 + gs])
    linear(L, hT, 128, 16, TG, w_in, chunks, evac_in)

    qn = P.sb([128, 4, T], BF16, name="qn")
    cn = P.sb([128, 4, T], BF16, name="cn")
    for (srcT, dstT, gi_) in ((qa, qn, 0), (ckv, cn, 1)):
        for gi, (g0, gs) in enumerate(TG):
            for kc in range(4):
                sq = sqb[kc % 2]
                P.act(sq[:, 0:gs], srcT[:, kc, g0:g0 + gs], AF.Square)
                P.mm(stat_ps[gi][:, 0:gs], ones[:, :], sq[:, 0:gs], start=(kc == 0), stop=(kc == 3))
            P.act(rstd[:, g0:g0 + gs], stat_ps[gi][:, 0:gs], AF.Sqrt, bias=et[:, 0:1], scale=1.0 / 512)
            P.recip(rstd[:, g0:g0 + gs], rstd[:, g0:g0 + gs])
        for kc in range(4):
            P.stt('dve', dstT[:, kc, :], srcT[:, kc, :], mgs[:, gi_, kc:kc + 1], rstd[:, :], ALU.mult, ALU.mult)

    qchunks = []
    for h in range(8):
        qchunks.append((h * 192, 128, ('n', h)))
        qchunks.append((h * 192 + 128, 32, ('a', h)))
        qchunks.append((h * 192 + 160, 32, ('b', h)))
    qA = [P.sb([32, 512], F32, name=f"qA{i}") for i in range(2)]

    def evac_q(ps, f0, fs, tag, g0, gs):
        kind, h = tag
        if kind == 'n':
            s = s16()
            P.act(s[0:128, 0:gs], ps[0:128, 0:gs], AF.Copy, scale=MLA_SCALE)
            P.dma_out('sp', o_mq[h, 0:128, g0:g0 + gs], s[0:128, 0:gs])
        elif kind == 'a':
            P.copy('dve', qA[g0 // 512][:, 0:gs], ps[0:32, 0:gs])
        else:
            rope(qA[g0 // 512][:, 0:gs], ps[0:32, 0:gs], g0, gs, MLA_SCALE,
                 o_mq[h, 128:160, g0:g0 + gs], o_mq[h, 160:192, g0:g0 + gs])
    linear(L, qn, 128, 4, TG, w_qb, qchunks, evac_q)

    kvchunks = []
    for h in range(8):
        kvchunks.append((h * 256, 128, ('k', h)))
        kvchunks.append((h * 256 + 128, 128, ('v', h)))

    def evac_kv(ps, f0, fs, tag, g0, gs):
        kind, h = tag
        s = s16()
        P.copy('dve', s[0:128, 0:gs], ps[0:128, 0:gs])
        dst = o_mk if kind == 'k' else o_mv
        P.dma_out('sp', dst[h, :, g0:g0 + gs], s[0:128, 0:gs])
    linear(L, cn, 128, 4, TG, w_kvb, kvchunks, evac_kv)
    P.emit()
    return nc


def rope_tables():
    inv = (10000.0 ** (-np.arange(0, 64, 2, dtype=np.float32) / np.float32(64))).astype(np.float32)
    ang = np.arange(S, dtype=np.float32)[:, None] * inv[None]
    return np.cos(ang).astype(np.float32), np.sin(ang).astype(np.float32)


def pk(v):
    return np.ascontiguousarray(v.reshape(-1, 128).T)


def run_l1(inp, mod):
    nc = build_l1()
    shift, scale = mod[0, 0:D], mod[0, D:2 * D]
    vecs = np.ascontiguousarray(np.stack([pk(inp['norm_g'][0]), pk(scale), pk(shift)], axis=1))
    mg = np.ascontiguousarray(np.stack([pk(inp['mla_qa_g'][0]), pk(inp['mla_kva_g'][0])], axis=1))
    cos, sin = rope_tables()
    xTfull = np.ascontiguousarray(inp['x'][0].T)
    maps = []
    for i in range(NCORES):
        sl = slice(i * TPC, (i + 1) * TPC)
        maps.append({"xT": np.ascontiguousarray(xTfull[:, sl]), "vecs": vecs,
                     "w_in": inp['a_w_in'][0], "mg": mg,
                     "w_qb": inp['mla_w_qb'][0], "w_kvb": inp['mla_w_kvb'][0],
                     "cs": np.ascontiguousarray(np.stack([cos[sl].T, sin[sl].T], axis=1))})
    res = run(nc, maps)
    out = {}
    for k in ("o_q", "o_kv", "o_g", "o_z", "o_mz", "o_kpe"):
        out[k] = np.concatenate([r[k] for r in res], axis=-1)
    for k in ("o_mq", "o_mk", "o_mv"):
        out[k] = np.concatenate([r[k] for r in res], axis=-1)
    return out


NEGM = -30000.0


def blk_of(i, j):
    return 8 * j + (i if j % 2 == 0 else 7 - i)


def split3(x):
    x = np.asarray(x, dtype=np.float64)
    x1 = x.astype(NPBF).astype(np.float64)
    x2 = (x - x1).astype(NPBF).astype(np.float64)
    x3 = (x - x1 - x2).astype(NPBF).astype(np.float64)
    return x1, x2, x3


def alibi_q_rows(tok):
    slopes = (2.0 ** (-8.0 * np.arange(1, 17, dtype=np.float32) / np.float32(16))).astype(np.float32)
    out = np.zeros((16, 10, len(tok)), np.float64)
    for h in range(16):
        s1, s2, s3 = split3(slopes[h])
        t1, t2, t3 = split3(np.float64(slopes[h]) * tok.astype(np.float64))
        out[h, 0] = s1; out[h, 1] = s2; out[h, 2] = s3
        out[h, 3] = s1; out[h, 4] = s2; out[h, 5] = s3
        out[h, 6] = -t1; out[h, 7] = -t2; out[h, 8] = -t3
        out[h, 9] = 1.0
    return out.astype(NPBF)


def alibi_k_rows(pos, valid=None):
    pos = np.asarray(pos, dtype=np.int64)
    out = np.zeros((10, len(pos)), np.float64)
    p = np.maximum(pos, 0)
    hi = (p // 128) * 128
    lo = p % 128
    out[0:3] = hi
    out[3:6] = lo
    out[6:9] = 1.0
    if valid is not None:
        out[9] = np.where(valid, 0.0, NEGM)
    return out.astype(NPBF)


def build_l2():
    nc, P = new_prog()
    NQ = 1024
    d_q = P.dram("qaug", [74, 16 * NQ], BF16)
    d_cmpin = P.dram("cmpin", [2, 2, 64, S], BF16)
    d_kcrows = P.dram("kcrows", [10, 512], BF16)
    d_w1 = P.dram("w1", [2, 64, 32 * 64], F32)
    d_w2 = P.dram("w2", [2, 64, 64], F32)
    d_peT = P.dram("peT", [2, 64, 32], F32)
    d_b1 = P.dram("b1", [64, 2], F32)
    d_vcconst = P.dram("vcconst", [128, 4 * 129], BF16)
    d_cmpmask = P.dram("cmpmask", [128, 4 * NQ], BF16)
    d_tailmask = P.dram("tailmask", [128, 64 * 128], BF16)
    d_winmask = P.dram("winmask", [128, 2 * 512], BF16)
    d_expand = P.dram("expand", [128, S], BF16)
    d_force = P.dram("force", [128, 8 * 128], F32)
    d_ks = P.dram("ks", [2, 74, S], BF16)
    d_vs = P.dram("vs", [2, 128, 64 * 65], BF16)
    d_kw = P.dram("kw", [2, 8, 74, 640], BF16)
    d_vw = P.dram("vw", [2, 8, 128, 5 * 65], BF16)
    d_gates = P.dram("gates", [128, 8 * 48], F32)
    d_sz = P.dram("sz", [8, 128, 1024], F32)
    d_mzz = P.dram("mzz", [8, 128, 1024], F32)
    d_mqn = P.dram("mqn", [128, 8 * NQ], BF16)
    d_mqp = P.dram("mqp", [64, 8 * NQ], BF16)
    d_mkn = P.dram("mkn", [8, 128, S], BF16)
    d_mkp = P.dram("mkp", [64, S], BF16)
    d_mv = P.dram("mv", [8, 128, 64 * 129], BF16)
    d_idb = P.dram("idb", [128, 128], BF16)
    d_idf = P.dram("idf", [128, 128], F32)
    o_yz = P.dram("o_yz", [8, 128, 2048], BF16, "ExternalOutput")

    Qb = P.sb([128, 16, NQ], BF16, name="Qb")
    obuf = P.sb([128, 8, 1024], F32, name="obuf")
    imp = [P.sb([128, 8, 128], F32, name=f"imp{g}") for g in range(2)]
    negmt = [P.sb([128, NQ], BF16, name=f"negmt{g}") for g in range(2)]
    expand = P.sb([128, S], BF16, name="expand")
    tailm = P.sb([128, 64, 128], BF16, name="tailm")
    cmpm = P.sb([128, 4, NQ], BF16, name="cmpm")
    winm = P.sb([128, 2, 512], BF16, name="winm")
    force = P.sb([128, 8, 128], F32, name="force")
    E = [P.sb([128, NQ], BF16, name=f"E{i}") for i in range(4)]
    bufK = P.sb([128, S], BF16, name="bufK")
    bufV = P.sb([128, 64 * 129], BF16, name="bufV")
    kpe = expand
    gat = P.sb([128, 8, 48], F32, name="gat")
    idb = P.sb([128, 128], BF16, name="idb")
    idf = P.sb([128, 128], F32, name="idf")
    zst = [P.sb([128, 1024], F32, name=f"zst{i}") for i in range(2)]
    yst = [P.sb([128, 1024], BF16, name=f"yst{i}") for i in range(2)]
    kcmp = [P.sb([74, 512], BF16, name=f"kcmp{g}") for g in range(2)]
    vcmp = [P.sb([128, 4, 193], BF16, name=f"vcmp{g}") for g in range(2)]
    w1s = obuf.re("p j f -> p (j f)")
    w1b = P.sb([64, 32, 64], BF16, name="w1b")
    w2s = P.sb([64, 64], F32, name="w2s")
    w2b = P.sb([64, 64], BF16, name="w2b")
    peTs = P.sb([64, 32], F32, name="peTs")
    peTb = P.sb([64, 32], BF16, name="peTb")
    b1s = P.sb([64, 2], F32, name="b1s")
    cst = P.sb([64, 1], F32, name="cst")
    hid = P.sb([64, 512], BF16, name="hid")
    sm = [P.sb([128, 8], F32, name=f"sm{i}") for i in range(8)]
    wk = [P.sb([128, 128], F32, name=f"wk{i}") for i in range(3)]
    Sps = [P.ps([128, NQ], F32, name=f"S{i}") for i in range(2)]
    Aps = [P.ps([128, 512], F32, name=f"A{i}") for i in range(4)]

    P.dma('sp', Qb[0:74, :, :], d_q.re("r (h q) -> r h q", h=16)[:, :, :])
    P.dma('sp', idb[:, :], d_idb[:, :])
    P.dma('sp', idf[:, :], d_idf[:, :])
    P.dma('sp', cmpm[:, :, :], d_cmpmask.re("p (c q) -> p c q", c=4)[:, :, :])
    P.dma('sp', gat[:, :, :], d_gates.re("p (j c) -> p j c", j=8)[:, :, :])
    P.dma('sp', force[:, :, :], d_force.re("p (j c) -> p j c", j=8)[:, :, :])
    P.dma('sp', tailm[:, :, :], d_tailmask.re("p (c q) -> p c q", c=64)[:, :, :])
    P.dma('sp', winm[:, :, :], d_winmask.re("p (c q) -> p c q", c=2)[:, :, :])
    P.dma('sp', expand[:, :], d_expand[:, :])
    P.dma('sp', b1s[:, :], d_b1[:, :])
    smi = [0]

    def small():
        smi[0] += 1
        return sm[smi[0] % 8]
    Ei = [0]

    def nextE():
        Ei[0] += 1
        return E[Ei[0] % 4]
    Si = [0]

    def nextS():
        Si[0] += 1
        return Sps[Si[0] % 2]
    Ai = [0]

    def nextA():
        Ai[0] += 1
        return Aps[Ai[0] % 4]

    kcv = bufK.re("p (n r) -> p n r", r=16)
    for g in range(2):
        P.dma('sp', kcmp[g][64:74, :], d_kcrows[:, :])
        P.dma('sp', vcmp[g][:, :, 64:193], d_vcconst.re("p (c f) -> p c f", c=4)[:, :, :])
        P.memset('pool', vcmp[g][:, :, 0:64], 0.0)
        P.memset('pool', kcmp[g][0:64, :], 0.0)
        for kv in range(2):
            P.dma('sp', bufK[0:64, :], d_cmpin[g, kv, :, :])
            P.dma('sp', w1s[0:64, 0:2048], d_w1[kv, :, :])
            P.dma('sp', w2s[:, :], d_w2[kv, :, :])
            P.dma('sp', peTs[:, :], d_peT[kv, :, :])
            P.copy('pool', w1b[:, :, :], obuf[0:64, 0:2, :].tt.re("p j (a e) -> p (j a) e", e=64)[0:64, 0:32, :])
            P.copy('pool', w2b[:, :], w2s[:, :])
            P.copy('pool', peTb[:, :], peTs[:, :])
            a1 = nextA()
            for l in range(32):
                P.mm(a1[0:64, 0:1], w1b[:, l, :], peTb[:, l:l + 1], start=(l == 0), stop=(l == 31))
            P.tt('dve', cst[:, :], a1[0:64, 0:1], b1s[:, kv:kv + 1], ALU.add)
            a2 = nextA()
            for l in range(32):
                rhs = kcv[0:64, 0:511, l] if l < 16 else kcv[0:64, 1:512, l - 16]
                P.mm(a2[0:64, 0:511], w1b[:, l, :], rhs, start=(l == 0), stop=(l == 31))
            P.memset('pool', hid[:, :], 0.0)
            P.act(hid[:, 0:511], a2[0:64, 0:511], AF.Silu, bias=cst[:, 0:1])
            if kv == 0:
                a3 = nextA()
                P.mm(a3[0:64, 0:511], w2b[:, :], hid[:, 0:511])
                P.copy('dve', kcmp[g][0:64, 0:511], a3[0:64, 0:511])
            else:
                for c4 in range(4):
                    a3 = nextA()
                    P.mm(a3[:, 0:64], hid[:, c4 * 128:(c4 + 1) * 128], w2b[:, :])
                    P.copy('dve', vcmp[g][:, c4, 0:64], a3[:, 0:64])

    def finish(acc_v, zcol, gate_v, dst, first):
        s_ = small()
        P.ts('dve', s_[:, 0:1], zcol, 1e-30, None, ALU.max)
        P.recip(s_[:, 1:2], s_[:, 0:1])
        if gate_v is not None:
            P.tt('dve', s_[:, 2:3], s_[:, 1:2], gate_v, ALU.mult)
            sc_ = s_[:, 2:3]
        else:
            sc_ = s_[:, 1:2]
        if first:
            P.ts('dve', dst, acc_v, sc_, None, ALU.mult)
        else:
            P.stt('dve', dst, acc_v, sc_, dst, ALU.mult, ALU.add)
        return s_

    for h in range(16):
        g = h // 8
        Es = []
        for c4 in range(4):
            sp_ = nextS()
            for hf in range(2):
                lo, hi = hf * 512, (hf + 1) * 512
                P.mm(sp_[:, lo:hi], kcmp[g][0:74, c4 * 128:(c4 + 1) * 128], Qb[0:74, h, lo:hi], start=True, stop=False)
                P.mm(sp_[:, lo:hi], idb[:, :], cmpm[:, c4, lo:hi], start=False, stop=True)
            e_ = nextE()
            P.act(e_[:, :], sp_[:, :], AF.Exp)
            Es.append(e_)
        for j in range(8):
            acc = nextA()
            for c4 in range(4):
                P.mm(acc[:, 0:193], Es[c4][:, j * 128:(j + 1) * 128], vcmp[g][:, c4, :], start=(c4 == 0), stop=(c4 == 3))
            s_ = finish(acc[:, 0:64], acc[:, 64:65], gat[:, j, h * 3:h * 3 + 1], obuf[:, j, h * 64:(h + 1) * 64], True)
            if h % 8 == 0:
                P.ts('dve', imp[g][:, j, :], acc[:, 65:193], s_[:, 1:2], None, ALU.mult)
            else:
                P.stt('dve', imp[g][:, j, :], acc[:, 65:193], s_[:, 1:2], imp[g][:, j, :], ALU.mult, ALU.add)
        if h % 8 == 7:
            for j in range(8):
                w0, w1_, w2_ = wk
                s_ = small()
                P.tt('dve', w0[:, :], imp[g][:, j, :], force[:, j, :], ALU.add)
                P.op('dve', lambda e, m0=s_[:, 0:8].ap, i0=w0[:, :].ap: e.max(out=m0, in_=i0), [w0], [s_])
                P.op('dve', lambda e, o=w1_[:, :].ap, m0=s_[:, 0:8].ap, i0=w0[:, :].ap:
                     e.match_replace(out=o, in_to_replace=m0, in_values=i0, imm_value=-1e30), [w0, s_], [w1_])
                s2 = small()
                P.op('dve', lambda e, m0=s2[:, 0:8].ap, i0=w1_[:, :].ap: e.max(out=m0, in_=i0), [w1_], [s2])
                s3 = small()
                P.op('dve', lambda e, o=s3[:, 0:1].ap, i=s2[:, 0:8].ap:
                     e.tensor_reduce(out=o, in_=i, axis=AX.X, op=ALU.min), [s2], [s3])
                P.ts('dve', w2_[:, :], w0[:, :], s3[:, 0:1], None, ALU.is_ge)
                P.ts('dve', w2_[:, :], w2_[:, :], -1.0, -NEGM, ALU.add, ALU.mult)
                a_ = nextA()
                P.transpose(a_[:, 0:128], w2_[:, :], idf[:, :])
                P.copy('dve', negmt[g][:, j * 128:(j + 1) * 128], a_[:, 0:128])

    accpair = [(Aps[0], Aps[1]), (Aps[2], Aps[3])]
    for g in range(2):
        P.dma('sp', bufK[0:74, :], d_ks[g, :, :])
        P.dma('sp', bufV[:, 0:64 * 65], d_vs[g, :, :])
        vsv = bufV.re("p (c f) -> p c f", f=129)
        for hh in range(8):
            h = g * 8 + hh
            accA, accB = accpair[h % 2]
            P.memset('dve', accA[:, :], 0.0)
            P.memset('dve', accB[:, :], 0.0)

            def accv(j, w0_, w):
                t_ = accA if j < 4 else accB
                o_ = (j % 4) * 128
                return t_[:, o_ + w0_:o_ + w0_ + w]
            for c in range(64):
                j0 = c // 8
                sp_ = nextS()
                for hf in range(2):
                    lo, hi = max(j0 * 128, hf * 512), (hf + 1) * 512
                    if lo >= hi:
                        continue
                    tail_here = (j0 * 128 >= hf * 512) and (j0 * 128 < hi)
                    P.mm(sp_[:, lo:hi], bufK[0:74, c * 128:(c + 1) * 128], Qb[0:74, h, lo:hi], start=True, stop=False)
                    P.mm(sp_[:, lo:hi], expand[:, c * 128:(c + 1) * 128], negmt[g][:, lo:hi], start=False,
                         stop=(not tail_here))
                    if tail_here:
                        P.mm(sp_[:, j0 * 128:(j0 + 1) * 128], idb[:, :], tailm[:, c, :], start=False, stop=True)
                e_ = nextE()
                P.act(e_[:, j0 * 128:NQ], sp_[:, j0 * 128:NQ], AF.Exp)
                for j in range(j0, 8):
                    P.mm(accv(j, 0, 65), e_[:, j * 128:(j + 1) * 128], bufV[:, c * 65:(c + 1) * 65],
                         start=False, stop=False, skip=True)
            for j in range(8):
                finish(accv(j, 0, 64), accv(j, 64, 1),
                       gat[:, j, h * 3 + 1:h * 3 + 2], obuf[:, j, h * 64:(h + 1) * 64], False)

    for g in range(2):
        for j in range(8):
            P.dma('sp', bufK[0:74, 0:640], d_kw[g, j, :, :])
            P.dma('sp', bufV[:, 0:5 * 65], d_vw[g, j, :, :])
            accA, accB = accpair[(g * 8 + j) % 2]
            P.memset('dve', accA[:, :], 0.0)
            P.memset('dve', accB[:, :], 0.0)

            def accw(hh, w0_, w):
                t_ = accA if hh < 4 else accB
                o_ = (hh % 4) * 128
                return t_[:, o_ + w0_:o_ + w0_ + w]
            for wc in range(5):
                sp_ = nextS()
                for hf in range(2):
                    lo, hi = hf * 512, (hf + 1) * 512
                    msk = wc in (0, 4)
                    P.mm(sp_.re("p (h q) -> p h q", q=128)[:, hf * 4:(hf + 1) * 4, :],
                         bufK[0:74, wc * 128:(wc + 1) * 128], Qb[0:74, g * 8 + hf * 4:g * 8 + hf * 4 + 4, j * 128:(j + 1) * 128],
                         start=True, stop=(not msk))
                    if msk:
                        P.mm(sp_[:, lo:hi], idb[:, :], winm[:, 0 if wc == 0 else 1, :], start=False, stop=True)
                e_ = nextE()
                P.act(e_[:, :], sp_[:, :], AF.Exp)
                for hh in range(8):
                    P.mm(accw(hh, 0, 65), e_[:, hh * 128:(hh + 1) * 128], bufV[:, wc * 65:(wc + 1) * 65],
                         start=False, stop=False, skip=True)
            for hh in range(8):
                h = g * 8 + hh
                finish(accw(hh, 0, 64), accw(hh, 64, 1), gat[:, j, h * 3 + 2:h * 3 + 3],
                       obuf[:, j, h * 64:(h + 1) * 64], False)

    for j in range(8):
        z_ = zst[j % 2]
        y_ = yst[j % 2]
        P.dma('sp', z_[:, :], d_sz[j, :, :])
        P.tt('pool', y_[:, :], obuf[:, j, :], z_[:, :], ALU.mult)
        P.dma_out('sp', o_yz[j, :, 0:1024], y_[:, :])

    P.dma('sp', Qb[:, 0:8, :], d_mqn.re("p (h q) -> p h q", h=8)[:, :, :])
    P.dma('sp', Qb[0:64, 8:16, :], d_mqp.re("p (h q) -> p h q", h=8)[:, :, :])
    P.dma('sp', kpe[0:64, :], d_mkp[:, :])
    for h in range(8):
        P.dma('sp', bufK[:, :], d_mkn[h, :, :])
        P.dma('sp', bufV[:, :], d_mv[h, :, :])
        for a_ in Aps[0:3]:
            P.memset('dve', a_[:, :], 0.0)

        def accm(j, w0_, w):
            t_ = Aps[j // 3]
            o_ = (j % 3) * 160
            return t_[:, o_ + w0_:o_ + w0_ + w]
        for c in range(64):
            j0 = c // 8
            sp_ = nextS()
            for hf in range(2):
                lo, hi = max(j0 * 128, hf * 512), (hf + 1) * 512
                if lo >= hi:
                    continue
                tail_here = (j0 * 128 >= hf * 512) and (j0 * 128 < hi)
                P.mm(sp_[:, lo:hi], bufK[:, c * 128:(c + 1) * 128], Qb[:, h, lo:hi], start=True, stop=False)
                P.mm(sp_[:, lo:hi], kpe[0:64, c * 128:(c + 1) * 128], Qb[0:64, 8 + h, lo:hi], start=False,
                     stop=(not tail_here))
                if tail_here:
                    P.mm(sp_[:, j0 * 128:(j0 + 1) * 128], idb[:, :], tailm[:, c, :], start=False, stop=True)
            e_ = nextE()
            P.act(e_[:, j0 * 128:NQ], sp_[:, j0 * 128:NQ], AF.Exp)
            for j in range(j0, 8):
                P.mm(accm(j, 0, 129), e_[:, j * 128:(j + 1) * 128], bufV[:, c * 129:(c + 1) * 129],
                     start=False, stop=False, skip=True)
        for j in range(8):
            finish(accm(j, 0, 128), accm(j, 128, 1), None, obuf[:, j, h * 128:(h + 1) * 128], True)
    for j in range(8):
        z_ = zst[j % 2]
        y_ = yst[j % 2]
        P.dma('sp', z_[:, :], d_mzz[j, :, :])
        P.tt('pool', y_[:, :], obuf[:, j, :], z_[:, :], ALU.mult)
        P.dma_out('sp', o_yz[j, :, 1024:2048], y_[:, :])
    P.emit()
    return nc


def run_l2(inp, l1):
    nc = build_l2()
    o_q, o_kv = l1['o_q'], l1['o_kv']
    bf = lambda a: np.ascontiguousarray(np.asarray(a).astype(NPBF))
    n_cmp = 511
    cmp_end = np.arange(512) * 16 + 31
    kcrows = alibi_k_rows(cmp_end)
    vcconst = np.zeros((128, 4, 129), np.float32)
    vcconst[:, :, 0] = 1.0
    for n in range(n_cmp):
        for jb in range(128):
            if 4 * jb - 1 <= n <= 4 * jb + 3:
                vcconst[n % 128, n // 128, 1 + jb] = 1.0
    vcconst = bf(vcconst.reshape(128, 4 * 129))
    expand = np.zeros((128, S), np.float32)
    expand[np.arange(S) // 64, np.arange(S)] = 1.0
    expand = bf(expand)
    kl = np.arange(128)[:, None]
    tl = np.arange(128)[None, :]
    wlo = np.where(kl > tl, 0.0, NEGM).astype(np.float32)
    whi = np.where(kl <= tl, 0.0, NEGM).astype(np.float32)
    winmask = bf(np.stack([np.tile(wlo, (1, 4)), np.tile(whi, (1, 4))], axis=1).reshape(128, 2 * 512))
    idb = bf(np.eye(128, dtype=np.float32))
    idf = np.eye(128, dtype=np.float32)
    w1 = np.ascontiguousarray(np.stack([inp['nsa_w1_k'][0], inp['nsa_w1_v'][0]]).transpose(0, 2, 1, 3).reshape(2, 64, 32 * 64))
    w2 = np.ascontiguousarray(np.stack([inp['nsa_w2_k'][0], inp['nsa_w2_v'][0]]))
    peT = np.ascontiguousarray(np.stack([inp['nsa_pe_k'][0].T, inp['nsa_pe_v'][0].T]))
    b1 = np.ascontiguousarray(np.stack([inp['nsa_b1_k'][0], inp['nsa_b1_v'][0]], axis=1))
    kvr = o_kv.reshape(6, 2, 64, S)
    cmpin = np.ascontiguousarray(np.stack([np.stack([kvr[0, g], kvr[1, g]]) for g in range(2)]))
    krows_all = alibi_k_rows(np.arange(S))
    ks = np.ascontiguousarray(np.stack([np.concatenate([kvr[2, g], krows_all], axis=0) for g in range(2)]))
    ones_col = np.ones((S, 1), NPBF)

    def tokmajor(vT, width):
        a = np.concatenate([vT.T, ones_col], axis=1)
        return np.ascontiguousarray(a.reshape(64, 128, width).transpose(1, 0, 2).reshape(128, 64 * width))
    vs = np.stack([tokmajor(kvr[3, g], 65) for g in range(2)])
    mkn = np.ascontiguousarray(l1['o_mk'])
    mkp = np.ascontiguousarray(l1['o_kpe'])
    mv = np.stack([tokmajor(l1['o_mv'][h], 129) for h in range(8)])
    gT, zT, mzT = l1['o_g'], l1['o_z'], l1['o_mz']
    maps = []
    toks = []
    for i in range(NCORES):
        blks = [blk_of(i, j) for j in range(8)]
        tok = np.concatenate([np.arange(b * 128, (b + 1) * 128) for b in blks])
        toks.append(tok)
        qrows = alibi_q_rows(tok)
        qa = np.concatenate([o_q[:, tok].reshape(16, 64, 1024), qrows], axis=1)
        qaug = np.ascontiguousarray(qa.transpose(1, 0, 2).reshape(74, 16 * 1024))
        cm = np.where(cmp_end[:, None] <= tok[None, :], 0.0, NEGM).astype(np.float32)
        cm[511, :] = NEGM
        cmpmask = bf(cm.reshape(4, 128, 1024).transpose(1, 0, 2).reshape(128, 4 * 1024))
        tm = np.zeros((128, 64, 128), np.float32)
        for c in range(64):
            j = c // 8
            kpos = c * 128 + np.arange(128)
            tpos = blks[j] * 128 + np.arange(128)
            tm[:, c, :] = np.where(kpos[:, None] <= tpos[None, :], 0.0, NEGM)
        tailmask = bf(tm.reshape(128, 64 * 128))
        fo = np.zeros((128, 8, 128), np.float32)
        for j in range(8):
            tpos = blks[j] * 128 + np.arange(128)
            cur = tpos // 64
            jb = np.arange(128)[None, :]
            f = np.zeros((128, 128), np.float32)
            f[jb == (cur[:, None] - 1)] = 3e9
            f[jb == cur[:, None]] = 2e9
            f[np.broadcast_to(jb == 0, (128, 128))] = 1e9
            fo[:, j, :] = f
        kw = np.zeros((2, 8, 74, 640), NPBF)
        vw = np.zeros((2, 8, 128, 5, 65), NPBF)
        for j in range(8):
            pos = (blks[j] - 4) * 128 + np.arange(640)
            valid = pos >= 0
            pc = np.maximum(pos, 0)
            rows = alibi_k_rows(pos, valid)
            for g in range(2):
                kk = kvr[4, g][:, pc].copy()
                kk[:, ~valid] = 0
                kw[g, j] = np.concatenate([kk, rows], axis=0)
                vv = kvr[5, g][:, pc].T.copy()
                vv[~valid] = 0
                vv = np.concatenate([vv, np.ones((640, 1), NPBF)], axis=1)
                vw[g, j] = vv.reshape(5, 128, 65).transpose(1, 0, 2)
        gates = np.ascontiguousarray(gT[:, tok].T.reshape(8, 128, 48).transpose(1, 0, 2).reshape(128, 8 * 48))
        sz = np.ascontiguousarray(zT[:, tok].T.reshape(8, 128, 1024))
        mzz = np.ascontiguousarray(mzT[:, tok].T.reshape(8, 128, 1024))
        mq = l1['o_mq'][:, :, tok]
        mqn = np.ascontiguousarray(mq[:, 0:128].transpose(1, 0, 2).reshape(128, 8 * 1024))
        mqp = np.ascontiguousarray(mq[:, 128:192].transpose(1, 0, 2).reshape(64, 8 * 1024))
        maps.append(dict(qaug=qaug, cmpin=cmpin, kcrows=kcrows, w1=w1, w2=w2, peT=peT, b1=b1, vcconst=vcconst,
                         cmpmask=cmpmask, tailmask=tailmask, winmask=winmask, expand=expand,
                         force=np.ascontiguousarray(fo.reshape(128, 8 * 128)), ks=ks, vs=vs,
                         kw=kw, vw=np.ascontiguousarray(vw.reshape(2, 8, 128, 5 * 65)), gates=gates, sz=sz, mzz=mzz,
                         mqn=mqn, mqp=mqp, mkn=mkn, mkp=mkp, mv=mv, idb=idb, idf=idf))
    res = run(nc, maps)
    yz = np.zeros((S, 2048), NPBF)
    for i in range(NCORES):
        yz[toks[i]] = res[i]["o_yz"].reshape(1024, 2048)
    return yz


def build_l3(stop=99, part='a'):
    nc, P = new_prog()
    T = TPC
    T1 = T + 1
    TGX = [(0, 1), (1, 512), (513, 512)]
    TG = [(0, 512), (512, 512)]
    A_ = "ExternalInput" if part in ('a', 'full') else "Internal"
    B_ = "ExternalInput" if part in ('b', 'full') else "Internal"
    AO = "ExternalOutput" if part in ('a', 'full') else "Internal"
    BO = "ExternalOutput" if part in ('b', 'full') else "Internal"
    IOK = {'a': ("ExternalOutput", "ExternalOutput"), 'b': ("ExternalInput", "ExternalInput"),
           'full': ("Internal", "Internal")}[part]
    yzT = P.dram("yzT", [D, T1], BF16, A_)
    xT = P.dram("xT", [D, T1], F32, A_)
    w_out = P.dram("w_out", [D, D], F32, A_)
    h_io = P.dram("h_io", [D, T1], F32, IOK[0])
    vecs = P.dram("vecs", [128, 10, 16], F32)
    mu_in = P.dram("mu", [128, 6, 16], F32)
    pm_in = P.dram("pm", [128, 1], F32)
    bones_in = P.dram("bones", [128, 128], F32)
    wr = P.dram("w_r", [D, D], F32, B_)
    wk_ = P.dram("w_k", [D, D], F32, A_)
    wv = P.dram("w_v", [D, D], F32, B_)
    wz = P.dram("w_z", [D, D], F32, B_)
    w1 = P.dram("w1", [D, 96], F32, A_)
    w2 = P.dram("w2", [96, D], F32, A_)
    a1 = P.dram("a1", [D, 96], F32, A_)
    a2 = P.dram("a2", [96, D], F32, A_)
    o_x1 = P.dram("o_x1", [D, T], F32, AO)
    o_r = P.dram("o_r", [D, T], BF16, BO)
    o_v = P.dram("o_v", [D, T], BF16, BO)
    o_kap = P.dram("o_kap", [D, T], BF16, AO)
    o_b = P.dram("o_b", [D, T], BF16, AO)
    o_km = P.dram("o_km", [D, T], BF16, AO)
    o_lw = P.dram("o_lw", [D, T], F32, AO)
    o_bonus = P.dram("o_bonus", [D, T], F32, BO)
    o_sz = P.dram("o_sz", [D, T], F32, BO)
    s_a = P.dram("s_a", [D, T], F32, "Internal")
    s_km = P.dram("s_km", [D, T], F32, IOK[1])
    s_rk = P.dram("s_rk", [D, T], F32, "Internal")

    L = LinCtx(P)
    ones = P.sb([128, 128], F32, name="ones")
    P.memset('pool', ones[:, :], 1.0)
    bones = P.sb([128, 128], F32, name="bones")
    P.dma('sp', bones[:, :], bones_in[:, :])
    vs = P.sb([128, 10, 16], F32, name="vs")
    P.dma('sp', vs[:, :, :], vecs[:, :, :])
    mus = P.sb([128, 6, 16], F32, name="mus")
    omu = P.sb([128, 6, 16], F32, name="omu")
    P.dma('sp', mus[:, :, :], mu_in[:, :, :])
    P.ts('dve', omu[:, :, :], mus[:, :, :], -1.0, 1.0, ALU.mult, ALU.add)
    pm = P.sb([128, 1], F32, name="pm")
    P.dma('sp', pm[:, :], pm_in[:, :])
    gsc = P.sb([128, 16], F32, name="gsc")
    P.stt('dve', gsc[:, :], vs[:, 2, :], 1.0, vs[:, 1, :], ALU.add, ALU.mult)
    ab = P.sb([128, 16, T1], BF16, name="ab")
    hs = P.sb([128, 16, T1], F32, name="hs")
    xb = [P.sb([128, T1], F32, name=f"xb{i}") for i in range(2)]
    sqb = [P.sb([128, 512], F32, name=f"sq{i}") for i in range(2)]
    stat_ps = [P.ps([128, 512], F32, name=f"sps{i}") for i in range(3)]
    rstd = P.sb([128, T1], F32, name="rstd")
    tmp = P.sb([128, T1], F32, name="tmp")
    st32 = [P.sb([128, 1024], F32, name=f"st32_{i}") for i in range(4)]
    st16 = [P.sb([128, 1024], BF16, name=f"st16_{i}") for i in range(4)]
    ld32 = [P.sb([128, 1024], F32, name=f"ld32_{i}") for i in range(2)]
    cur = {}
    e32 = [P.sb([128, 512], F32, name=f"e32_{i}") for i in range(3)]
    lora = P.sb([96, 1, T], BF16, name="lora")
    cnt = {'a': 0, 'b': 0, 'c': 0, 'd': 0}

    def s32(key, g0):
        if g0 == 0:
            cnt['a'] += 1
            cur[key] = st32[cnt['a'] % 4]
        return cur[key]

    def s16(key, g0):
        if g0 == 0:
            cnt['b'] += 1
            cur[key] = st16[cnt['b'] % 4]
        return cur[key]

    def l32(src, f0, g0):
        if g0 == 0:
            cnt['c'] += 1
            t_ = ld32[cnt['c'] % 2]
            P.dma('act', t_[:, :], src[f0:f0 + 128, :])
            cur[('l', f0)] = t_
        return cur[('l', f0)]

    def t32():
        cnt['d'] += 1
        return e32[cnt['d'] % 3]

    if part == 'b':
        P.mute = True
    P.dma('sp', ab[:, :, :], yzT.re("(kc p) t -> p kc t", p=128)[:, :, :])
    xv = xT.re("(kc p) t -> p kc t", p=128)
    allch = [(f0, 128, None) for f0 in range(0, D, 128)]
    xcur = {}

    def evac_out(ps, f0, fs, tag, g0, gs):
        fc = f0 // 128
        if g0 == 0:
            b = xb[fc % 2]
            P.dma('sp', b[:, :], xv[:, fc, :])
            xcur[fc] = b
        b = xcur[fc]
        P.stt('dve', hs[:, fc, g0:g0 + gs], ps[:, 0:gs], vs[:, 0, fc:fc + 1], b[:, g0:g0 + gs], ALU.mult, ALU.add)
        if g0 == 513:
            P.dma_out('sp', o_x1[f0:f0 + 128, :], hs[:, fc, 1:T1])
    linear(L, ab, 128, 16, TGX, w_out, allch, evac_out)
    if stop == 1:
        P.emit()
        return nc

    et = eps_tile(P, EPS)
    for kc in range(16):
        for gi, (g0, gs) in enumerate(TGX):
            sq = sqb[gi % 2]
            P.act(sq[:, 0:gs], hs[:, kc, g0:g0 + gs], AF.Square)
            P.mm(stat_ps[gi][:, 0:gs], ones[:, :], sq[:, 0:gs], start=(kc == 0), stop=(kc == 15))
    for gi, (g0, gs) in enumerate(TGX):
        P.act(rstd[:, g0:g0 + gs], stat_ps[gi][:, 0:gs], AF.Sqrt, bias=et[:, 0:1], scale=1.0 / D)
        P.recip(rstd[:, g0:g0 + gs], rstd[:, g0:g0 + gs])
    P.ts('dve', rstd[:, 0:1], rstd[:, 0:1], pm[:, 0:1], None, ALU.mult)
    for kc in range(16):
        P.tt('pool', tmp[:, :], hs[:, kc, :], rstd[:, :], ALU.mult)
        P.ts('dve', hs[:, kc, :], tmp[:, :], gsc[:, kc:kc + 1], vs[:, 3, kc:kc + 1], ALU.mult, ALU.add)
    for kc in range(16):
        P.ts('dve', hs[:, kc, 0:1], hs[:, kc, 0:1], pm[:, 0:1], None, ALU.mult)

    def mix(m):
        for kc in range(16):
            P.ts('pool', tmp[:, 0:T], hs[:, kc, 0:T], mus[:, m, kc:kc + 1], None, ALU.mult)
            P.stt('dve', ab[:, kc, 0:T], hs[:, kc, 1:T1], omu[:, m, kc:kc + 1], tmp[:, 0:T], ALU.mult, ALU.add)

    def store(dst, f0, g0, gs, st, scratch=False):
        if g0 == 512:
            if scratch:
                P.dma('sp', dst[f0:f0 + 128, :], st[:, :], owner=st)
            else:
                P.dma_out('sp', dst[f0:f0 + 128, :], st[:, :])

    if stop == 2:
        P.emit()
        return nc
    mix(4)

    def evac_l1(func):
        def f(ps, f0, fs, tag, g0, gs):
            P.act(lora[0:96, 0, g0:g0 + gs], ps[0:96, 0:gs], func)
        return f
    linear(L, ab, 128, 16, TG, a1, [(0, 96, None)], evac_l1(AF.Copy))

    def evac_a(ps, f0, fs, tag, g0, gs):
        fc = f0 // 128
        s = s32('a', g0)
        P.act(s[:, g0:g0 + gs], ps[:, 0:gs], AF.Sigmoid, bias=vs[:, 5, fc:fc + 1])
        store(s_a, f0, g0, gs, s, scratch=True)
    linear(L, lora, 96, 1, TG, a2, allch, evac_a)

    if stop == 3:
        P.emit()
        return nc
    mix(1)
    linear(L, ab, 128, 16, TG, w1, [(0, 96, None)], evac_l1(AF.Tanh))

    def evac_w(ps, f0, fs, tag, g0, gs):
        fc = f0 // 128
        s = s32('w', g0)
        P.act(s[:, g0:g0 + gs], ps[:, 0:gs], AF.Sigmoid, bias=vs[:, 4, fc:fc + 1])
        P.ts('dve', s[:, g0:g0 + gs], s[:, g0:g0 + gs], -float(np.exp(-0.5)), None, ALU.mult)
        store(o_lw, f0, g0, gs, s)
    linear(L, lora, 96, 1, TG, w2, allch, evac_w)

    if stop == 4:
        P.emit()
        return nc
    mix(2)
    nka = P.sb([128, 16], F32, name="nka")
    P.ts('dve', nka[:, :], vs[:, 7, :], -1.0, None, ALU.mult)

    def evac_k(ps, f0, fs, tag, g0, gs):
        fc = f0 // 128
        laf = l32(s_a, f0, g0)
        la = TT_view(laf, laf.h[:, g0:g0 + gs])
        kkr = t32()
        P.ts('dve', kkr[:, 0:gs], ps[:, 0:gs], vs[:, 6, fc:fc + 1], None, ALU.mult)
        sq = t32()
        P.tt('pool', sq[:, 0:gs], kkr[:, 0:gs], kkr[:, 0:gs], ALU.mult)
        bp = stat_ps[(g0 // 512) % 2]
        P.mm(bp[:, 0:gs], bones[:, :], sq[:, 0:gs])
        nr = t32()
        P.act(nr[:, 0:gs], bp[:, 0:gs], AF.Sqrt)
        P.ts('dve', nr[:, 0:gs], nr[:, 0:gs], 1e-12, None, ALU.max)
        P.recip(nr[:, 0:gs], nr[:, 0:gs])
        P.tt('dve', kkr[:, 0:gs], kkr[:, 0:gs], nr[:, 0:gs], ALU.mult)
        s = s16('kap', g0)
        P.copy('pool', s[:, g0:g0 + gs], kkr[:, 0:gs])
        store(o_kap, f0, g0, gs, s)
        s = s16('b', g0)
        P.tt('dve', s[:, g0:g0 + gs], kkr[:, 0:gs], la[:, 0:gs], ALU.mult)
        store(o_b, f0, g0, gs, s)
        P.ts('dve', sq[:, 0:gs], la[:, 0:gs], vs[:, 7, fc:fc + 1], nka[:, fc:fc + 1], ALU.mult, ALU.add)
        km = s32('km', g0)
        P.stt('dve', km[:, g0:g0 + gs], sq[:, 0:gs], 1.0, ps[:, 0:gs], ALU.add, ALU.mult)
        s = s16('km16', g0)
        P.copy('pool', s[:, g0:g0 + gs], km[:, g0:g0 + gs])
        store(s_km, f0, g0, gs, km, scratch=True)
        store(o_km, f0, g0, gs, s)
    linear(L, ab, 128, 16, TG, wk_, allch, evac_k)

    if part == 'a':
        hv_ = h_io.re("(kc p) t -> p kc t", p=128)
        for kc in range(16):
            P.dma_out('sp', hv_[:, kc, :], hs[:, kc, :])
        P.emit()
        return nc
    P.mute = False
    if part == 'b':
        for kc in range(16):
            P.dma('sp', hs[:, kc, :], h_io.re("(kc p) t -> p kc t", p=128)[:, kc, :])
    mix(0)

    def evac_r(ps, f0, fs, tag, g0, gs):
        fc = f0 // 128
        lkf = l32(s_km, f0, g0)
        lk = TT_view(lkf, lkf.h[:, g0:g0 + gs])
        s = s16('r', g0)
        P.copy('dve', s[:, g0:g0 + gs], ps[:, 0:gs])
        store(o_r, f0, g0, gs, s)
        t_ = t32()
        P.stt('dve', t_[:, 0:gs], ps[:, 0:gs], vs[:, 8, fc:fc + 1], lk[:, 0:gs], ALU.mult, ALU.mult)
        bp = stat_ps[(g0 // 512) % 2]
        P.mm(bp[:, 0:gs], bones[:, :], t_[:, 0:gs])
        s = s32('rk', g0)
        P.copy('act', s[:, g0:g0 + gs], bp[:, 0:gs])
        store(s_rk, f0, g0, gs, s, scratch=True)
    linear(L, ab, 128, 16, TG, wr, allch, evac_r)

    if stop == 6:
        P.emit()
        return nc
    mix(3)

    def evac_v(ps, f0, fs, tag, g0, gs):
        lkf = l32(s_rk, f0, g0)
        lk = TT_view(lkf, lkf.h[:, g0:g0 + gs])
        s = s16('v', g0)
        P.copy('dve', s[:, g0:g0 + gs], ps[:, 0:gs])
        store(o_v, f0, g0, gs, s)
        s = s32('bon', g0)
        P.tt('dve', s[:, g0:g0 + gs], ps[:, 0:gs], lk[:, 0:gs], ALU.mult)
        store(o_bonus, f0, g0, gs, s)
    linear(L, ab, 128, 16, TG, wv, allch, evac_v)

    mix(5)

    def evac_z(ps, f0, fs, tag, g0, gs):
        s = s32('z', g0)
        P.act(s[:, g0:g0 + gs], ps[:, 0:gs], AF.Silu)
        store(o_sz, f0, g0, gs, s)
    linear(L, ab, 128, 16, TG, wz, allch, evac_z)
    P.emit()
    return nc


def run_l3(inp, mod, yz, stop=99):
    gate0 = mod[0, 2 * D:3 * D]
    shift1, scale1 = mod[1, 0:D], mod[1, D:2 * D]
    vecs = np.ascontiguousarray(np.stack([pk(gate0), pk(inp['norm_g'][1]), pk(scale1), pk(shift1),
                                          pk(inp['r_w0'][0]), pk(inp['r_a0'][0]), pk(inp['r_k_k'][0]),
                                          pk(inp['r_k_a'][0]), pk(inp['r_r_k'][0]), pk(np.zeros(D, np.float32))], axis=1))
    mu = np.ascontiguousarray(np.stack([pk(inp['r_mu'][0][m]) for m in range(6)], axis=1))
    bones = np.zeros((128, 128), np.float32)
    bones[0:64, 0:64] = 1.0
    bones[64:128, 64:128] = 1.0
    xTf = np.ascontiguousarray(inp['x'][0].T)
    yzTf = np.ascontiguousarray(yz.T)
    xTp = np.concatenate([np.zeros((D, 1), np.float32), xTf], axis=1)
    yzTp = np.concatenate([np.zeros((D, 1), NPBF), yzTf], axis=1)
    nc = build_l3(stop, 'full')
    maps = []
    for i in range(NCORES):
        sl = slice(i * TPC, (i + 1) * TPC + 1)
        maps.append(dict(yzT=np.ascontiguousarray(yzTp[:, sl]), xT=np.ascontiguousarray(xTp[:, sl]),
                         w_out=inp['a_w_out'][0], vecs=vecs, mu=mu,
                         pm=np.full((128, 1), 0.0 if i == 0 else 1.0, np.float32), bones=bones,
                         w_k=inp['r_w_k'][0], w_r=inp['r_w_r'][0], w_v=inp['r_w_v'][0], w_z=inp['r_w_z'][0],
                         w1=inp['r_w1'][0], w2=inp['r_w2'][0], a1=inp['r_a1'][0], a2=inp['r_a2'][0]))
    res = run(nc, maps)
    out = {}
    for k in ("o_x1", "o_kap", "o_b", "o_km", "o_lw", "o_r", "o_v", "o_bonus", "o_sz"):
        out[k] = np.concatenate([r[k] for r in res], axis=1)
    return out


LNX_EPS = 64e-5
CH = 64
NCH = S // CH


def build_l4():
    nc, P = new_prog()
    SCN = 8
    NSC = NCH // SCN
    d_tok = {k: P.dram("t_" + k, [64, NCH * 256], BF16) for k in ("v", "kap", "b", "km")}
    d_tlw = P.dram("t_lw", [64, NCH * 256], F32)
    d_f = {k: P.dram("f_" + k, [64, 4, S], BF16) for k in ("r", "kap", "b", "km")}
    d_flw = P.dram("f_lw", [64, 4, S], F32)
    d_c = P.dram("consts", [64, 6, 256], F32)
    o_yn = P.dram("o_yn", [64, NCH * 256], F32, "ExternalOutput")

    cs_ = P.sb([64, 6, 256], F32, name="consts")
    P.dma('sp', cs_[:, :, :], d_c[:, :, :])
    tri = cs_[:, 0, 0:64]
    ones64 = cs_[:, 1, 0:64]
    id64 = cs_[:, 2, 0:64]
    I4 = cs_[:, 2, :]
    mUs = cs_[:, 3, :]
    mUi = cs_[:, 4, :]
    mLs = cs_[:, 5, :]
    Tt = {k: P.sb([64, SCN, 256], BF16, name="T_" + k) for k in ("v", "kap", "b", "km")}
    Tlw = P.sb([64, SCN, 256], F32, name="T_lw")
    Ft = {k: P.sb([64, 4, 512], BF16, name="F_" + k) for k in ("r", "kap", "b", "km")}
    Flw = P.sb([64, 4, 512], F32, name="F_lw")
    At = P.sb([64, SCN, 256], F32, name="At")
    Bh = P.sb([64, SCN, 256], F32, name="Bh")
    Kh = P.sb([64, SCN, 256], F32, name="Kh")
    Vt = P.sb([64, SCN, 256], F32, name="Vt")
    Rt = P.sb([64, 4, 512], F32, name="Rt")
    AtT = P.sb([64, 4, 512], F32, name="AtT")
    BtT = P.sb([64, 4, 512], F32, name="BtT")
    KtT = P.sb([64, 4, 512], F32, name="KtT")
    gC = P.sb([64, 4, SCN], F32, name="gC")
    lg = P.sb([64, SCN, 256], F32, name="lg")
    d1 = P.sb([64, SCN, 256], F32, name="d1")
    d2 = P.sb([64, SCN, 256], F32, name="d2")
    lgf = P.sb([64, 4, 512], F32, name="lgf")
    ef = P.sb([64, 4, 512], F32, name="ef")
    ef2 = P.sb([64, 4, 512], F32, name="ef2")
    banks = [P.ps([128, 512], F32, name=f"bk{i}") for i in range(8)]
    bi = [0]

    def nb():
        bi[0] += 1
        return banks[bi[0] % 8]
    tmps = {}

    def tm(name, n=2, shape=(64, 256)):
        if name not in tmps:
            tmps[name] = [[P.sb(list(shape), F32, name=f"{name}{i}") for i in range(n)], 0]
        l = tmps[name]
        l[1] += 1
        return l[0][l[1] % n]
    Hs = [P.sb([64, 256], F32, name=f"H{i}") for i in range(2)]
    P.memset('pool', Hs[0][:, :], 0.0)
    hsl = lambda h: slice(h * 64, (h + 1) * 64)

    def mm4(fn_l, fn_r):
        ps = nb()
        for h in range(4):
            P.mm(ps[0:64, hsl(h)], fn_l(h), fn_r(h))
        return ps

    for sc in range(NSC):
        c0 = sc * SCN
        for k in ("v", "kap", "b", "km"):
            P.dma('sp', Tt[k][:, :, :], d_tok[k].re("p (c f) -> p c f", f=256)[:, c0:c0 + SCN, :])
        for k in ("r", "kap", "b", "km"):
            P.dma('sp', Ft[k][:, :, :], d_f[k][:, :, sc * 512:(sc + 1) * 512])
        P.dma('sp', Tlw[:, :, :], d_tlw.re("p (c f) -> p c f", f=256)[:, c0:c0 + SCN, :])
        P.dma('sp', Flw[:, :, :], d_flw[:, :, sc * 512:(sc + 1) * 512])
        for p_ in range(SCN // 2):
            psL = nb()
            psC = nb()
            for cc in range(2):
                c = 2 * p_ + cc
                P.mm(psL[0:64, cc * 256:(cc + 1) * 256], tri, Tlw[:, c, :])
                P.mm(psC[0:64, cc * 256:(cc + 1) * 256], ones64, Tlw[:, c, :])
            P.copy('act', lg.re("p c f -> p (c f)")[:, p_ * 512:(p_ + 1) * 512], psL[0:64, :])
            P.tt('dve', d2.re("p c f -> p (c f)")[:, p_ * 512:(p_ + 1) * 512], psC[0:64, :],
                 lg.re("p c f -> p (c f)")[:, p_ * 512:(p_ + 1) * 512], ALU.subtract)
        P.tt('pool', d1[:, :, :], lg[:, :, :], Tlw[:, :, :], ALU.subtract)
        P.act(d1[:, :, :], d1[:, :, :], AF.Exp)
        P.act(d2[:, :, :], d2[:, :, :], AF.Exp)
        P.stt('dve', At[:, :, :], Tt["kap"][:, :, :], -1.0, d1[:, :, :], ALU.mult, ALU.mult)
        P.tt('pool', Bh[:, :, :], Tt["b"][:, :, :], d2[:, :, :], ALU.mult)
        P.tt('dve', Kh[:, :, :], Tt["km"][:, :, :], d2[:, :, :], ALU.mult)
        P.copy('pool', Vt[:, :, :], Tt["v"][:, :, :])
        for h in range(4):
            psF = nb()
            for c in range(SCN):
                P.mm(psF[0:64, c * 64:(c + 1) * 64], Tlw[:, c, hsl(h)], tri)
            P.copy('act', lgf[:, h, :], psF[0:64, :])
        P.act(ef[:, :, :], lgf[:, :, :], AF.Exp)
        P.tt('dve', Rt[:, :, :], Ft["r"][:, :, :], ef[:, :, :], ALU.mult)
        P.copy('pool', gC[:, :, :], ef.re("p h (c t) -> p h c t", t=64)[:, :, :, 63])
        P.act(ef2[:, :, :], lgf[:, :, :], AF.Exp, scale=-1.0)
        P.tt('dve', BtT[:, :, :], Ft["b"][:, :, :], ef2[:, :, :], ALU.mult)
        P.tt('pool', KtT[:, :, :], Ft["km"][:, :, :], ef2[:, :, :], ALU.mult)
        P.tt('pool', lgf[:, :, :], lgf[:, :, :], Flw[:, :, :], ALU.subtract)
        P.act(ef2[:, :, :], lgf[:, :, :], AF.Exp)
        P.stt('dve', AtT[:, :, :], Ft["kap"][:, :, :], -1.0, ef2[:, :, :], ALU.mult, ALU.mult)

        def chunk_gen(c):
            cs = slice(c * 64, (c + 1) * 64)
            cg = c0 + c
            ps = mm4(lambda h: BtT[:, h, cs], lambda h: AtT[:, h, cs])
            XT = tm("XT", 4)
            P.tt('dve', XT[:, :], ps[0:64, 0:256], mUs, ALU.mult)
            ps = mm4(lambda h: AtT[:, h, cs], lambda h: BtT[:, h, cs])
            X = tm("X", 4)
            P.tt('dve', X[:, :], ps[0:64, 0:256], mLs, ALU.mult)
            ps = mm4(lambda h: KtT[:, h, cs], lambda h: AtT[:, h, cs])
            AakT = tm("AakT")
            P.tt('dve', AakT[:, :], ps[0:64, 0:256], mUs, ALU.mult)
            ps = mm4(lambda h: BtT[:, h, cs], lambda h: Rt[:, h, cs])
            ArbT = tm("ArbT")
            P.tt('dve', ArbT[:, :], ps[0:64, 0:256], mUi, ALU.mult)
            ps = mm4(lambda h: KtT[:, h, cs], lambda h: Rt[:, h, cs])
            ArkT = tm("ArkT")
            P.tt('dve', ArkT[:, :], ps[0:64, 0:256], mUi, ALU.mult)
            W = tm("W", 4)
            P.tt('pool', W[:, :], XT[:, :], I4, ALU.add)
            yield
            for it in range(5):
                psa = mm4(lambda h: XT[:, hsl(h)], lambda h: X[:, hsl(h)])
                Xn = tm("X", 4)
                P.copy('act', Xn[:, :], psa[0:64, 0:256])
                if it < 4:
                    psb = mm4(lambda h: X[:, hsl(h)], lambda h: XT[:, hsl(h)])
                    XTn = tm("XT", 4)
                    P.copy('act', XTn[:, :], psb[0:64, 0:256])
                psc = mm4(lambda h: Xn[:, hsl(h)], lambda h: W[:, hsl(h)])
                Wn = tm("W", 4)
                P.tt('dve', Wn[:, :], psc[0:64, 0:256], W[:, :], ALU.add)
                X, W = Xn, Wn
                if it < 4:
                    XT = XTn
                yield
            ps = mm4(lambda h: AakT[:, hsl(h)], lambda h: Vt[:, c, hsl(h)])
            X2 = tm("X2")
            P.copy('act', X2[:, :], ps[0:64, 0:256])
            yield
            ps = mm4(lambda h: W[:, hsl(h)], lambda h: X2[:, hsl(h)])
            U2 = tm("U2")
            P.copy('act', U2[:, :], ps[0:64, 0:256])
            ps = mm4(lambda h: W[:, hsl(h)], lambda h: At[:, c, hsl(h)])
            Ap = tm("Ap")
            P.copy('act', Ap[:, :], ps[0:64, 0:256])
            yield
            ps = mm4(lambda h: Ap[:, hsl(h)], lambda h: ArbT[:, hsl(h)])
            RpT = tm("RpT")
            P.tt('dve', RpT.re("p (h t) -> p h t", h=4)[:, :, :], ps.re("p (h t) -> p h t", t=64)[0:64, 0:4, :],
                 Rt[:, :, cs], ALU.add)
            ps = mm4(lambda h: Ap[:, hsl(h)], lambda h: Bh[:, c, hsl(h)])
            PhiT = tm("PhiT")
            for h in range(4):
                P.stt('dve', PhiT[:, hsl(h)], id64, gC[:, h, c:c + 1], ps[0:64, hsl(h)], ALU.mult, ALU.add)
            yield
            Hc, Hn = Hs[cg % 2], Hs[(cg + 1) % 2]
            psY = nb()
            psH = nb()
            for h in range(4):
                P.mm(psY[0:64, hsl(h)], RpT[:, hsl(h)], Hc[:, hsl(h)], start=True, stop=False)
                P.mm(psY[0:64, hsl(h)], ArbT[:, hsl(h)], U2[:, hsl(h)], start=False, stop=False)
                P.mm(psY[0:64, hsl(h)], ArkT[:, hsl(h)], Vt[:, c, hsl(h)], start=False, stop=True)
            for h in range(4):
                P.mm(psH[0:64, hsl(h)], PhiT[:, hsl(h)], Hc[:, hsl(h)], start=True, stop=False)
                P.mm(psH[0:64, hsl(h)], Bh[:, c, hsl(h)], U2[:, hsl(h)], start=False, stop=False)
                P.mm(psH[0:64, hsl(h)], Kh[:, c, hsl(h)], Vt[:, c, hsl(h)], start=False, stop=True)
            P.copy('act', Hn[:, :], psH[0:64, 0:256])
            yield
            ysb = tm("ysb")
            P.copy('act', ysb[:, :], psY[0:64, 0:256])
            st = tm("st", 4, (64, 16))
            ysb3 = ysb.re("p (h v) -> p h v", h=4)
            P.op('dve', lambda e, o=st[:, 0:4].ap, i=ysb3[:, :, :].ap: e.tensor_reduce(out=o, in_=i, axis=AX.X, op=ALU.add),
                 [ysb], [st])
            P.ts('dve', st[:, 4:8], st[:, 0:4], -1.0 / 64, None, ALU.mult)
            cen = tm("cen")
            for h in range(4):
                P.ts('pool' if h % 2 else 'dve', cen[:, hsl(h)], ysb[:, hsl(h)], st[:, 4 + h:5 + h], None, ALU.add)
            sq = tm("sqq")
            P.tt('pool', sq[:, :], cen[:, :], cen[:, :], ALU.mult)
            st2 = tm("st", 4, (64, 16))
            P.op('dve', lambda e, o=st2[:, 0:4].ap, i=sq.re("p (h v) -> p h v", h=4)[:, :, :].ap:
                 e.tensor_reduce(out=o, in_=i, axis=AX.X, op=ALU.add), [sq], [st2])
            P.ts('dve', st2[:, 4:8], st2[:, 0:4], 1.0 / 64, LNX_EPS, ALU.mult, ALU.add)
            P.act(st2[:, 8:12], st2[:, 4:8], AF.Sqrt)
            P.recip(st2[:, 12:16], st2[:, 8:12])
            yo = tm("yo", 4)
            for h in range(4):
                P.ts('pool' if h % 2 else 'dve', yo[:, hsl(h)], cen[:, hsl(h)], st2[:, 12 + h:13 + h], None, ALU.mult)
            P.dma_out('sp', o_yn[:, cg * 256:(cg + 1) * 256], yo[:, :])

        for c2 in range(0, SCN, 2):
            gens = [chunk_gen(c2), chunk_gen(c2 + 1)]
            alive = True
            while alive:
                alive = False
                for g_ in gens:
                    try:
                        next(g_)
                        alive = True
                    except StopIteration:
                        pass
    P.emit()
    return nc


def run_l4(l3):
    nc = build_l4()
    consts = np.zeros((64, 6, 256), np.float32)
    s_ = np.arange(64)[:, None]
    t_ = np.arange(64)[None, :]
    consts[:, 0, 0:64] = (s_ <= t_)
    consts[:, 1, 0:64] = 1.0
    consts[:, 2, :] = np.tile(np.eye(64, dtype=np.float32), (1, 4))
    consts[:, 3, :] = np.tile((t_ > s_).astype(np.float32), (1, 4))
    consts[:, 4, :] = np.tile((t_ >= s_).astype(np.float32), (1, 4))
    consts[:, 5, :] = np.tile((t_ < s_).astype(np.float32), (1, 4))
    maps = []
    for i in range(NCORES):
        chs = slice(i * 256, (i + 1) * 256)
        m = {"consts": consts}

        def tokl(a):
            return np.ascontiguousarray(a[chs, :].T.reshape(NCH, 64, 256).transpose(1, 0, 2).reshape(64, NCH * 256))

        def featl(a):
            return np.ascontiguousarray(a[chs, :].reshape(4, 64, S).transpose(1, 0, 2))
        for k, src in (("v", "o_v"), ("kap", "o_kap"), ("b", "o_b"), ("km", "o_km")):
            m["t_" + k] = tokl(l3[src])
        m["t_lw"] = tokl(l3["o_lw"])
        for k, src in (("r", "o_r"), ("kap", "o_kap"), ("b", "o_b"), ("km", "o_km")):
            m["f_" + k] = featl(l3[src])
        m["f_lw"] = featl(l3["o_lw"])
        maps.append(m)
    res = run(nc, maps)
    yn = np.zeros((S, D), np.float32)
    for i in range(NCORES):
        a = res[i]["o_yn"].reshape(64, NCH, 256).transpose(1, 0, 2).reshape(S, 256)
        yn[:, i * 256:(i + 1) * 256] = a
    return yn


def build_l5():
    nc, P = new_prog()
    T = TPC
    TG = [(0, 512), (512, 512)]
    ynT = P.dram("ynT", [D, T], F32)
    bonus = P.dram("bonus", [D, T], F32)
    szT = P.dram("szT", [D, T], F32)
    x1T = P.dram("x1T", [D, T], F32)
    w_o = P.dram("w_o", [D, D], F32)
    vecs = P.dram("vecs", [128, 4, 16], F32)
    o_out = P.dram("o_out", [D, T], F32, "ExternalOutput")
    L = LinCtx(P)
    ones = P.sb([128, 128], F32, name="ones")
    P.memset('pool', ones[:, :], 1.0)
    vs = P.sb([128, 4, 16], F32, name="vs")
    P.dma('sp', vs[:, :, :], vecs[:, :, :])
    yb = P.sb([128, 16, T], BF16, name="yb")
    xs = P.sb([128, 16, T], F32, name="xs")
    lb = [[P.sb([128, T], F32, name=f"lb{j}_{i}") for i in range(2)] for j in range(3)]
    sqb = [P.sb([128, 512], F32, name=f"sq{i}") for i in range(2)]
    stat_ps = [P.ps([128, 512], F32, name=f"sps{i}") for i in range(2)]
    rstd = P.sb([128, T], F32, name="rstd")
    x1b = [P.sb([128, T], F32, name=f"x1b{i}") for i in range(2)]
    ost = [P.sb([128, T], F32, name=f"ost{i}") for i in range(2)]
    for kc in range(16):
        a, b, c = lb[0][kc % 2], lb[1][kc % 2], lb[2][kc % 2]
        P.dma('sp', a[:, :], ynT.re("(kc p) t -> p kc t", p=128)[:, kc, :])
        P.dma('sp', b[:, :], bonus.re("(kc p) t -> p kc t", p=128)[:, kc, :])
        P.dma('sp', c[:, :], szT.re("(kc p) t -> p kc t", p=128)[:, kc, :])
        P.ts('dve', a[:, :], a[:, :], vs[:, 0, kc:kc + 1], vs[:, 1, kc:kc + 1], ALU.mult, ALU.add)
        P.tt('pool', a[:, :], a[:, :], b[:, :], ALU.add)
        P.tt('dve', yb[:, kc, :], a[:, :], c[:, :], ALU.mult)
    xcur = {}

    def evac(ps, f0, fs, tag, g0, gs):
        fc = f0 // 128
        if g0 == 0:
            b = x1b[fc % 2]
            P.dma('sp', b[:, :], x1T.re("(kc p) t -> p kc t", p=128)[:, fc, :])
            xcur[fc] = b
        b = xcur[fc]
        P.stt('dve', xs[:, fc, g0:g0 + gs], ps[:, 0:gs], vs[:, 2, fc:fc + 1], b[:, g0:g0 + gs], ALU.mult, ALU.add)
    linear(L, yb, 128, 16, TG, w_o, [(f0, 128, None) for f0 in range(0, D, 128)], evac)
    et = eps_tile(P, EPS)
    for kc in range(16):
        for gi, (g0, gs) in enumerate(TG):
            sq = sqb[gi]
            P.act(sq[:, 0:gs], xs[:, kc, g0:g0 + gs], AF.Square)
            P.mm(stat_ps[gi][:, 0:gs], ones[:, :], sq[:, 0:gs], start=(kc == 0), stop=(kc == 15))
    for gi, (g0, gs) in enumerate(TG):
        P.act(rstd[:, g0:g0 + gs], stat_ps[gi][:, 0:gs], AF.Sqrt, bias=et[:, 0:1], scale=1.0 / D)
        P.recip(rstd[:, g0:g0 + gs], rstd[:, g0:g0 + gs])
    for kc in range(16):
        o = ost[kc % 2]
        P.stt('dve', o[:, :], xs[:, kc, :], vs[:, 3, kc:kc + 1], rstd[:, :], ALU.mult, ALU.mult)
        P.dma_out('sp', o_out[kc * 128:(kc + 1) * 128, :], o[:, :])
    P.emit()
    return nc


def run_l5(inp, mod, l3, yn):
    nc = build_l5()
    gate1 = mod[1, 2 * D:3 * D]
    vecs = np.ascontiguousarray(np.stack([pk(inp['r_lnx_g'][0]), pk(inp['r_lnx_b'][0]), pk(gate1),
                                          pk(inp['final_g'])], axis=1))
    ynT = np.ascontiguousarray(yn.T)
    maps = []
    for i in range(NCORES):
        sl = slice(i * TPC, (i + 1) * TPC)
        maps.append(dict(ynT=np.ascontiguousarray(ynT[:, sl]), bonus=np.ascontiguousarray(l3["o_bonus"][:, sl]),
                         szT=np.ascontiguousarray(l3["o_sz"][:, sl]), x1T=np.ascontiguousarray(l3["o_x1"][:, sl]),
                         w_o=inp['r_w_o'][0], vecs=vecs))
    res = run(nc, maps)
    outT = np.concatenate([r["o_out"] for r in res], axis=1)
    return np.ascontiguousarray(outT.T)[None].astype(np.float32)


def kernel(**inputs):
    inp = {k: np.asarray(v) for k, v in inputs.items()}
    mod = run_l0(inp)
    l1 = run_l1(inp, mod)
    yz = run_l2(inp, l1)
    del l1
    l3 = run_l3(inp, mod, yz)
    yn = run_l4(l3)
    return run_l5(inp, mod, l3, yn)
```
